# Optimizing a Trainium2 kernel written in Bass

```python
import math
import jax, jax.numpy as jnp
from jax import lax
import numpy as np


D_MODEL = 1024
BATCH = 2
SEQ = 8192
DEPTH = 2

N_MIXERS = 2
N_POOL_LAYERS = (DEPTH + N_MIXERS - 1) // N_MIXERS
N_SSM_LAYERS = DEPTH // N_MIXERS
POOL_WINDOWS = (2, 4, 8, 16)
POOL_GROUPS = len(POOL_WINDOWS)
POOL_GROUP_DIM = D_MODEL // POOL_GROUPS
SSM_GROUP_DIM = 16
SSM_GROUPS = D_MODEL // SSM_GROUP_DIM
SSM_STATE = 64
N_DIRS = 2
D_FF = ((8 * D_MODEL) // 3 + 63) // 64 * 64
N_SUBLAYERS = 3
N_MOD = 3
EPS = 1e-6
DT_MIN = 1e-3
DT_MAX = 1e-1

kernel_name = 'hybrid_pool_s5_macaron_encoder'


def rms_norm(x, g):
    xf = x.astype(jnp.float32)
    y = xf * lax.rsqrt(jnp.mean(xf * xf, axis=-1, keepdims=True) + EPS)
    return (y * g.astype(jnp.float32)).astype(x.dtype)


def swiglu_ffn(h, w_in, w_out):
    gu = h @ w_in
    gate, up = jnp.split(gu, 2, axis=-1)
    return (jax.nn.silu(gate) * up) @ w_out


def centred_window_mean(u, window):
    S = u.shape[1]
    left = window // 2
    right = window - 1 - left
    csum = jnp.cumsum(u.astype(jnp.float32), axis=1)
    csum = jnp.pad(csum, ((0, 0), (1, 0), (0, 0)))
    t = jnp.arange(S, dtype=jnp.int32)
    lo = jnp.maximum(t - left, 0)
    hi = jnp.minimum(t + right, S - 1)
    total = jnp.take(csum, hi + 1, axis=1) - jnp.take(csum, lo, axis=1)
    count = (hi - lo + 1).astype(jnp.float32)
    return (total / count[None, :, None]).astype(u.dtype)


def pool_mixer(h, w_grp, b, scale):
    Bsz, S, D = h.shape
    hg = h.reshape(Bsz, S, POOL_GROUPS, POOL_GROUP_DIM)
    pooled = []
    for gi, window in enumerate(POOL_WINDOWS):
        ug = hg[:, :, gi]
        pooled.append(centred_window_mean(ug, window) - ug)
    p = jnp.stack(pooled, axis=2)
    y = jnp.einsum('bsgc,gcd->bsgd', p, w_grp).reshape(Bsz, S, D) + b
    return y * scale


def zoh_discretise(lam_re, lam_im, log_dt, b_re, b_im):
    dt = jnp.exp(log_dt)[:, None]
    mag = jnp.exp(lam_re * dt)
    ar = mag * jnp.cos(lam_im * dt)
    ai = mag * jnp.sin(lam_im * dt)
    den = lam_re * lam_re + lam_im * lam_im
    fr = ((ar - 1.0) * lam_re + ai * lam_im) / den
    fi = (ai * lam_re - (ar - 1.0) * lam_im) / den
    bb_re = fr[..., None] * b_re - fi[..., None] * b_im
    bb_im = fr[..., None] * b_im + fi[..., None] * b_re
    return ar, ai, bb_re, bb_im


def complex_linear_scan(a_re, a_im, bu_re, bu_im, reverse):
    S = bu_re.shape[1]
    a_re_s = jnp.broadcast_to(a_re, (1, S) + a_re.shape)
    a_im_s = jnp.broadcast_to(a_im, (1, S) + a_im.shape)

    def combine(e1, e2):
        ar1, ai1, br1, bi1 = e1
        ar2, ai2, br2, bi2 = e2
        ar = ar1 * ar2 - ai1 * ai2
        ai = ar1 * ai2 + ai1 * ar2
        br = ar2 * br1 - ai2 * bi1 + br2
        bi = ar2 * bi1 + ai2 * br1 + bi2
        return (ar, ai, br, bi)

    _, _, xr, xi = lax.associative_scan(combine, (a_re_s, a_im_s, bu_re, bu_im), reverse=reverse, axis=1)
    return xr, xi


def s5_mixer(h, lam_re, lam_im, log_dt, b_re, b_im, c_re, c_im, d_skip, w_glu, b_glu):
    Bsz, S, D = h.shape
    hf = h.astype(jnp.float32)
    u = hf.reshape(Bsz, S, SSM_GROUPS, SSM_GROUP_DIM)
    y = d_skip.astype(jnp.float32) * hf
    for direction in range(N_DIRS):
        ar, ai, bb_re, bb_im = zoh_discretise(
            lam_re[direction].astype(jnp.float32), lam_im[direction].astype(jnp.float32),
            log_dt[direction].astype(jnp.float32),
            b_re[direction].astype(jnp.float32), b_im[direction].astype(jnp.float32))
        bu_re = jnp.einsum('bsgh,gph->bsgp', u, bb_re)
        bu_im = jnp.einsum('bsgh,gph->bsgp', u, bb_im)
        xr, xi = complex_linear_scan(ar, ai, bu_re, bu_im, reverse=(direction == 1))
        yd = (jnp.einsum('bsgp,ghp->bsgh', xr, c_re[direction].astype(jnp.float32))
              - jnp.einsum('bsgp,ghp->bsgh', xi, c_im[direction].astype(jnp.float32)))
        y = y + yd.reshape(Bsz, S, D)
    y = jax.nn.gelu(y).astype(h.dtype)
    z = y @ w_glu + b_glu
    return z[..., :D] * jax.nn.sigmoid(z[..., D:])


def setup_inputs(seed: int = 0) -> dict:
    key = jax.random.key(seed)
    ks = jax.random.split(key, 24)
    D, G, P, H = D_MODEL, SSM_GROUPS, SSM_STATE, SSM_GROUP_DIM
    f32 = jnp.float32
    x = jax.random.normal(ks[0], (BATCH, SEQ, D), f32)
    c = jax.random.normal(ks[1], (BATCH, D), f32)
    mod_w = jax.random.normal(ks[2], (DEPTH, D, N_SUBLAYERS * N_MOD * D), f32) * (0.5 * D ** -0.5)
    mod_b = jax.random.normal(ks[3], (DEPTH, N_SUBLAYERS * N_MOD * D), f32) * 0.01
    norm_g = 1.0 + 0.05 * jax.random.normal(ks[4], (DEPTH, N_SUBLAYERS, D), f32)
    ffn_w_in = jax.random.normal(ks[5], (DEPTH, 2, D, 2 * D_FF), f32) * D ** -0.5
    ffn_w_out = jax.random.normal(ks[6], (DEPTH, 2, D_FF, D), f32) * D_FF ** -0.5
    pool_w = jax.random.normal(ks[7], (N_POOL_LAYERS, POOL_GROUPS, POOL_GROUP_DIM, POOL_GROUP_DIM), f32) * POOL_GROUP_DIM ** -0.5
    pool_b = jax.random.normal(ks[8], (N_POOL_LAYERS, D), f32) * 0.01
    pool_scale = 1.0 + 0.1 * jax.random.normal(ks[9], (N_POOL_LAYERS, D), f32)
    n_idx = jnp.arange(P, dtype=f32)
    ssm_lam_re = -0.5 + 0.01 * jax.random.normal(ks[10], (N_SSM_LAYERS, N_DIRS, G, P), f32)
    ssm_lam_im = math.pi * n_idx + 0.01 * jax.random.normal(ks[11], (N_SSM_LAYERS, N_DIRS, G, P), f32)
    ssm_log_dt = jax.random.uniform(ks[12], (N_SSM_LAYERS, N_DIRS, G), f32,
                                    minval=math.log(DT_MIN), maxval=math.log(DT_MAX))
    b_std = (2.0 * H) ** -0.5
    ssm_b_re = jax.random.normal(ks[13], (N_SSM_LAYERS, N_DIRS, G, P, H), f32) * b_std
    ssm_b_im = jax.random.normal(ks[14], (N_SSM_LAYERS, N_DIRS, G, P, H), f32) * b_std
    c_std = P ** -0.5
    ssm_c_re = jax.random.normal(ks[15], (N_SSM_LAYERS, N_DIRS, G, H, P), f32) * c_std
    ssm_c_im = jax.random.normal(ks[16], (N_SSM_LAYERS, N_DIRS, G, H, P), f32) * c_std
    ssm_d = jax.random.normal(ks[17], (N_SSM_LAYERS, D), f32)
    glu_w = jax.random.normal(ks[18], (N_SSM_LAYERS, D, 2 * D), f32) * D ** -0.5
    glu_b = jax.random.normal(ks[19], (N_SSM_LAYERS, 2 * D), f32) * 0.01
    final_g = 1.0 + 0.05 * jax.random.normal(ks[20], (D,), f32)
    return {'x': x, 'c': c, 'mod_w': mod_w, 'mod_b': mod_b, 'norm_g': norm_g,
            'ffn_w_in': ffn_w_in, 'ffn_w_out': ffn_w_out,
            'pool_w': pool_w, 'pool_b': pool_b, 'pool_scale': pool_scale,
            'ssm_lam_re': ssm_lam_re, 'ssm_lam_im': ssm_lam_im, 'ssm_log_dt': ssm_log_dt,
            'ssm_b_re': ssm_b_re, 'ssm_b_im': ssm_b_im, 'ssm_c_re': ssm_c_re, 'ssm_c_im': ssm_c_im,
            'ssm_d': ssm_d, 'glu_w': glu_w, 'glu_b': glu_b, 'final_g': final_g}


def reference(x, c, mod_w, mod_b, norm_g, ffn_w_in, ffn_w_out, pool_w, pool_b, pool_scale,
              ssm_lam_re, ssm_lam_im, ssm_log_dt, ssm_b_re, ssm_b_im, ssm_c_re, ssm_c_im,
              ssm_d, glu_w, glu_b, final_g):
    Bsz, S, D = x.shape
    cond = jax.nn.silu(c)
    for i in range(DEPTH):
        mod = (cond @ mod_w[i] + mod_b[i]).reshape(Bsz, N_SUBLAYERS, N_MOD, D)
        shift = mod[:, :, 0][:, :, None, :]
        scale = mod[:, :, 1][:, :, None, :]
        gate = mod[:, :, 2][:, :, None, :]

        h = rms_norm(x, norm_g[i, 0]) * (1.0 + scale[:, 0]) + shift[:, 0]
        x = x + 0.5 * gate[:, 0] * swiglu_ffn(h, ffn_w_in[i, 0], ffn_w_out[i, 0])

        h = rms_norm(x, norm_g[i, 1]) * (1.0 + scale[:, 1]) + shift[:, 1]
        j = i // N_MIXERS
        if i % N_MIXERS == 0:
            m = pool_mixer(h, pool_w[j], pool_b[j], pool_scale[j])
        else:
            m = s5_mixer(h, ssm_lam_re[j], ssm_lam_im[j], ssm_log_dt[j], ssm_b_re[j], ssm_b_im[j],
                         ssm_c_re[j], ssm_c_im[j], ssm_d[j], glu_w[j], glu_b[j])
        x = x + gate[:, 1] * m

        h = rms_norm(x, norm_g[i, 2]) * (1.0 + scale[:, 2]) + shift[:, 2]
        x = x + 0.5 * gate[:, 2] * swiglu_ffn(h, ffn_w_in[i, 1], ffn_w_out[i, 1])
    return rms_norm(x, final_g)
```

```python
import math
from contextlib import ExitStack

import numpy as np
import ml_dtypes

import concourse.bass as bass
import concourse.mybir as mybir
from concourse.ap import AP
from concourse.bass_utils import run_bass_kernel_spmd

F32 = mybir.dt.float32
BF16 = mybir.dt.bfloat16
AF = mybir.ActivationFunctionType
ALU = mybir.AluOpType

D = 1024
KT = 8
DFF = 2752
FT = 22
SEQ = 8192
EPS = 1e-6
LCH = 16


class Cfg:
    def __init__(self, T=2048, CPS=4, NCORES=8, do_s5=True, do_cc=True, mode="fused"):
        self.mode = mode
        self.dbg = None
        self.T = T
        self.CPS = CPS
        self.NCORES = NCORES
        self.S = T * CPS
        self.NTB = T // 512
        self.NCH = T // LCH
        self.do_s5 = do_s5
        self.do_cc = do_cc and CPS > 1


class Prog:
    ENG = ("pe", "act", "dve", "pool", "sp")

    def __init__(self, nc, es):
        self.nc, self.es = nc, es
        self.q = {e: [] for e in self.ENG}
        self.sem = {}
        self.cnt = {}
        self.waited = {}
        self.tagw = {e: {} for e in self.ENG}
        self.tagr = {e: {} for e in self.ENG}

    def _sem(self, name):
        if name not in self.sem:
            self.sem[name] = self.es.enter_context(self.nc.semaphore(name))
            self.cnt[name] = 0
        return name

    def wait(self, eng, sig):
        if sig is None:
            return
        name, val = sig
        if self.waited.get((eng, name), 0) >= val:
            return
        self.waited[(eng, name)] = val
        self.q[eng].append(("w", name, val))

    def op(self, eng, fn, waits=(), reads=(), writes=(), mark=None, dma=None, inc=16):
        for s in waits:
            self.wait(eng, s)
        if dma is None and eng != "pe":
            tw, tr = self.tagw[eng], self.tagr[eng]
            for t in reads:
                if t in tw:
                    self.wait(eng, tw[t])
            for t in writes:
                if t in tw:
                    self.wait(eng, tw[t])
                if t in tr:
                    self.wait(eng, tr[t])
        if dma is not None:
            name = self._sem(dma)
            self.cnt[name] += inc
            sig = (name, self.cnt[name])
            self.q[eng].append(("o", fn, name, inc))
            return sig
        if eng == "pe" and not mark:
            self.q[eng].append(("o", fn, None, 0))
            return None
        name = self._sem("m_" + eng)
        self.cnt[name] += 1
        sig = (name, self.cnt[name])
        self.q[eng].append(("o", fn, name, 1))
        if eng != "pe":
            for t in reads:
                self.tagr[eng][t] = sig
            for t in writes:
                self.tagw[eng][t] = sig
        return sig

    def emit(self):
        with self.nc.Block() as block:
            for eng, deco in (("pe", block.tensor), ("act", block.scalar), ("dve", block.vector),
                              ("pool", block.gpsimd), ("sp", block.sync)):
                def f(e, eng=eng):
                    for it in self.q[eng]:
                        if it[0] == "w":
                            e.wait_ge(self.sem[it[1]], it[2])
                        else:
                            ins = it[1](e)
                            if it[2] is not None:
                                ins.then_inc(self.sem[it[2]], it[3])
                deco(f)


def sub(ap, p0, p1):
    return ap[p0:p1]


def _pk(v):
    v = np.asarray(v, np.float32).reshape(-1, 128)
    return np.ascontiguousarray(v.T)


def _gen_layout(a):
    a = np.asarray(a, np.float32)
    rest = a.shape[3:]
    a = a.reshape(2, 32, 2, 64, *rest)
    a = np.moveaxis(a, [2, 3], [0, 1])
    return np.ascontiguousarray(a.reshape(128, 64, *rest))


def const_tables():
    k = np.arange(128)
    hk, gk, sk = k % 16, k // 16, k // 16
    E = np.zeros((128, 8, 15, 16), np.float32)
    Fm = np.zeros((128, 8, 15, 16), np.float32)
    for kk in range(128):
        E[kk, gk[kk], 7, hk[kk]] = 1.0
        Fm[kk, sk[kk], 7, hk[kk]] = 1.0
    ident = np.eye(128, dtype=np.float32)
    s_lo = (np.arange(128) // 16)[:, None]
    t_lo = (np.arange(128) // 16)[None, :]
    mle = (s_lo <= t_lo).astype(np.float32)
    mge = (s_lo >= t_lo).astype(np.float32)
    ones = np.full((128, 128), 1.0 / D, np.float32)
    cb = np.concatenate([E.reshape(128, -1), Fm.reshape(128, -1), ident, mle, mge, ones], axis=1)
    return cb.astype(ml_dtypes.bfloat16)


NCB = 2 * 1920 + 4 * 128
V_NG = 0
V_FG = 48
V_PB = 56
V_PS = 64
V_SD = 72
V_GB = 80
V_MB = 96
V_C = 240
V_EDGE = 248
V_ICE = 250
V_CM = 314
NV = 338
POOLW = (2, 4, 8, 16)


def prep_inputs(cfg, inp):
    T, CPS = cfg.T, cfg.CPS
    S = cfg.S
    x = np.asarray(inp["x"], np.float32)
    cb = const_tables()
    ssm = {}
    ssm["lamre"] = _gen_layout(inp["ssm_lam_re"][0])
    ssm["lamim"] = _gen_layout(inp["ssm_lam_im"][0])
    ld = np.asarray(inp["ssm_log_dt"][0], np.float32)
    ssm["logdt"] = _gen_layout(np.broadcast_to(ld[:, :, None], (2, 64, 64)))
    ssm["bre"] = _gen_layout(inp["ssm_b_re"][0])
    ssm["bim"] = _gen_layout(inp["ssm_b_im"][0])
    ssm["cre"] = _gen_layout(np.swapaxes(np.asarray(inp["ssm_c_re"][0]), 2, 3))
    ssm["cim"] = _gen_layout(np.swapaxes(np.asarray(inp["ssm_c_im"][0]), 2, 3))
    ssmp = np.concatenate([ssm["lamre"], ssm["lamim"], ssm["logdt"]], axis=1)
    ssmbc = np.concatenate([ssm[k].reshape(128, -1) for k in ("bre", "bim", "cre", "cim")], axis=1)
    shared = {
        "mod_w": np.ascontiguousarray(inp["mod_w"], np.float32),
        "w_in": np.ascontiguousarray(inp["ffn_w_in"], np.float32),
        "w_out": np.ascontiguousarray(inp["ffn_w_out"], np.float32),
        "pool_w": np.ascontiguousarray(inp["pool_w"][0], np.float32),
        "glu_w": np.ascontiguousarray(inp["glu_w"][0], np.float32),
        "cb": cb, "ssmp": np.ascontiguousarray(ssmp), "ssmbc": np.ascontiguousarray(ssmbc),
    }
    maps = []
    for k in range(cfg.NCORES):
        b, pos = k // CPS, k % CPS
        t0 = pos * T
        xs = x[b, t0:t0 + T, :]
        xh = np.zeros((16, D), np.float32)
        if t0 >= 8:
            xh[0:8] = x[b, t0 - 8:t0]
        if t0 + T + 8 <= S:
            xh[8:16] = x[b, t0 + T:t0 + T + 8]
        vec = np.zeros((128, NV), np.float32)
        vec[:, V_NG:V_NG + 48] = _pk(np.asarray(inp["norm_g"]).reshape(-1))
        vec[:, V_FG:V_FG + 8] = _pk(inp["final_g"])
        vec[:, V_PB:V_PB + 8] = _pk(inp["pool_b"][0])
        vec[:, V_PS:V_PS + 8] = _pk(inp["pool_scale"][0])
        vec[:, V_SD:V_SD + 8] = _pk(inp["ssm_d"][0])
        vec[:, V_GB:V_GB + 16] = _pk(inp["glu_b"][0])
        vec[:, V_MB:V_MB + 144] = _pk(np.asarray(inp["mod_b"]).reshape(-1))
        vec[:, V_C:V_C + 8] = _pk(inp["c"][b])
        vec[:, V_EDGE] = 1.0 if t0 >= 8 else 0.0
        vec[:, V_EDGE + 1] = 1.0 if t0 + T + 8 <= S else 0.0
        for wi, w in enumerate(POOLW):
            left, right = w // 2, w - 1 - w // 2
            for e in range(16):
                t = t0 + e if e < 8 else t0 + T - 16 + e
                lo, hi = max(t - left, 0), min(t + right, S - 1)
                vec[:, V_ICE + wi * 16 + e] = 1.0 / float(hi - lo + 1)
        for i in range(3):
            jf, jb = pos - 3 + i, pos + 3 - i
            if 0 <= jf < CPS:
                vec[:, V_CM + (0 * 3 + i) * 4 + jf] = 1.0
            if 0 <= jb < CPS:
                vec[:, V_CM + (1 * 3 + i) * 4 + jb] = 1.0
        m = dict(shared)
        m["xT"] = np.ascontiguousarray(xs.T)
        m["xTh"] = np.ascontiguousarray(xh.T)
        m["vecs"] = vec
        maps.append(m)
    return maps


class K:
    def __init__(self, cfg):
        self.cfg = cfg
        T = cfg.T
        nc = self.nc = bass.Bass("TRN2", target_bir_lowering=False)
        es = self.es = ExitStack()
        P = self.P = Prog(nc, es)

        def din(name, shape, dt=F32):
            return nc.dram_tensor(name, list(shape), dt, kind="ExternalInput").ap()

        self.d_xT = din("xT", [D, T])
        self.d_xTh = din("xTh", [D, 16])
        self.d_vecs = din("vecs", [128, NV])
        self.d_cb = din("cb", [128, NCB], BF16)
        self.d_ssmp = din("ssmp", [128, 192])
        self.d_ssmbc = din("ssmbc", [128, 4096])
        self.d_modw = din("mod_w", [2, D, 9 * D])
        self.d_win = din("w_in", [2, 2, D, 2 * DFF])
        self.d_wout = din("w_out", [2, 2, DFF, D])
        self.d_poolw = din("pool_w", [4, 256, 256])
        self.d_gluw = din("glu_w", [D, 2 * D])
        if cfg.mode == "A":
            self.d_xTo = nc.dram_tensor("xTo", [D, T], F32, kind="ExternalOutput").ap()
            self.d_hTo = nc.dram_tensor("hTo", [D, T], BF16, kind="ExternalOutput").ap()
            self.d_Eo = nc.dram_tensor("Eo", [128, 128], F32, kind="ExternalOutput").ap()
        else:
            self.d_out = nc.dram_tensor("outT", [D, T], F32, kind="ExternalOutput").ap()
        if cfg.mode == "B":
            self.d_hTi = din("hTi", [D, T], BF16)
            self.d_Eg = din("Eg", [128, cfg.CPS * 128])
        self.d_BA = nc.dram_tensor("BAd", [32, 128, 1024], BF16, kind="Internal").ap()
        self.d_CA = nc.dram_tensor("CAd", [32, 128, 1024], BF16, kind="Internal").ap()
        self.d_T = nc.dram_tensor("Td", [64, 128, 512], BF16, kind="Internal").ap()
        self.d_bounce = nc.dram_tensor("ebounce", [128, 128], F32).ap()
        self.d_gath = nc.dram_tensor("egath", [cfg.CPS * 128, 128], F32).ap()

        def sb(name, shape, dt=F32):
            return es.enter_context(nc.sbuf_tensor(name, list(shape), dt))

        self.xT = sb("xT_sb", [128, KT, T])
        self.TH = T + 16
        AR_H = KT * self.TH // 2
        self.AR_R = 9216
        self.AR = sb("arena", [128, max(AR_H + self.AR_R, 16896)])
        self.hT = self.AR[:, 0:AR_H].bitcast(BF16).rearrange("p (k t) -> p k t", k=KT)
        self.R0 = AR_H
        self.wstage = sb("wstage", [128, 3, 1024])
        self.wbf = sb("wbf", [128, 6, 4096], BF16)
        self.cb = sb("cb_sb", [128, NCB], BF16)
        self.vecs = sb("vecs_sb", [128, NV])
        self.modT = sb("modT", [128, 2, 72])
        self.der = sb("der", [128, 2, 3, 3, KT])
        self.xTh = sb("xTh_sb", [128, KT, 16])
        self.hTh = sb("hTh_sb", [128, KT, 16], BF16)
        self.condb = sb("condb", [128, KT], BF16)
        self.modrow = sb("modrow", [1, 1, 256])
        self.one11 = sb("one11", [1, 1])
        self.epsc = sb("epsc", [128, 1])
        self.pv = sb("poolvec", [128, 2, KT])
        self.scan = sb("scanc", [128, 4, 128])
        self.ps = [es.enter_context(nc.psum_tensor(f"ps{i}", [128, 512], F32)) for i in range(8)]
        self.slot_free = [None] * 3
        self.si = 0
        self.buf_free = [None] * 6
        c = self.cb
        self.cE = c[:, 0:1920].rearrange("p (g u h) -> p g u h", g=8, u=15)
        self.cF = c[:, 1920:3840].rearrange("p (g u h) -> p g u h", g=8, u=15)
        self.cI = c[:, 3840:3968]
        self.cMLE = c[:, 3968:4096]
        self.cMGE = c[:, 4096:4224]
        self.cONE = c[:, 4224:4352]
        self.sig_x = None
        self.sig_c = None

    def R(self, off, n):
        assert off + n <= self.AR_R, (off, n)
        return self.AR[:, self.R0 + off:self.R0 + off + n]

    def Rb(self, off, n):
        assert off + n // 2 <= self.AR_R
        return self.AR[:, self.R0 + off:self.R0 + off + n // 2].bitcast(BF16)

    def barrier(self, engs=("pe", "act", "dve", "pool", "sp")):
        P = self.P
        sigs = []
        for e in ("pe", "act", "dve", "pool"):
            nm = "m_" + e
            if nm in P.cnt and P.cnt[nm] > 0:
                sigs.append((nm, P.cnt[nm]))
        for e in engs:
            for s in sigs:
                if s[0] != "m_" + e:
                    P.wait(e, s)

    def witem(self, src, dst, npart, a, b, dst_wait=None, cast_eng="pool"):
        P = self.P
        s = self.si % 3
        self.si += 1
        st = self.wstage[:npart, s, 0:a * b].rearrange("p (a b) -> p a b", a=a)
        sd = P.op("sp", lambda e: e.dma_start(out=st, in_=src), waits=[self.slot_free[s]], dma="wd%d" % s)
        w = [sd]
        if dst_wait is not None:
            w.append(dst_wait)
        sc = P.op(cast_eng, lambda e: e.tensor_copy(out=dst, in_=st) if cast_eng != "act"
                  else e.activation(out=dst, in_=st, func=AF.Copy),
                  waits=w, reads=["wstage%d" % s], writes=["wbf_%d" % self.si])
        self.slot_free[s] = sc
        return sc

    def startup(self):
        P, cfg = self.P, self.cfg
        q = "act"
        s1 = P.op(q, lambda e: e.dma_start(out=self.vecs[:], in_=self.d_vecs[:, :]), dma="ld")
        s2 = P.op(q, lambda e: e.dma_start(out=self.cb[:], in_=self.d_cb[:, :]), dma="ld")
        self.sig_c = s2
        if cfg.do_s5:
            xs_ = self.xT[:].rearrange("p k t -> p (k t)")
            raw_ = self.wbf[:, 4:6, :].rearrange("p a n -> p (a n)").bitcast(F32)
            self.sig_sp = P.op(q, lambda e: e.dma_start(out=xs_[:, 3200:3392], in_=self.d_ssmp[:, :]), dma="ldg1")
            self.sig_sbc = P.op(q, lambda e: e.dma_start(out=raw_, in_=self.d_ssmbc[:, :]), dma="ldg2")
        if not cfg.do_s5:
            self.load_x()
        self.sig_one = P.op("dve", lambda e: e.memset(self.one11[:], 1.0), writes=["one11"])
        self.sig_eps = P.op("dve", lambda e: e.memset(self.epsc[:], EPS), writes=["epsc"])
        self.sig_cond = P.op("act", lambda e: e.activation(out=self.condb[:], in_=self.vecs[:, V_C:V_C + 8],
                                                           func=AF.Silu), waits=[s2], writes=["condb"])

    def load_x(self):
        P = self.P
        q = "act"
        xv = self.d_xT.rearrange("(k p) t -> p k t", p=128)
        sx = None
        for k in range(KT):
            sx = P.op(q, lambda e, k=k: e.dma_start(out=self.xT[:, k, :], in_=xv[:, k, :]), dma="ldx")
        sx = P.op(q, lambda e: e.dma_start(out=self.xTh[:], in_=self.d_xTh.rearrange("(k p) t -> p k t", p=128)),
                  dma="ldx")
        self.sig_x = sx

    def mod_block(self, L, col0, ncol, dst, b, cast_eng, pr, pt, defer=False):
        P = self.P
        src = self.d_modw[L].rearrange("(k p) n -> p k n", p=128)
        kper = 1024 // ncol
        sc = None
        for i in range(8 // kper):
            sc = self.witem(src[:, kper * i:kper * (i + 1), col0:col0 + ncol], dst[:, kper * i:kper * (i + 1), :],
                            128, kper, ncol, dst_wait=self.mbuf_free[b] if i == 0 else None, cast_eng=cast_eng)
        sg = None
        for k in range(KT):
            sg = P.op("pe", lambda e, k=k: e.matmul(pr, lhsT=self.condb[:, k:k + 1], rhs=dst[:, k, :],
                                                    start=(k == 0), stop=(k == KT - 1)),
                      waits=[sc, self.sig_cond, self.mod_ps_free] + list(self.o_free) if k == 0 else (),
                      mark=(k == KT - 1))
        self.mbuf_free[b] = sg
        r = 0
        s_ev = P.op("act", lambda e: e.activation(out=self.modrow[0:1, r, 0:ncol], in_=pr, func=AF.Copy),
                    waits=[sg, self.modrow_free[r]], writes=["modrow%d" % r])
        s_t = None
        nt = ncol // 128
        for j in range(nt):
            s_t = P.op("pe", lambda e, j=j: e.matmul(pt[:, j:j + 1], lhsT=self.modrow[0:1, r, j * 128:(j + 1) * 128],
                                                     rhs=self.one11[0:1, 0:1], start=True, stop=True),
                       waits=[s_ev, self.modT_ps_free, self.sig_one, self.pss_free] if j == 0 else (), mark=(j == nt - 1))
        self.modrow_free[r] = s_t
        self.mod_ps_free = s_ev
        if defer:
            return s_t
        c0 = col0 // 128
        s_m = P.op("dve", lambda e: e.tensor_tensor(out=self.modT[:, L, c0:c0 + nt], in0=pt[:, 0:nt],
                                                    in1=self.vecs[:, V_MB + L * 72 + c0:V_MB + L * 72 + c0 + nt],
                                                    op=ALU.add),
                   waits=[s_t], writes=["modT"])
        self.modT_ps_free = s_m
        return s_m

    def mod_init(self, nb):
        self.mod_ri = 0
        self.mbuf_free = [None] * nb
        self.modrow_free = [None, None]
        self.mod_ps_free = None
        self.modT_ps_free = None

    def mod_derive(self, L, sig):
        P = self.P
        out = None
        for s in range(3):
            mt = self.modT[:, L, s * 24:(s + 1) * 24].rearrange("p (m k) -> p m k", m=3)
            ng = self.vecs[:, V_NG + (L * 3 + s) * 8:V_NG + (L * 3 + s) * 8 + 8]
            P.op("dve", lambda e, mt=mt, ng=ng, s=s: e.scalar_tensor_tensor(
                out=self.der[:, L, s, 0, :], in0=mt[:, 1, :], scalar=1.0, in1=ng, op0=ALU.add, op1=ALU.mult),
                waits=[sig], writes=["der"])
            P.op("dve", lambda e, mt=mt, s=s: e.tensor_copy(out=self.der[:, L, s, 1, :], in_=mt[:, 0, :]),
                 writes=["der"])
            out = P.op("dve", lambda e, mt=mt, s=s: e.tensor_scalar(
                out=self.der[:, L, s, 2, :], in0=mt[:, 2, :], scalar1=(1.0 if s == 1 else 0.5), scalar2=None,
                op0=ALU.mult), writes=["der"])
        return out

    def norm_block(self, xsrc, n, dst, L, s, waits=(), final=False, outbuf=None):
        P = self.P
        i = self.nb_i
        self.nb_i += 1
        xsq = self.Rb(3072, KT * 512).rearrange("p (k t) -> p k t", k=KT)
        rstd = self.R(7168 + (i % 2) * 512, 512)
        pss = self.ps[6]
        s_sq = P.op("act", lambda e: e.activation(out=xsq[:, :, 0:n], in_=xsrc, func=AF.Square),
                    waits=list(waits) + [self.xsq_free[0]], writes=["xsq"])
        s_mm = None
        for k in range(KT):
            s_mm = P.op("pe", lambda e, k=k: e.matmul(pss[:, 0:n], lhsT=self.cONE, rhs=xsq[:, k, 0:n],
                                                      start=(k == 0), stop=(k == KT - 1)),
                        waits=[s_sq, self.pss_free, self.sig_c, self.modT_ps_free] + list(self.o_free)
                        if k == 0 else (), mark=(k == KT - 1))
        self.xsq_free[0] = s_mm
        s_q = P.op("act", lambda e: e.activation(out=rstd[:, 0:n], in_=pss[:, 0:n], func=AF.Sqrt, bias=self.epsc[:, 0:1]),
                   waits=[s_mm, self.rstd_free[i % 2], self.sig_eps], writes=["rstd%d" % (i % 2)])
        self.pss_free = s_q
        s_r = P.op("dve", lambda e: e.reciprocal(out=rstd[:, 0:n], in_=rstd[:, 0:n]),
                   waits=[s_q], writes=["rstd%d" % (i % 2)])
        last = None
        for k in range(KT):
            if final:
                gk = self.vecs[:, V_FG + k:V_FG + k + 1]
                last = P.op("dve", lambda e, k=k, gk=gk: e.scalar_tensor_tensor(
                    out=outbuf[:, k, 0:n], in0=xsrc[:, k, :], scalar=gk, in1=rstd[:, 0:n], op0=ALU.mult, op1=ALU.mult),
                    reads=["rstd%d" % (i % 2)], writes=["outbuf"])
            else:
                G = self.der[:, L, s, 0, k:k + 1]
                Sh = self.der[:, L, s, 1, k:k + 1]
                j = self.tmp_i
                self.tmp_i += 1
                tmp = self.R(8192 + (j % 2) * 512, 512)
                s_t = P.op("dve", lambda e, k=k, tmp=tmp: e.tensor_tensor(out=tmp[:, 0:n], in0=xsrc[:, k, :],
                                                                          in1=rstd[:, 0:n], op=ALU.mult),
                           waits=[self.tmp_free[j % 2]], reads=["rstd%d" % (i % 2)], writes=["tmpn%d" % (j % 2)])
                last = P.op("act", lambda e, k=k, tmp=tmp, G=G, Sh=Sh: e.activation(
                    out=dst[:, k, :], in_=tmp[:, 0:n], func=AF.Identity, bias=Sh, scale=G),
                    waits=[s_t], writes=["hT"])
                self.tmp_free[j % 2] = last
        self.rstd_free[i % 2] = last
        return last

    def norm_init(self):
        self.nb_i = 0
        self.tmp_i = 0
        self.xsq_free = [None, None]
        self.rstd_free = [None, None]
        self.tmp_free = [None, None]
        self.pss_free = None

    def norm_all(self, L, s, waits=(), halo=False):
        cfg = self.cfg
        sigs = []
        for tb in range(cfg.NTB):
            c0 = tb * 512
            sigs.append(self.norm_block(self.xT[:, :, c0:c0 + 512], 512, self.hT[:, :, 8 + c0:8 + c0 + 512], L, s,
                                        waits=waits))
        if halo:
            sigs.append(self.norm_block(self.xTh[:, :, :], 16, self.hTh[:, :, :], L, s, waits=waits))
        return sigs

    def ffn(self, L, j, s, h_sigs, halo=False, extra=None):
        P, cfg = self.P, self.cfg
        chunks = [[0, 1, 2, 3], [4, 5, 6, 7], [8, 9, 10, 11], [12, 13, 14, 15], [16, 17, 18, 19], [20, 21]]
        win = self.d_win[L, j].rearrange("(k p) n -> p k n", p=128)
        wout = self.d_wout[L, j]
        nblk = cfg.NTB + (1 if halo else 0)
        units = []
        nob = 2 if extra is not None else 4

        def load_chunk(q):
            F = chunks[q]
            f0 = F[0] * 128
            W = sum(128 if f < 21 else 64 for f in F)
            bidx = [self.wb_i % 6, (self.wb_i + 1) % 6, (self.wb_i + 2) % 6]
            self.wb_i += 3
            g = self.wbf[:, bidx[0], :].rearrange("p (k n) -> p k n", k=8)
            u = self.wbf[:, bidx[1], :].rearrange("p (k n) -> p k n", k=8)
            o = self.wbf[:, bidx[2], :].rearrange("p (f n) -> p f n", f=4)
            items = []
            for (dst, cbase, bi) in ((g, f0, bidx[0]), (u, DFF + f0, bidx[1])):
                for i in range(4):
                    items.append((win[:, 2 * i:2 * i + 2, cbase:cbase + W], dst[:, 2 * i:2 * i + 2, 0:W], 128, 2, W,
                                  bi if i == 0 else None))
            for fi, f in enumerate(F):
                rows = 128 if f < 21 else 64
                items.append((wout[f * 128:f * 128 + rows, :].rearrange("p (a n) -> p a n", a=1),
                              o[:rows, fi:fi + 1, :], rows, 1, 1024, bidx[2] if fi == 0 else None))
            return dict(F=F, g=g, u=u, o=o, bidx=bidx, ready=[None, None], items=items, n0=len(items))

        def pump(ck, n):
            while n > 0 and ck["items"]:
                src, dst, npart, a, b, bi = ck["items"].pop(0)
                k = self.cast_i
                self.cast_i += 1
                eng = "pool" if k % 3 == 2 else "dve"
                sc = self.witem(src, dst, npart, a, b, dst_wait=self.buf_free[bi] if bi is not None else None,
                                cast_eng=eng)
                ck["ready"][0 if eng == "dve" else 1] = sc
                n -= 1

        def blk_cols(tb):
            if tb < cfg.NTB:
                return (self.hT[:, :, 8 + tb * 512:8 + tb * 512 + 512], self.xT[:, :, tb * 512:tb * 512 + 512], 512)
            return (self.hTh[:, :, :], self.xTh[:, :, :], 16)

        def emit_gu(ck, tb, ui):
            hv, xv, n = blk_cols(tb)
            ab = ui % 2
            actT = self.Rb(ab * 1024, 4 * 512).rearrange("p (f t) -> p f t", f=4)
            sig = None
            for fi, f in enumerate(ck["F"]):
                rows = 128 if f < 21 else 64
                pi = self.gu_i % 2
                self.gu_i += 1
                pg, pu = self.ps[pi], self.ps[2 + pi]
                w0 = list(ck["ready"]) + [h_sigs[tb], self.gu_free[pi]]
                if fi == 0:
                    w0.append(self.act_free[ab])
                for (pt, wt) in ((pg, ck["g"]), (pu, ck["u"])):
                    for k in range(KT):
                        sig = P.op("pe", lambda e, pt=pt, wt=wt, k=k, fi=fi, rows=rows: e.matmul(
                            pt[:rows, 0:n], lhsT=wt[:, k, fi * 128:fi * 128 + rows], rhs=hv[:, k, :],
                            start=(k == 0), stop=(k == KT - 1)),
                            waits=w0 if (pt is pg and k == 0) else (), mark=(pt is pu and k == KT - 1))
                sgb = self.R(2048 + pi * 512, 512)
                s_a = P.op("act", lambda e, pg=pg, sgb=sgb, rows=rows: e.activation(
                    out=sgb[:rows, 0:n], in_=pg[:rows, 0:n], func=AF.Silu),
                    waits=[sig, self.sg_free[pi]], writes=["sg%d" % pi])
                s_d = P.op("dve", lambda e, pu=pu, sgb=sgb, rows=rows, fi=fi, actT=actT: e.tensor_tensor(
                    out=actT[:rows, fi, 0:n], in0=sgb[:rows, 0:n], in1=pu[:rows, 0:n], op=ALU.mult),
                    waits=[s_a], writes=["actT%d" % ab])
                self.gu_free[pi] = s_d
                self.sg_free[pi] = s_d
            return s_d

        def emit_out(ck, tb, ui, act_sig, lastchunk):
            hv, xv, n = blk_cols(tb)
            ab = ui % 2
            actT = self.Rb(ab * 1024, 4 * 512).rearrange("p (f t) -> p f t", f=4)
            F = ck["F"]
            sig = None
            s_x = None
            for dj in range(KT):
                pi = self.o_i % nob
                self.o_i += 1
                po = self.ps[4 + pi]
                for fi, f in enumerate(F):
                    rows = 128 if f < 21 else 64
                    sig = P.op("pe", lambda e, po=po, fi=fi, rows=rows, dj=dj: e.matmul(
                        po[:, 0:n], lhsT=ck["o"][:rows, fi, dj * 128:(dj + 1) * 128], rhs=actT[:rows, fi, 0:n],
                        start=(fi == 0), stop=(fi == len(F) - 1)),
                        waits=[act_sig, self.o_free[pi], self.pss_free, self.modT_ps_free, self.mod_ps_free]
                        if fi == 0 else (), mark=(fi == len(F) - 1))
                Gt = self.der[:, L, s, 2, dj:dj + 1]
                s_x = P.op("dve", lambda e, po=po, dj=dj, Gt=Gt: e.scalar_tensor_tensor(
                    out=xv[:, dj, :], in0=po[:, 0:n], scalar=Gt, in1=xv[:, dj, :], op0=ALU.mult, op1=ALU.add),
                    waits=[sig], writes=["xT%d" % dj])
                self.o_free[pi] = s_x
            self.act_free[ab] = sig
            return sig, s_x

        seq = [(q, tb) for q in range(len(chunks)) for tb in range(nblk)]
        cks = {}
        cks[0] = load_chunk(0)
        pump(cks[0], 1000)
        done = [None] * nblk
        prev = None
        for ui, (q, tb) in enumerate(seq):
            a_sig = emit_gu(cks[q], tb, self.unit_i + ui)
            if prev is not None:
                pq, ptb, pui, pa = prev
                pe_sig, x_sig = emit_out(cks[pq], ptb, pui, pa, pq == len(chunks) - 1)
                if ptb == nblk - 1:
                    for b in cks[pq]["bidx"]:
                        self.buf_free[b] = pe_sig
                if pq == len(chunks) - 1:
                    done[ptb] = x_sig
            prev = (q, tb, self.unit_i + ui, a_sig)
            if q + 1 < len(chunks):
                if tb == 0:
                    cks[q + 1] = load_chunk(q + 1)
                    if extra is not None:
                        extra(q)
                per = -(-cks[q + 1]["n0"] // nblk)
                pump(cks[q + 1], 1000 if tb == nblk - 1 else per)
        pq, ptb, pui, pa = prev
        pe_sig, x_sig = emit_out(cks[pq], ptb, pui, pa, True)
        for b in cks[pq]["bidx"]:
            self.buf_free[b] = pe_sig
        done[ptb] = x_sig
        self.unit_i += len(seq)
        return done

    def ffn_init(self):
        self.wb_i = 0
        self.cast_i = 0
        self.gu_i = 0
        self.o_i = 0
        self.unit_i = 0
        self.gu_free = [None, None]
        self.sg_free = [None, None]
        self.o_free = [None, None, None, None]
        self.act_free = [None, None]

    def pool_mixer(self, L, h_sigs):
        P, cfg = self.P, self.cfg
        T, TH = cfg.T, self.TH
        hw = h_sigs
        sl = P.op("dve", lambda e: e.tensor_scalar(out=self.hT[:, :, 0:8], in0=self.hTh[:, :, 0:8],
                                                   scalar1=self.vecs[:, V_EDGE:V_EDGE + 1], scalar2=None, op0=ALU.mult),
                  waits=hw, writes=["hT"])
        sr = P.op("dve", lambda e: e.tensor_scalar(out=self.hT[:, :, 8 + T:16 + T], in0=self.hTh[:, :, 8:16],
                                                   scalar1=self.vecs[:, V_EDGE + 1:V_EDGE + 2], scalar2=None,
                                                   op0=ALU.mult), writes=["hT"])
        b = self.wb_i % 6
        self.wb_i += 1
        pw = self.wbf[:, b, 0:2048].rearrange("p (k g d) -> p k g d", k=2, g=4)
        src = self.d_poolw.rearrange("g (k p) d -> p k g d", p=128)
        scw = None
        for kk in range(2):
            scw = self.witem(src[:, kk, :, :], pw[:, kk, :, :], 128, 4, 256,
                             dst_wait=self.buf_free[b] if kk == 0 else None)
        Gt = self.der[:, L, 1, 2, :]
        P.op("dve", lambda e: e.tensor_tensor(out=self.pv[:, 0, :], in0=Gt, in1=self.vecs[:, V_PS:V_PS + 8],
                                              op=ALU.mult), writes=["pv"])
        s_pv = P.op("dve", lambda e: e.tensor_tensor(out=self.pv[:, 1, :], in0=self.pv[:, 0, :],
                                                     in1=self.vecs[:, V_PB:V_PB + 8], op=ALU.mult),
                    reads=["pv"], writes=["pv"])
        sa = self.R(0, TH)
        sbb = self.R(TH, TH)
        done = [None] * cfg.NTB
        pT_free = [None, None]
        last_pe = None
        for gi, w in enumerate(POOLW):
            pT = self.Rb(2 * TH + 32 + (gi % 2) * T, 2 * T).rearrange("p (k t) -> p k t", k=2)
            s_p = None
            for kk in range(2):
                k = 2 * gi + kk
                hk = self.hT[:, k, :]
                sg_ = P.op("dve", lambda e, hk=hk: e.tensor_tensor(out=sa[:, 1:TH], in0=hk[:, 0:TH - 1], in1=hk[:, 1:TH],
                                                                   op=ALU.add), waits=[sl, sr], reads=["hT"], writes=["sa"])
                cur, oth = sa, sbb
                curn, othn = "sa", "sb"
                step = 1
                lo, hi = 1, TH
                ww = 2
                while ww < w:
                    nlo, nhi = lo + step, hi - step
                    P.op("dve", lambda e, cur=cur, oth=oth, nlo=nlo, nhi=nhi, step=step: e.tensor_tensor(
                        out=oth[:, nlo:nhi], in0=cur[:, nlo - step:nhi - step], in1=cur[:, nlo + step:nhi + step],
                        op=ALU.add), reads=[curn], writes=[othn])
                    cur, oth = oth, cur
                    curn, othn = othn, curn
                    lo, hi = nlo, nhi
                    step *= 2
                    ww *= 2
                P.op("dve", lambda e, cur=cur, hk=hk, kk=kk, pT=pT, w=w: e.scalar_tensor_tensor(
                    out=pT[:, kk, :], in0=cur[:, 8:8 + T], scalar=1.0 / w, in1=hk[:, 8:8 + T],
                    op0=ALU.mult, op1=ALU.subtract), waits=[pT_free[gi % 2]], reads=[curn], writes=["pT"])
                ice = self.vecs[:, V_ICE + gi * 16:V_ICE + gi * 16 + 16]
                for (c0, e0) in ((8, 0), (T, 8)):
                    tmpe = self.R(2 * TH, 8)
                    P.op("dve", lambda e, cur=cur, c0=c0, e0=e0, ice=ice, tmpe=tmpe: e.tensor_tensor(
                        out=tmpe[:, 0:8], in0=cur[:, c0:c0 + 8], in1=ice[:, e0:e0 + 8], op=ALU.mult),
                        reads=[curn, "tmpe"], writes=["tmpe"])
                    s_p = P.op("dve", lambda e, hk=hk, c0=c0, kk=kk, pT=pT, tmpe=tmpe: e.tensor_tensor(
                        out=pT[:, kk, c0 - 8:c0], in0=tmpe[:, 0:8], in1=hk[:, c0:c0 + 8], op=ALU.subtract),
                        reads=["tmpe", "pT"], writes=["pT"])
            for dl in range(2):
                dj = 2 * gi + dl
                for tb in range(cfg.NTB):
                    pi = self.o_i % 2
                    self.o_i += 1
                    po = self.ps[4 + pi]
                    sig = None
                    for kk in range(2):
                        sig = P.op("pe", lambda e, po=po, kk=kk, gi=gi, dl=dl, tb=tb, pT=pT: e.matmul(
                            po[:, 0:512], lhsT=pw[:, kk, gi, dl * 128:(dl + 1) * 128],
                            rhs=pT[:, kk, tb * 512:(tb + 1) * 512], start=(kk == 0), stop=(kk == 1)),
                            waits=[s_p, scw, self.o_free[pi]] if kk == 0 else (), mark=(kk == 1))
                    last_pe = sig
                    xv = self.xT[:, dj, tb * 512:(tb + 1) * 512]
                    P.op("dve", lambda e, po=po, xv=xv, dj=dj: e.scalar_tensor_tensor(
                        out=xv, in0=po[:, 0:512], scalar=self.pv[:, 0, dj:dj + 1], in1=xv, op0=ALU.mult, op1=ALU.add),
                        waits=[sig, s_pv], writes=["xTa"])
                    s_x = P.op("dve", lambda e, xv=xv, dj=dj: e.tensor_scalar(
                        out=xv, in0=xv, scalar1=self.pv[:, 1, dj:dj + 1], scalar2=None, op0=ALU.add),
                        reads=["xTa"], writes=["xTa"])
                    self.o_free[pi] = s_x
                    done[tb] = s_x
            pT_free[gi % 2] = last_pe
        self.buf_free[b] = last_pe
        return done

    def final(self, waits_per_tb):
        P, cfg = self.P, self.cfg
        ov = self.d_out.rearrange("(k p) t -> p k t", p=128)
        st = None
        ob_free = [None, None]
        for tb in range(cfg.NTB):
            ob = self.wbf[:, 2 * (tb % 2):2 * (tb % 2) + 2, :].rearrange("p a n -> p (a n)").bitcast(F32) \
                .rearrange("p (k t) -> p k t", k=KT)
            sig = self.norm_block(self.xT[:, :, tb * 512:(tb + 1) * 512], 512, None, 0, 0,
                                  waits=[waits_per_tb[tb], ob_free[tb % 2]], final=True, outbuf=ob)
            st = P.op("sp", lambda e, ob=ob, tb=tb: e.dma_start(out=ov[:, :, tb * 512:(tb + 1) * 512], in_=ob),
                      waits=[sig], dma="st")
            ob_free[tb % 2] = st
        P.wait("sp", st)
        for e in ("pe", "act", "dve", "pool"):
            P.wait(e, st)

    def build(self):
        P, cfg = self.P, self.cfg
        self.norm_init()
        self.ffn_init()
        self.startup()
        if cfg.mode == "B":
            self.mod_init(3)
            sm = None
            for cbi in range(36):
                b = cbi % 3
                dst = self.wbf[:, b, 0:2048].rearrange("p (k n) -> p k n", k=8)
                sm = self.mod_block(1, cbi * 256, 256, dst, b, "act", self.ps[7][0:1, 0:256], self.ps[6][:, 0:2])
                self.buf_free[b] = self.mbuf_free[b]
            d1 = self.mod_derive(1, sm)
            self.s5_gen()
            self.barrier()
            self.load_x()
            s_h = P.op("act", lambda e: e.dma_start(out=self.hT[:, :, 8:8 + cfg.T],
                                                    in_=self.d_hTi.rearrange("(k p) t -> p k t", p=128)), dma="ldh")
            for e in ("pe", "act", "dve", "pool", "sp"):
                P.wait(e, s_h)
                P.wait(e, self.sig_x)
            self.barrier()
            self.wb_i = 0
            done = self.s5_mixer(1, [s_h, d1])
            hs = self.norm_all(1, 2, waits=done)
            done = self.ffn(1, 1, 2, hs)
            self.barrier()
            self.final(done)
            P.emit()
            return self.nc
        self.mod_init(3)
        sm = None
        for cbi in range(36):
            b = cbi % 3
            dst = self.wbf[:, b, 0:2048].rearrange("p (k n) -> p k n", k=8)
            sm = self.mod_block(0, cbi * 256, 256, dst, b, "act", self.ps[7][0:1, 0:256],
                                self.ps[6][:, 2 * cbi:2 * cbi + 2], defer=True)
            self.buf_free[b] = self.mbuf_free[b]
        sm = P.op("dve", lambda e: e.tensor_tensor(out=self.modT[:, 0, :], in0=self.ps[6][:, 0:72],
                                                   in1=self.vecs[:, V_MB:V_MB + 72], op=ALU.add),
                  waits=[sm, self.sig_c], writes=["modT"])
        self.modT_ps_free = sm
        d0 = self.mod_derive(0, sm)
        if cfg.do_s5:
            self.s5_gen()
            self.barrier()
            self.load_x()
        self.barrier()
        self.wb_i = 0
        hs = self.norm_all(0, 0, waits=[self.sig_x, d0], halo=True)
        done = self.ffn(0, 0, 0, hs, halo=True)
        hs = self.norm_all(0, 1, waits=done, halo=True)
        self.barrier()
        done = self.pool_mixer(0, hs)
        self.barrier()
        hs = self.norm_all(0, 2, waits=done)
        self.mod_init(2)
        st = {"cb": 0, "sm": None}

        def mod1():
            cbi = st["cb"]
            b = cbi % 2
            dst = self.Rb(5120 + b * 1024, 8 * 256).rearrange("p (k n) -> p k n", k=8)
            st["sm"] = self.mod_block(1, cbi * 256, 256, dst, b, "pool", self.ps[7][0:1, 0:256], self.ps[6][:, 0:2])
            st["cb"] += 1

        def extra(q):
            for _ in range(6 if q < 4 else 12):
                if st["cb"] < 36:
                    mod1()

        done = self.ffn(0, 1, 2, hs, extra=extra)
        while st["cb"] < 36:
            mod1()
        d1 = self.mod_derive(1, st["sm"])
        hs = self.norm_all(1, 0, waits=list(done) + [d1])
        done = self.ffn(1, 0, 0, hs)
        if cfg.do_s5 and not (cfg.dbg or "").startswith("g"):
            hs = self.norm_all(1, 1, waits=done)
            self.barrier()
            done = self.s5_mixer(1, hs)
            if cfg.mode == "A":
                P.emit()
                return self.nc
            self.barrier()
        hs = self.norm_all(1, 2, waits=done)
        done = self.ffn(1, 1, 2, hs)
        self.barrier()
        self.final(done)
        P.emit()
        return self.nc


def build_nc(cfg):
    k = K(cfg)
    return k.build()


def kernel(**inputs):
    cfg = Cfg(mode="fused")
    maps = prep_inputs(cfg, inputs)
    nc = build_nc(cfg)
    res = run_bass_kernel_spmd(nc, maps, core_ids=list(range(cfg.NCORES)))
    out = np.zeros((cfg.NCORES // cfg.CPS, cfg.S, D), np.float32)
    for k in range(cfg.NCORES):
        b, pos = k // cfg.CPS, k % cfg.CPS
        out[b, pos * cfg.T:(pos + 1) * cfg.T, :] = np.asarray(res.results[k]["outT"], np.float32).T
    return out


def kernel_two_launch(**inputs):
    cfgA = Cfg(mode="A")
    cfgB = Cfg(mode="B")
    maps = prep_inputs(cfgA, inputs)
    ncA = build_nc(cfgA)
    resA = run_bass_kernel_spmd(ncA, maps, core_ids=list(range(cfgA.NCORES)))
    mapsB = []
    for k in range(cfgB.NCORES):
        b = k // cfgB.CPS
        m = dict(maps[k])
        m["xT"] = np.ascontiguousarray(resA.results[k]["xTo"])
        m["hTi"] = np.ascontiguousarray(resA.results[k]["hTo"])
        m["Eg"] = np.ascontiguousarray(np.concatenate(
            [np.asarray(resA.results[b * cfgB.CPS + j]["Eo"], np.float32) for j in range(cfgB.CPS)], axis=1))
        mapsB.append(m)
    ncB = build_nc(cfgB)
    resB = run_bass_kernel_spmd(ncB, mapsB, core_ids=list(range(cfgB.NCORES)))
    out = np.zeros((cfgB.NCORES // cfgB.CPS, cfgB.S, D), np.float32)
    for k in range(cfgB.NCORES):
        b, pos = k // cfgB.CPS, k % cfgB.CPS
        out[b, pos * cfgB.T:(pos + 1) * cfgB.T, :] = np.asarray(resB.results[k]["outT"], np.float32).T
    return out


def bc_last(ap, n):
    return AP(tensor=ap.tensor, offset=ap.offset, ap=[list(x) for x in ap.ap] + [[0, n]])


def bc_mid(ap, n):
    a = [list(x) for x in ap.ap]
    return AP(tensor=ap.tensor, offset=ap.offset, ap=[a[0], [0, n]] + a[1:])


def _s5_gen(self):
    P, cfg = self.P, self.cfg
    NCH = cfg.NCH
    xs = self.xT[:].rearrange("p k t -> p (k t)")
    ar_ = self.AR

    def xt(off, n):
        return xs[:, off:off + n]

    PWR = xt(0, 1088).rearrange("p (k c) -> p k c", k=17)
    PWI = xt(1088, 1088).rearrange("p (k c) -> p k c", k=17)
    IPR = xt(2176, 512).rearrange("p (k c) -> p k c", k=8)
    IPI = xt(2688, 512).rearrange("p (k c) -> p k c", k=8)
    m = [xt(3200 + i * 64, 64) for i in range(14)]
    negpi = self.pv[:, 0, 0:1]
    raw = self.wbf[:, 4:6, :].rearrange("p a n -> p (a n)").bitcast(F32)
    bre = raw[:, 0:1024].rearrange("p (c h) -> p c h", h=16)
    bim = raw[:, 1024:2048].rearrange("p (c h) -> p c h", h=16)
    cre = raw[:, 2048:3072].rearrange("p (c h) -> p c h", h=16)
    cim = raw[:, 3072:4096].rearrange("p (c h) -> p c h", h=16)
    bb = self.wbf[:, 3, :].bitcast(F32)
    Bbr = bb[:, 0:1024].rearrange("p (c h) -> p c h", h=16)
    Bbi = bb[:, 1024:2048].rearrange("p (c h) -> p c h", h=16)

    s_p = self.sig_sp
    s_bc = self.sig_sbc
    s_np = P.op("dve", lambda e: e.memset(negpi, -math.pi), writes=["pv"])

    st = {"a": None, "d": s_np}

    def A(out, in_, func, bias=None):
        kw = {}
        if bias is not None:
            kw["bias"] = bias
        st["a"] = P.op("act", lambda e: e.activation(out=out, in_=in_, func=func, **kw),
                       waits=[st["d"], s_p], reads=["g"], writes=["g"])

    def Dtt(out, a, b, op, eng="dve"):
        st["d"] = P.op(eng, lambda e: e.tensor_tensor(out=out, in0=a, in1=b, op=op),
                       waits=[st["a"], s_p, s_bc, st["d"]], reads=["g"], writes=["g"])

    def Dts(out, a, s1, op0, s2=None, op1=None, eng="dve"):
        if op1 is None:
            st["d"] = P.op(eng, lambda e: e.tensor_scalar(out=out, in0=a, scalar1=s1, scalar2=None, op0=op0),
                           waits=[st["a"], s_p, st["d"]], reads=["g"], writes=["g"])
        else:
            st["d"] = P.op(eng, lambda e: e.tensor_scalar(out=out, in0=a, scalar1=s1, scalar2=s2, op0=op0, op1=op1),
                           waits=[st["a"], s_p, st["d"]], reads=["g"], writes=["g"])

    def Drec(out, a):
        st["d"] = P.op("dve", lambda e: e.reciprocal(out=out, in_=a), waits=[st["a"], st["d"]], reads=["g"],
                       writes=["g"])

    MUL, ADD, SUB = ALU.mult, ALU.add, ALU.subtract
    A(m[2], m[2], AF.Exp)
    Dtt(m[3], m[0], m[2], MUL)
    Dtt(m[4], m[1], m[2], MUL)
    A(m[3], m[3], AF.Exp)
    I32 = mybir.dt.int32

    def reduce_pi(dst, src, shift):
        Dts(m[9], src, shift, ADD)
        Dts(m[10], m[9], 1.0 / (2 * math.pi), MUL)
        st["d"] = P.op("dve", lambda e: e.tensor_copy(out=m[11].bitcast(I32), in_=m[10]), waits=[st["d"]],
                       reads=["g"], writes=["g"])
        st["d"] = P.op("dve", lambda e: e.tensor_copy(out=m[10], in_=m[11].bitcast(I32)), reads=["g"], writes=["g"])
        st["d"] = P.op("dve", lambda e: e.scalar_tensor_tensor(out=dst, in0=m[10], scalar=-2 * math.pi, in1=m[9],
                                                              op0=MUL, op1=ADD), reads=["g"], writes=["g"])
        Dts(m[10], dst, math.pi, ALU.is_gt, 2 * math.pi, MUL)
        Dtt(dst, dst, m[10], SUB)
        Dts(m[10], dst, -math.pi, ALU.is_lt, 2 * math.pi, MUL)
        Dtt(dst, dst, m[10], ADD)

    reduce_pi(m[5], m[4], 0.0)
    reduce_pi(m[6], m[4], 0.5 * math.pi)
    A(m[5], m[5], AF.Sin)
    A(m[6], m[6], AF.Sin)
    Dtt(m[7], m[3], m[6], MUL)
    Dtt(m[8], m[3], m[5], MUL)
    Dtt(m[9], m[0], m[0], MUL)
    Dtt(m[10], m[1], m[1], MUL)
    Dtt(m[9], m[9], m[10], ADD)
    Drec(m[11], m[9])
    Dts(m[9], m[7], -1.0, ADD)
    Dtt(m[10], m[9], m[0], MUL)
    Dtt(m[12], m[8], m[1], MUL)
    Dtt(m[12], m[10], m[12], ADD)
    Dtt(m[12], m[12], m[11], MUL)
    Dtt(m[10], m[8], m[0], MUL)
    Dtt(m[13], m[9], m[1], MUL)
    Dtt(m[13], m[10], m[13], SUB)
    Dtt(m[13], m[13], m[11], MUL)
    tg = [ar_[:, 12288 + i * 1024:12288 + (i + 1) * 1024].rearrange("p (c h) -> p c h", h=16) for i in range(4)]
    frb, fib = bc_last(m[12], 16), bc_last(m[13], 16)
    Dtt(tg[0], bre, frb, MUL)
    Dtt(tg[1], bim, fib, MUL)
    Dtt(Bbr, tg[0], tg[1], SUB)
    Dtt(tg[2], bim, frb, MUL)
    Dtt(tg[3], bre, fib, MUL)
    Dtt(Bbi, tg[2], tg[3], ADD)
    P.op("dve", lambda e: e.memset(PWR[:, 0, :], 1.0), reads=["g"], writes=["g"])
    P.op("dve", lambda e: e.memset(PWI[:, 0, :], 0.0), reads=["g"], writes=["g"])
    P.op("dve", lambda e: e.memset(IPR[:, 0, :], 1.0), reads=["g"], writes=["g"])
    P.op("dve", lambda e: e.memset(IPI[:, 0, :], 0.0), reads=["g"], writes=["g"])
    P.op("dve", lambda e: e.tensor_copy(out=PWR[:, 1, :], in_=m[7]), reads=["g"], writes=["g"])
    P.op("dve", lambda e: e.tensor_copy(out=PWI[:, 1, :], in_=m[8]), reads=["g"], writes=["g"])

    def cmul(orr, oii, xr, xi, yr, yi, t1, t2):
        Dtt(t1, xr, yr, MUL)
        Dtt(t2, xi, yi, MUL)
        Dtt(orr, t1, t2, SUB)
        Dtt(t1, xr, yi, MUL)
        Dtt(t2, xi, yr, MUL)
        Dtt(oii, t1, t2, ADD)

    for k in range(2, 17):
        cmul(PWR[:, k, :], PWI[:, k, :], PWR[:, k - 1, :], PWI[:, k - 1, :], m[7], m[8], m[9], m[10])
    Dtt(m[9], m[3], m[3], MUL)
    Drec(m[9], m[9])
    Dtt(IPR[:, 1, :], m[7], m[9], MUL)
    Dtt(m[5], m[8], m[9], MUL)
    Dts(IPI[:, 1, :], m[5], -1.0, MUL)
    for k in range(2, 8):
        cmul(IPR[:, k, :], IPI[:, k, :], IPR[:, k - 1, :], IPI[:, k - 1, :], IPR[:, 1, :], IPI[:, 1, :], m[9], m[10])
    AA, BB, ZA, ZB = (self.scan[:, i, :] for i in range(4))
    P.op("dve", lambda e: e.tensor_copy(out=AA[:, 0:64], in_=PWR[:, 16, :]), reads=["g"], writes=["scan"])
    P.op("dve", lambda e: e.tensor_copy(out=AA[:, 64:128], in_=PWR[:, 16, :]), reads=["g"], writes=["scan"])
    P.op("dve", lambda e: e.tensor_scalar(out=BB[:, 0:64], in0=PWI[:, 16, :], scalar1=-1.0, scalar2=None,
                                          op0=MUL), reads=["g"], writes=["scan"])
    P.op("dve", lambda e: e.tensor_copy(out=BB[:, 64:128], in_=PWI[:, 16, :]), reads=["g"], writes=["scan"])
    Dtt(m[0], PWR[:, 16, :], PWR[:, 0, :], MUL)
    Dtt(m[1], PWI[:, 16, :], PWR[:, 0, :], MUL)
    nsq = int(round(math.log2(NCH)))
    assert 2 ** nsq == NCH
    for _ in range(nsq):
        Dtt(m[9], m[0], m[1], MUL)
        Dtt(m[10], m[0], m[0], MUL)
        Dtt(m[6], m[1], m[1], MUL)
        Dtt(m[0], m[10], m[6], SUB)
        Dts(m[1], m[9], 2.0, MUL)
    P.op("dve", lambda e: e.tensor_copy(out=ZA[:, 0:64], in_=m[0]), reads=["g"], writes=["scan"])
    P.op("dve", lambda e: e.tensor_copy(out=ZA[:, 64:128], in_=m[0]), reads=["g"], writes=["scan"])
    P.op("dve", lambda e: e.tensor_scalar(out=ZB[:, 0:64], in0=m[1], scalar1=-1.0, scalar2=None, op0=MUL),
         reads=["g"], writes=["scan"])
    s_small = P.op("dve", lambda e: e.tensor_copy(out=ZB[:, 64:128], in_=m[1]), reads=["g"], writes=["scan"])

    if cfg.dbg == "g1":
        for e in ("pe", "act", "dve", "pool", "sp"):
            P.wait(e, s_small)
        return
    bufA = ar_[:, 0:4096].bitcast(BF16)
    bufC = ar_[:, 4096:8192].bitcast(BF16)
    bufD = ar_[:, 8192:12288].bitcast(BF16)
    stg = [ar_[:, 16384 + i * 256:16384 + (i + 1) * 256].bitcast(BF16).rearrange("p (a n) -> p a n", a=4)
           for i in range(2)]

    def bview(buf):
        return buf.rearrange("p (r g s h) -> p r g s h", r=2, g=32, s=8)

    def gen_tab(name, buf, dirn, Xr, Xi, PR, PI, kfun, neg_im, waits):
        bv = bview(buf)
        sigs = []
        for s_lo in range(8):
            eng = "dve"
            tb = 12288 + (0 if eng == "dve" else 2048)
            T = [ar_[:, tb + i * 512:tb + (i + 1) * 512].rearrange("p (c h) -> p c h", h=16) for i in range(4)]
            k = kfun(s_lo)
            pr = bc_last(PR[:, k, dirn * 32:(dirn + 1) * 32], 16)
            pi = bc_last(PI[:, k, dirn * 32:(dirn + 1) * 32], 16)
            xr = Xr[:, dirn * 32:(dirn + 1) * 32, :]
            xi = Xi[:, dirn * 32:(dirn + 1) * 32, :]
            tag = "gt_" + eng
            w = list(waits) + [s_small, st["d"], s_bc]
            P.op(eng, lambda e, T=T, xr=xr, pr=pr: e.tensor_tensor(out=T[0], in0=xr, in1=pr, op=MUL), waits=w,
                 reads=[tag], writes=[tag])
            P.op(eng, lambda e, T=T, xi=xi, pi=pi: e.tensor_tensor(out=T[1], in0=xi, in1=pi, op=MUL),
                 reads=[tag], writes=[tag])
            P.op(eng, lambda e, T=T, bv=bv, s_lo=s_lo: e.tensor_tensor(out=bv[:, 0, :, s_lo, :], in0=T[0], in1=T[1],
                                                                       op=SUB), reads=[tag], writes=[tag, name])
            P.op(eng, lambda e, T=T, xi=xi, pr=pr: e.tensor_tensor(out=T[2], in0=xi, in1=pr, op=MUL),
                 reads=[tag], writes=[tag])
            P.op(eng, lambda e, T=T, xr=xr, pi=pi: e.tensor_tensor(out=T[3], in0=xr, in1=pi, op=MUL),
                 reads=[tag], writes=[tag])
            if neg_im and eng == "dve":
                sg = P.op(eng, lambda e, T=T, bv=bv, s_lo=s_lo: e.scalar_tensor_tensor(
                    out=bv[:, 1, :, s_lo, :], in0=T[2], scalar=-1.0, in1=T[3], op0=MUL, op1=SUB),
                    reads=[tag], writes=[tag, name])
            elif neg_im:
                P.op(eng, lambda e, T=T: e.tensor_scalar(out=T[2], in0=T[2], scalar1=-1.0, scalar2=None, op0=MUL),
                     reads=[tag], writes=[tag])
                sg = P.op(eng, lambda e, T=T, bv=bv, s_lo=s_lo: e.tensor_tensor(
                    out=bv[:, 1, :, s_lo, :], in0=T[2], in1=T[3], op=SUB), reads=[tag], writes=[tag, name])
            else:
                sg = P.op(eng, lambda e, T=T, bv=bv, s_lo=s_lo: e.tensor_tensor(
                    out=bv[:, 1, :, s_lo, :], in0=T[2], in1=T[3], op=ADD), reads=[tag], writes=[tag, name])
            sigs.append(sg)
        return sigs[-1:]

    evi = [0]
    stg_free = [None, None]
    psb_free = [None, None]

    def t_blocks(dirn, bufL, bufR, blk, mask, w_in):
        L, Rr = bview(bufL), bview(bufR)
        last_pe = None
        dT2 = self.d_T.rearrange("(q t) p n -> t q p n", t=2)
        for g0 in range(0, 64, 4):
            i = evi[0] % 2
            evi[0] += 1
            pst = self.ps[i][:, :].rearrange("p (a n) -> p a n", a=4)
            sig = None
            g2 = g0 // 32
            gq0 = (g0 % 32)
            for gg in range(4):
                gq = gq0 + gg
                for ri in range(2):
                    sig = P.op("pe", lambda e, gg=gg, gq=gq, g2=g2, ri=ri, pst=pst, L=L, Rr=Rr: e.matmul(
                        pst[:, gg, :], lhsT=L[64 * g2:64 * g2 + 64, ri, gq, :, :].rearrange("p s h -> p (s h)"),
                        rhs=Rr[64 * g2:64 * g2 + 64, ri, gq, :, :].rearrange("p s h -> p (s h)"),
                        start=(ri == 0), stop=(ri == 1)),
                        waits=list(w_in) + [psb_free[i]] if (gg == 0 and ri == 0) else (),
                        mark=(gg == 3 and ri == 1))
            last_pe = sig
            sb_ = stg[i]
            if mask is not None:
                s_e = P.op("dve", lambda e, pst=pst, sb_=sb_: e.tensor_tensor(out=sb_, in0=pst, in1=bc_mid(mask, 4),
                                                                             op=MUL),
                           waits=[sig, stg_free[i], self.sig_c], writes=["stg%d" % i])
            else:
                s_e = P.op("act", lambda e, pst=pst, sb_=sb_: e.activation(out=sb_, in_=pst, func=AF.Copy),
                           waits=[sig, stg_free[i]], writes=["stg%d" % i])
            psb_free[i] = s_e
            dst = dT2[g2, gq0:gq0 + 4, :, blk * 128:(blk + 1) * 128].rearrange("g p n -> p g n")
            stg_free[i] = P.op("sp", lambda e, dst=dst, sb_=sb_: e.dma_start(out=dst, in_=sb_), waits=[s_e],
                               dma="tbl%d" % i)
        return last_pe

    def ba_out(dirn, s_hi, buf, w_in):
        bv = bview(buf)
        last_pe = None
        for ri in range(2):
            for q0 in range(0, 32, 4):
                i = evi[0] % 2
                evi[0] += 1
                pst = self.ps[i][:, 0:256].bitcast(BF16).rearrange("p (a n) -> p a n", a=4)
                sig = None
                for qq in range(4):
                    sig = P.op("pe", lambda e, qq=qq, q0=q0, ri=ri, pst=pst, bv=bv: e.transpose(
                        pst[:, qq, :], bv[:, ri, q0 + qq, :, :].rearrange("p s h -> p (s h)"), self.cI),
                        waits=list(w_in) + [psb_free[i], self.sig_c] if qq == 0 else (), mark=(qq == 3))
                last_pe = sig
                sb_ = stg[i]
                s_e = P.op("act", lambda e, pst=pst, sb_=sb_: e.activation(out=sb_, in_=pst, func=AF.Copy),
                           waits=[sig, stg_free[i]], writes=["stg%d" % i])
                psb_free[i] = s_e
                col = ((dirn * 2 + s_hi) * 2 + ri) * 128
                dst = self.d_BA[q0:q0 + 4, :, col:col + 128].rearrange("g p n -> p g n")
                stg_free[i] = P.op("sp", lambda e, dst=dst, sb_=sb_: e.dma_start(out=dst, in_=sb_), waits=[s_e],
                                   dma="tbl%d" % i)
        return last_pe

    def ca_out(dirn, mt, buf, w_in):
        bv = bview(buf)
        sg = None
        for ri in range(2):
            col = ((dirn * 2 + mt) * 2 + ri) * 128
            for q0 in range(0, 32, 4):
                dst = self.d_CA[q0:q0 + 4, :, col:col + 128].rearrange("g p n -> p g n")
                sg = P.op("sp", lambda e, dst=dst, ri=ri, bv=bv, q0=q0: e.dma_start(
                    out=dst, in_=bv[:, ri, q0:q0 + 4, :, :].rearrange("p g s h -> p g (s h)")), waits=list(w_in),
                    dma="tblc")
        return sg

    free = {"A": [], "C": [], "D": []}
    for dirn in range(2):
        hA = 1 if dirn == 0 else 0
        hD = 0 if dirn == 0 else 1
        if dirn == 0:
            kBA = lambda s_hi: (lambda s_lo: 15 - (8 * s_hi + s_lo))
            kCA = lambda mt: (lambda t_lo: 8 * mt + t_lo + 1)
            kC0 = lambda t_lo: 7 - t_lo
            mask = self.cMLE
        else:
            kBA = lambda s_hi: (lambda s_lo: 8 * s_hi + s_lo)
            kCA = lambda mt: (lambda t_lo: 16 - (8 * mt + t_lo))
            kC0 = lambda t_lo: t_lo
            mask = self.cMGE
        sA = gen_tab("bA", bufA, dirn, Bbr, Bbi, PWR, PWI, kBA(hA), False, free["A"])
        sC = gen_tab("bC", bufC, dirn, cre, cim, IPR, IPI, kC0, True, free["C"])
        sD = gen_tab("bD", bufD, dirn, cre, cim, PWR, PWI, kCA(hD), True, free["D"])
        if cfg.dbg == "g2":
            for e in ("pe", "act", "dve", "pool", "sp"):
                for sg_ in sA + sC + sD:
                    P.wait(e, sg_)
            return
        pe1 = t_blocks(dirn, bufA, bufC, 0 + dirn, mask, sA + sC)
        pe2 = t_blocks(dirn, bufA, bufD, 2 + dirn, None, sA + sD)
        if cfg.dbg == "g3":
            for e in ("pe", "act", "dve", "pool", "sp"):
                for sg_ in [pe1, pe2, stg_free[0], stg_free[1]]:
                    P.wait(e, sg_)
            return
        pe3 = ba_out(dirn, hA, bufA, sA)
        if cfg.dbg == "g4":
            for e in ("pe", "act", "dve", "pool", "sp"):
                for sg_ in [pe3, stg_free[0], stg_free[1]]:
                    P.wait(e, sg_)
            return
        d1 = ca_out(dirn, hD, bufD, sD)
        sC2 = gen_tab("bC", bufC, dirn, Bbr, Bbi, PWR, PWI, kBA(1 - hA), False, [pe1])
        pe4 = ba_out(dirn, 1 - hA, bufC, sC2)
        sA2 = gen_tab("bA", bufA, dirn, cre, cim, PWR, PWI, kCA(1 - hD), True, [pe3, pe2])
        d2 = ca_out(dirn, 1 - hD, bufA, sA2)
        free["A"] = [d2]
        free["C"] = [pe4]
        free["D"] = [d1, pe2]
    fin = [stg_free[0], stg_free[1], free["A"][0], free["D"][0]]
    for e in ("pe", "act", "dve", "pool", "sp"):
        for s in fin:
            P.wait(e, s)


K.s5_gen = _s5_gen


def _s5_views(self):
    cfg = self.cfg
    NCH, T = cfg.NCH, cfg.T
    v = {}
    v["SX"] = self.Rb(0, (NCH + 1) * 128).rearrange("p (j c) -> p j c", c=128)
    o = (NCH + 1) * 64
    v["Q"] = [self.R(o + i * 128, 128) for i in range(2)]
    v["t"] = [self.R(o + 256 + i * 128, 128) for i in range(3)]
    v["W"] = self.wbf[:, 0:4, :].rearrange("p a n -> p (a n)")[:, 0:64 * 2 * NCH].rearrange(
        "p (g k c) -> p g k c", g=64, k=2)
    misc = self.wbf[:, 4:6, :].rearrange("p a n -> p (a n)")
    v["ring"] = [misc[:, i * 1024:(i + 1) * 1024].rearrange("p (a n) -> p a n", a=8) for i in range(2)]
    v["Tr"] = [misc[:, 2048 + i * 1024:2048 + (i + 1) * 1024].rearrange("p (g b n) -> p g b n", g=2, b=4)
               for i in range(2)]
    v["YW"] = misc[:, 4096:4096 + 16 * NCH].rearrange("p (g m c) -> p g m c", g=8, m=2)
    mf = misc[:, 6144:8192].bitcast(F32)
    v["Eg"] = mf[:, 0:512].rearrange("p (r c) -> p r c", c=128)
    v["F"] = [mf[:, 512 + i * 128:512 + (i + 1) * 128] for i in range(3)]
    ws = self.wstage[:].rearrange("p a n -> p (a n)")
    v["ybuf"] = ws[:, 0:T]
    v["gt"] = [ws[:, 2048 + i * 512:2048 + (i + 1) * 512] for i in range(2)]
    return v


def swap_ri(ap):
    a = [list(x) for x in ap.ap]
    assert a[1] == [1, 128], a
    return AP(tensor=ap.tensor, offset=ap.offset + 64, ap=[a[0], [-64, 2], [1, 64]])


def as3(ap):
    a = [list(x) for x in ap.ap]
    return AP(tensor=ap.tensor, offset=ap.offset, ap=[a[0], [64, 2], [1, 64]])


def _cols(ap, d):
    a = [list(x) for x in ap.ap]
    return AP(tensor=ap.tensor, offset=ap.offset + d * 32, ap=[a[0], [64, 2], [1, 32]])


def _s5_p1(self, h_sigs):
    P, cfg = self.P, self.cfg
    NCH = cfg.NCH
    v = self.s5v
    SX, W = v["SX"], v["W"]
    s0 = P.op("dve", lambda e: e.memset(SX[:, 0, :], 0.0), writes=["SX0"])
    ring_free = [None, None]
    psw_free = [None, None]
    pss_free = [None, None]
    last = s0
    hw = list(h_sigs)
    for gq in range(32):
        r = gq % 2
        kt, gi0 = gq // 4, 2 * (gq % 4)
        BAr = v["ring"][r]
        s_ba = P.op("sp", lambda e, gq=gq, BAr=BAr: e.dma_start(out=BAr.rearrange("p a n -> p (a n)"),
                                                               in_=self.d_BA[gq]), waits=[ring_free[r]],
                    dma="ba%d" % r)
        psW = self.ps[r][:, 0:4 * NCH].rearrange("p (g k c) -> p g k c", g=2, k=2)
        sig = None
        hrow = self.hT[:, kt, :]
        pstr = list(hrow.ap[0])
        for g2 in range(2):
            gi = gi0 + g2
            for ks in range(2):
                for s_lo in range(8):
                    rhs = AP(tensor=hrow.tensor, offset=hrow.offset + 8 + 8 * ks + s_lo, ap=[pstr, [16, NCH]])
                    first = (g2 == 0 and ks == 0 and s_lo == 0)
                    lastm = (g2 == 1 and ks == 1 and s_lo == 7)
                    sig = P.op("pe", lambda e, g2=g2, ks=ks, s_lo=s_lo, gi=gi, rhs=rhs, psW=psW: e.matmul(
                        psW[:, g2, ks, :], lhsT=self.cE[:, gi, 7 - s_lo:15 - s_lo, :].rearrange("p u h -> p (u h)"),
                        rhs=rhs, start=(s_lo == 0), stop=(s_lo == 7)),
                        waits=hw + [psw_free[r], self.sig_c] if first else (), mark=lastm)
        s_w = P.op("act", lambda e, gq=gq, psW=psW: e.activation(out=W[:, 2 * gq:2 * gq + 2, :, :], in_=psW,
                                                                func=AF.Copy), waits=[sig], writes=["W"])
        psw_free[r] = s_w
        psS = self.ps[2 + r][:, 0:4 * NCH].rearrange("p (a c) -> p a c", a=4)
        sig = None
        for dirn in range(2):
            for ri in range(2):
                for g2 in range(2):
                    g = 2 * gq + g2
                    for ks in range(2):
                        first = (dirn == 0 and ri == 0 and g2 == 0 and ks == 0)
                        lastm = (dirn == 1 and ri == 1 and g2 == 1 and ks == 1)
                        sig = P.op("pe", lambda e, dirn=dirn, ri=ri, g2=g2, g=g, ks=ks, psS=psS, BAr=BAr: e.matmul(
                            psS[64 * g2:64 * g2 + 64, dirn * 2 + ri, :],
                            lhsT=BAr[:, (dirn * 2 + ks) * 2 + ri, 64 * g2:64 * g2 + 64], rhs=W[:, g, ks, :],
                            start=(ks == 0), stop=(ks == 1)),
                            waits=[s_w, s_ba, pss_free[r]] if first else (), mark=lastm)
        ring_free[r] = sig
        for dirn in range(2):
            col = dirn * 32 + gq
            if dirn == 0:
                o = AP(tensor=SX.tensor, offset=SX.offset + 128 + col, ap=[list(SX.ap[0]), [64, 2], [128, NCH]])
            else:
                o = AP(tensor=SX.tensor, offset=SX.offset + NCH * 128 + col, ap=[list(SX.ap[0]), [64, 2], [-128, NCH]])
            last = P.op("dve", lambda e, o=o, dirn=dirn, psS=psS: e.tensor_copy(out=o, in_=psS[:, 2 * dirn:2 * dirn + 2, :]),
                        waits=[sig], writes=["SXs"])
        pss_free[r] = last
    return last, ring_free


def _s5_scan(self, init, w_in):
    P, cfg = self.P, self.cfg
    NCH = cfg.NCH
    v = self.s5v
    SX = v["SX"]
    Q, t = v["Q"], v["t"]
    AA, BB = self.scan[:, 0, :], self.scan[:, 1, :]
    MUL, ADD = ALU.mult, ALU.add
    if init is None:
        s = P.op("dve", lambda e: e.memset(Q[0], 0.0), waits=w_in, writes=["Q0"])
    else:
        s = P.op("dve", lambda e: e.tensor_copy(out=Q[0], in_=init), waits=w_in, writes=["Q0"])
        P.op("dve", lambda e: e.tensor_copy(out=SX[:, 0, :], in_=init), writes=["SX0"])
    sx_w = None
    for j in range(1, NCH + 1):
        qp, qn = Q[(j - 1) % 2], Q[j % 2]
        tp, tn = "Q%d" % ((j - 1) % 2), "Q%d" % (j % 2)
        P.op("dve", lambda e, qp=qp: e.tensor_tensor(out=t[0], in0=qp, in1=AA, op=MUL), waits=w_in if j == 1 else (),
             reads=[tp], writes=["t0"])
        P.op("dve", lambda e, qp=qp: e.tensor_tensor(out=as3(t[1]), in0=swap_ri(qp), in1=as3(BB), op=MUL),
             reads=[tp], writes=["t1"])
        P.op("dve", lambda e: e.tensor_tensor(out=t[2], in0=t[0], in1=t[1], op=ADD), reads=["t0", "t1"], writes=["t2"])
        s = P.op("dve", lambda e, qn=qn, j=j: e.tensor_tensor(out=qn, in0=t[2], in1=SX[:, j, :], op=ADD),
                 waits=[sx_w] if sx_w is not None else (), reads=["t2", "SXs"], writes=[tn])
        sx_w = P.op("act", lambda e, qn=qn, j=j: e.activation(out=SX[:, j, :], in_=qn, func=AF.Copy), waits=[s],
                    writes=["SXa"])
    return Q[NCH % 2], s, sx_w


def _s5_p3(self, L, w_in):
    P, cfg = self.P, self.cfg
    NCH, T = cfg.NCH, cfg.T
    v = self.s5v
    SX, W, YW = v["SX"], v["W"], v["YW"]
    ybuf, gt = v["ybuf"], v["gt"]
    ring_free = [None, None]
    psy_free = [None, None]
    psi_free = [None, None]
    yw_free = None
    ii = 0
    last_g = None
    gelu_free = [None, None]
    for gq in range(32):
        r = gq % 2
        kt, gi0 = gq // 4, 2 * (gq % 4)
        CAr, Tr = v["ring"][r], v["Tr"][r]
        s_ca = P.op("sp", lambda e, gq=gq, CAr=CAr: e.dma_start(out=CAr.rearrange("p a n -> p (a n)"),
                                                               in_=self.d_CA[gq]), waits=[ring_free[r]] + list(w_in),
                    dma="ca%d" % r)
        s_t = P.op("sp", lambda e, gq=gq, Tr=Tr: e.dma_start(
            out=Tr.rearrange("p g b n -> p g (b n)"),
            in_=self.d_T[2 * gq:2 * gq + 2, :, :].rearrange("g p n -> p g n")), dma="tt%d" % r)
        psY = self.ps[r][:, 0:4 * NCH].rearrange("p (g m c) -> p g m c", g=2, m=2)
        sig = None
        for g2 in range(2):
            g = 2 * gq + g2
            for mt in range(2):
                ops = [(Tr[:, g2, 0, :], W[:, g, mt, :]), (Tr[:, g2, 1, :], W[:, g, mt, :])]
                if mt == 1:
                    ops.append((Tr[:, g2, 2, :], W[:, g, 0, :]))
                else:
                    ops.append((Tr[:, g2, 3, :], W[:, g, 1, :]))
                for dirn in range(2):
                    for ri in range(2):
                        col = ri * 64 + dirn * 32 + gq
                        pst = list(SX.ap[0])
                        base = SX.offset + 64 * g2 * pst[0]
                        if dirn == 0:
                            rhs = AP(tensor=SX.tensor, offset=base + col, ap=[[pst[0], 64], [128, NCH]])
                        else:
                            rhs = AP(tensor=SX.tensor, offset=base + (NCH - 1) * 128 + col,
                                     ap=[[pst[0], 64], [-128, NCH]])
                        ops.append((CAr[64 * g2:64 * g2 + 64, (dirn * 2 + mt) * 2 + ri, :], rhs))
                for oi, (lh, rh) in enumerate(ops):
                    first = (g2 == 0 and mt == 0 and oi == 0)
                    lastm = (g2 == 1 and mt == 1 and oi == len(ops) - 1)
                    sig = P.op("pe", lambda e, g2=g2, mt=mt, lh=lh, rh=rh, oi=oi, n=len(ops), psY=psY: e.matmul(
                        psY[:, g2, mt, :], lhsT=lh, rhs=rh, start=(oi == 0), stop=(oi == n - 1)),
                        waits=[s_ca, s_t, psy_free[r]] + list(w_in) if first else (), mark=lastm)
        ring_free[r] = sig
        s_y = P.op("act", lambda e, gi0=gi0, psY=psY: e.activation(out=YW[:, gi0:gi0 + 2, :, :], in_=psY, func=AF.Copy),
                   waits=[sig, yw_free] if gq % 4 == 0 else [sig], writes=["YW"])
        psy_free[r] = s_y
        if gq % 4 != 3:
            continue
        sig = None
        hrow = self.hT[:, kt, :]
        pstr = list(hrow.ap[0])
        s_e = None
        for sgrp in range(4):
            pi = ii % 2
            ii += 1
            psI = self.ps[2 + pi][:, 0:4 * NCH].rearrange("p (s c) -> p s c", s=4)
            for si in range(4):
                s = 4 * sgrp + si
                mt, s_lo = s // 8, s % 8
                for gi in range(8):
                    sig = P.op("pe", lambda e, si=si, mt=mt, s_lo=s_lo, gi=gi, psI=psI: e.matmul(
                        psI[:, si, :], lhsT=self.cF[:, s_lo, 7 - gi:15 - gi, :].rearrange("p u h -> p (u h)"),
                        rhs=YW[:, gi, mt, :], start=(gi == 0), stop=(gi == 7)),
                        waits=[s_y, psi_free[pi]] if (si == 0 and gi == 0) else (), mark=(si == 3 and gi == 7))
            yo = AP(tensor=ybuf.tensor, offset=ybuf.offset + 4 * sgrp, ap=[list(ybuf.ap[0]), [1, 4], [16, NCH]])
            hi = AP(tensor=hrow.tensor, offset=hrow.offset + 8 + 4 * sgrp, ap=[pstr, [1, 4], [16, NCH]])
            s_e = P.op("dve", lambda e, yo=yo, hi=hi, psI=psI, kt=kt: e.scalar_tensor_tensor(
                out=yo, in0=hi, scalar=self.vecs[:, V_SD + kt:V_SD + kt + 1], in1=psI, op0=ALU.mult, op1=ALU.add),
                waits=[sig, last_g], writes=["ybuf"])
            psi_free[pi] = s_e
        yw_free = sig
        for c0 in range(0, T, 512):
            b = (c0 // 512) % 2
            yv = ybuf[:, c0:c0 + 512]
            s1 = P.op("act", lambda e, yv=yv, b=b: e.activation(out=gt[b], in_=yv, func=AF.Square),
                      waits=[s_e, gelu_free[b]], writes=["gt%d" % b])
            P.op("dve", lambda e, b=b: e.tensor_scalar(out=gt[b], in0=gt[b], scalar1=0.044715, scalar2=1.0,
                                                       op0=ALU.mult, op1=ALU.add), waits=[s1], writes=["gt%d" % b])
            s2 = P.op("dve", lambda e, b=b, yv=yv: e.tensor_tensor(out=gt[b], in0=gt[b], in1=yv, op=ALU.mult),
                      reads=["ybuf"], writes=["gt%d" % b])
            s3 = P.op("act", lambda e, b=b: e.activation(out=gt[b], in_=gt[b], func=AF.Sigmoid, scale=1.5957691216),
                      waits=[s2], writes=["gt%d" % b])
            last_g = P.op("dve", lambda e, b=b, yv=yv, kt=kt, c0=c0: e.tensor_tensor(
                out=self.hT[:, kt, 8 + c0:8 + c0 + 512], in0=gt[b], in1=yv, op=ALU.mult),
                waits=[s3], reads=["ybuf"], writes=["hTg"])
            gelu_free[b] = last_g
    return last_g


def _s5_glu(self, L, w_in):
    P, cfg = self.P, self.cfg
    ws = self.wstage[:].rearrange("p a n -> p (a n)")
    tmpa = [ws[:, i * 512:(i + 1) * 512] for i in range(2)]
    tmpb = [ws[:, 1024 + i * 512:1024 + (i + 1) * 512] for i in range(2)]
    gw = self.d_gluw.rearrange("(k p) n -> p k n", p=128)
    done = [None] * cfg.NTB
    pa_free = [None, None]
    pb_free = [None, None]
    ta_free = [None, None]
    n = 0
    last_pe = None
    for ci in range(2):
        bufA = self.wbf[:, 2 * ci, :].rearrange("p (k n) -> p k n", k=8)
        bufB = self.wbf[:, 2 * ci + 1, :].rearrange("p (k n) -> p k n", k=8)
        stg = self.R(0, 4096).rearrange("p (s n) -> p s n", s=4)
        sc = None
        for (dst, cbase) in ((bufA, 512 * ci), (bufB, 1024 + 512 * ci)):
            for i in range(4):
                sl = self.glu_si % 4
                self.glu_si += 1
                st = stg[:, sl, :].rearrange("p (a b) -> p a b", a=2)
                sd = P.op("sp", lambda e, st=st, i=i, cbase=cbase: e.dma_start(
                    out=st, in_=gw[:, 2 * i:2 * i + 2, cbase:cbase + 512]),
                    waits=[self.glu_slot_free[sl]] + list(w_in), dma="gl%d" % sl)
                sc = P.op("pool", lambda e, st=st, dst=dst, i=i: e.tensor_copy(out=dst[:, 2 * i:2 * i + 2, :], in_=st),
                          waits=[sd] + list(w_in), reads=["glst"], writes=["wbf"])
                self.glu_slot_free[sl] = sc
        for tb in range(cfg.NTB):
            gv = self.hT[:, :, 8 + tb * 512:8 + tb * 512 + 512]
            for dl in range(4):
                dj = 4 * ci + dl
                pi = n % 2
                n += 1
                pa, pb = self.ps[pi], self.ps[2 + pi]
                sig = None
                for (pt, wt) in ((pa, bufA), (pb, bufB)):
                    for k in range(KT):
                        sig = P.op("pe", lambda e, pt=pt, wt=wt, k=k, dl=dl, gv=gv: e.matmul(
                            pt[:, 0:512], lhsT=wt[:, k, dl * 128:(dl + 1) * 128], rhs=gv[:, k, :],
                            start=(k == 0), stop=(k == KT - 1)),
                            waits=[sc, pa_free[pi], pb_free[pi]] + list(w_in) if (pt is pa and k == 0) else (),
                            mark=(pt is pb and k == KT - 1))
                last_pe = sig
                s_s = P.op("act", lambda e, pb=pb, pi=pi, dj=dj: e.activation(
                    out=tmpa[pi], in_=pb[:, 0:512], func=AF.Sigmoid, bias=self.vecs[:, V_GB + 8 + dj:V_GB + 9 + dj]),
                    waits=[sig, ta_free[pi]], writes=["ta%d" % pi])
                pb_free[pi] = s_s
                s_m = P.op("dve", lambda e, pa=pa, pi=pi, dj=dj: e.scalar_tensor_tensor(
                    out=tmpb[pi], in0=pa[:, 0:512], scalar=self.vecs[:, V_GB + dj:V_GB + dj + 1], in1=tmpa[pi],
                    op0=ALU.add, op1=ALU.mult), waits=[s_s], writes=["tb%d" % pi])
                pa_free[pi] = s_m
                ta_free[pi] = s_m
                xv = self.xT[:, dj, tb * 512:(tb + 1) * 512]
                sx = P.op("dve", lambda e, pi=pi, xv=xv, dj=dj: e.scalar_tensor_tensor(
                    out=xv, in0=tmpb[pi], scalar=self.der[:, L, 1, 2, dj:dj + 1], in1=xv, op0=ALU.mult, op1=ALU.add),
                    reads=["tb%d" % pi], writes=["xg%d" % dj])
                if ci == 1:
                    done[tb] = sx
    return done, last_pe


K.s5_p1 = _s5_p1
K.s5_scan = _s5_scan
K.s5_p3 = _s5_p3
K.s5_glu = _s5_glu


def _s5_cin(self, w_in):
    P, cfg = self.P, self.cfg
    v = self.s5v
    Eg, F, t = v["Eg"], v["F"], v["t"]
    ZA, ZB = self.scan[:, 2, :], self.scan[:, 3, :]
    MUL, ADD = ALU.mult, ALU.add
    sig = None
    for i in range(3):
        for d in range(2):
            for j in range(cfg.CPS):
                mcol = V_CM + (d * 3 + i) * 4 + j
                msc = self.vecs[:, mcol:mcol + 1]
                if j == 0:
                    sig = P.op("dve", lambda e, i=i, d=d, j=j, msc=msc: e.tensor_scalar(
                        out=_cols(F[i], d), in0=_cols(Eg[:, j, :], d), scalar1=msc, scalar2=None, op0=MUL),
                        waits=w_in, reads=["Eg"], writes=["F%d" % i])
                else:
                    sig = P.op("dve", lambda e, i=i, d=d, j=j, msc=msc: e.scalar_tensor_tensor(
                        out=_cols(F[i], d), in0=_cols(Eg[:, j, :], d), scalar=msc, in1=_cols(F[i], d),
                        op0=MUL, op1=ADD), reads=["Eg", "F%d" % i], writes=["F%d" % i])
    for i in (1, 2):
        P.op("dve", lambda e: e.tensor_tensor(out=t[0], in0=F[0], in1=ZA, op=MUL), reads=["F0"], writes=["t0"])
        P.op("dve", lambda e: e.tensor_tensor(out=as3(t[1]), in0=swap_ri(F[0]), in1=as3(ZB), op=MUL),
             reads=["F0"], writes=["t1"])
        P.op("dve", lambda e: e.tensor_tensor(out=t[2], in0=t[0], in1=t[1], op=ADD), reads=["t0", "t1"], writes=["t2"])
        sig = P.op("dve", lambda e, i=i: e.tensor_tensor(out=F[0], in0=t[2], in1=F[i], op=ADD),
                   reads=["t2", "F%d" % i], writes=["F0"])
    return F[0], sig


def _s5_corr(self, cin, w_in):
    P, cfg = self.P, self.cfg
    NCH = cfg.NCH
    v = self.s5v
    SX, Q, t = v["SX"], v["Q"], v["t"]
    AA, BB = self.scan[:, 0, :], self.scan[:, 1, :]
    MUL, ADD = ALU.mult, ALU.add
    s = P.op("dve", lambda e: e.tensor_copy(out=Q[0], in_=cin), waits=w_in, writes=["Q0"])
    last = P.op("dve", lambda e: e.tensor_copy(out=SX[:, 0, :], in_=cin), waits=list(w_in) + [s], writes=["SXp"])
    for j in range(1, NCH + 1):
        qp, qn = Q[(j - 1) % 2], Q[j % 2]
        tp, tn = "Q%d" % ((j - 1) % 2), "Q%d" % (j % 2)
        P.op("dve", lambda e, qp=qp: e.tensor_tensor(out=t[0], in0=qp, in1=AA, op=MUL), reads=[tp], writes=["t0"])
        P.op("dve", lambda e, qp=qp: e.tensor_tensor(out=as3(t[1]), in0=swap_ri(qp), in1=as3(BB), op=MUL),
             reads=[tp], writes=["t1"])
        s = P.op("dve", lambda e, qn=qn: e.tensor_tensor(out=qn, in0=t[0], in1=t[1], op=ADD),
                 reads=["t0", "t1"], writes=[tn])
        if j < NCH:
            last = P.op("dve", lambda e, qn=qn, j=j: e.tensor_tensor(out=SX[:, j, :], in0=SX[:, j, :], in1=qn, op=ADD),
                        reads=[tn], writes=["SXp"])
    return last


def _s5_mixer(self, L, h_sigs):
    P, cfg = self.P, self.cfg
    self.s5v = _s5_views(self)
    v = self.s5v
    self.glu_si = 0
    self.glu_slot_free = [None] * 4
    s_p1, _ = self.s5_p1(h_sigs)
    self.barrier()
    if cfg.dbg == "p1":
        return [s_p1] * cfg.NTB
    if cfg.mode == "A":
        q, s_q, s_x = self.s5_scan(None, [s_p1])
        o1 = P.op("sp", lambda e: e.dma_start(out=self.d_Eo[:, :], in_=q), waits=[s_q], dma="oA")
        o2 = P.op("sp", lambda e: e.dma_start(out=self.d_xTo.rearrange("(k p) t -> p k t", p=128), in_=self.xT[:]),
                  dma="oA")
        o3 = P.op("sp", lambda e: e.dma_start(out=self.d_hTo.rearrange("(k p) t -> p k t", p=128),
                                              in_=self.hT[:, :, 8:8 + cfg.T]), dma="oA")
        for e in ("sp", "pe", "act", "dve", "pool"):
            P.wait(e, o3)
        return None
    if cfg.mode == "B":
        s_e = P.op("sp", lambda e: e.dma_start(out=v["Eg"].rearrange("p r c -> p (r c)")[:, 0:cfg.CPS * 128],
                                               in_=self.d_Eg[:, :]), dma="ldE")
        cin, s_c = _s5_cin(self, [s_e, s_p1])
        q, s_q, s_x = self.s5_scan(cin, [s_c])
    else:
        q, s_q, s_x = self.s5_scan(None, [s_p1])
        if cfg.do_cc:
            o1 = P.op("sp", lambda e: e.dma_start(out=self.d_bounce[:, :], in_=q), waits=[s_q], dma="cc1")
            groups = [list(range(b * cfg.CPS, (b + 1) * cfg.CPS)) for b in range(cfg.NCORES // cfg.CPS)]
            s_cc = P.op("pool", lambda e: e.collective_compute(
                "AllGather", ALU.bypass, replica_groups=groups, ins=[self.d_bounce.opt()],
                outs=[self.d_gath.opt()]), waits=[o1], dma="ccs", inc=1)
            s_e = P.op("sp", lambda e: e.dma_start(out=v["Eg"][:, 0:cfg.CPS, :],
                                                   in_=self.d_gath.rearrange("(r p) n -> p r n", p=128)),
                       waits=[s_cc], dma="cc2")
            cin, s_c = _s5_cin(self, [s_e, s_x])
            s_x = _s5_corr(self, cin, [s_c, s_x])
    self.barrier()
    if cfg.dbg == "scan":
        return [s_x] * cfg.NTB
    last_g = self.s5_p3(L, [s_x])
    self.barrier()
    if cfg.dbg == "p3":
        self.buf_free = [None] * 6
        self.slot_free = [None] * 3
        self.wb_i = 0
        return [last_g] * cfg.NTB
    done, last_pe = self.s5_glu(L, [last_g])
    self.barrier()
    self.buf_free = [None] * 6
    self.slot_free = [None] * 3
    self.wb_i = 0
    return done


K.s5_mixer = _s5_mixer
```

```python
import math
from contextlib import ExitStack

import numpy as np
import ml_dtypes

import concourse.bass as bass
import concourse.mybir as mybir
from concourse.ap import AP
from concourse.bass_utils import run_bass_kernel_spmd

F32 = mybir.dt.float32
BF16 = mybir.dt.bfloat16
AF = mybir.ActivationFunctionType
ALU = mybir.AluOpType

D = 1024
KT = 8
DFF = 2752
FT = 22
SEQ = 8192
EPS = 1e-6
LCH = 16


class Cfg:
    def __init__(self, T=2048, CPS=4, NCORES=8, do_s5=True, do_cc=True, mode="fused"):
        self.mode = mode
        self.dbg = None
        self.T = T
        self.CPS = CPS
        self.NCORES = NCORES
        self.S = T * CPS
        self.NTB = T // 512
        self.NCH = T // LCH
        self.do_s5 = do_s5
        self.do_cc = do_cc and CPS > 1


class Prog:
    ENG = ("pe", "act", "dve", "pool", "sp")

    def __init__(self, nc, es):
        self.nc, self.es = nc, es
        self.q = {e: [] for e in self.ENG}
        self.sem = {}
        self.cnt = {}
        self.waited = {}
        self.tagw = {e: {} for e in self.ENG}
        self.tagr = {e: {} for e in self.ENG}

    def _sem(self, name):
        if name not in self.sem:
            self.sem[name] = self.es.enter_context(self.nc.semaphore(name))
            self.cnt[name] = 0
        return name

    def wait(self, eng, sig):
        if sig is None:
            return
        name, val = sig
        if self.waited.get((eng, name), 0) >= val:
            return
        self.waited[(eng, name)] = val
        self.q[eng].append(("w", name, val))

    def op(self, eng, fn, waits=(), reads=(), writes=(), mark=None, dma=None, inc=16):
        for s in waits:
            self.wait(eng, s)
        if dma is None and eng != "pe":
            tw, tr = self.tagw[eng], self.tagr[eng]
            for t in reads:
                if t in tw:
                    self.wait(eng, tw[t])
            for t in writes:
                if t in tw:
                    self.wait(eng, tw[t])
                if t in tr:
                    self.wait(eng, tr[t])
        if dma is not None:
            name = self._sem(dma)
            self.cnt[name] += inc
            sig = (name, self.cnt[name])
            self.q[eng].append(("o", fn, name, inc))
            return sig
        if eng == "pe" and not mark:
            self.q[eng].append(("o", fn, None, 0))
            return None
        name = self._sem("m_" + eng)
        self.cnt[name] += 1
        sig = (name, self.cnt[name])
        self.q[eng].append(("o", fn, name, 1))
        if eng != "pe":
            for t in reads:
                self.tagr[eng][t] = sig
            for t in writes:
                self.tagw[eng][t] = sig
        return sig

    def emit(self):
        with self.nc.Block() as block:
            for eng, deco in (("pe", block.tensor), ("act", block.scalar), ("dve", block.vector),
                              ("pool", block.gpsimd), ("sp", block.sync)):
                def f(e, eng=eng):
                    for it in self.q[eng]:
                        if it[0] == "w":
                            e.wait_ge(self.sem[it[1]], it[2])
                        else:
                            ins = it[1](e)
                            if it[2] is not None:
                                ins.then_inc(self.sem[it[2]], it[3])
                deco(f)


def sub(ap, p0, p1):
    return ap[p0:p1]


def _pk(v):
    v = np.asarray(v, np.float32).reshape(-1, 128)
    return np.ascontiguousarray(v.T)


def _gen_layout(a):
    a = np.asarray(a, np.float32)
    rest = a.shape[3:]
    a = a.reshape(2, 32, 2, 64, *rest)
    a = np.moveaxis(a, [2, 3], [0, 1])
    return np.ascontiguousarray(a.reshape(128, 64, *rest))


def const_tables():
    k = np.arange(128)
    hk, gk, sk = k % 16, k // 16, k // 16
    E = np.zeros((128, 8, 15, 16), np.float32)
    Fm = np.zeros((128, 8, 15, 16), np.float32)
    for kk in range(128):
        E[kk, gk[kk], 7, hk[kk]] = 1.0
        Fm[kk, sk[kk], 7, hk[kk]] = 1.0
    ident = np.eye(128, dtype=np.float32)
    s_lo = (np.arange(128) // 16)[:, None]
    t_lo = (np.arange(128) // 16)[None, :]
    mle = (s_lo <= t_lo).astype(np.float32)
    mge = (s_lo >= t_lo).astype(np.float32)
    ones = np.full((128, 128), 1.0 / D, np.float32)
    cb = np.concatenate([E.reshape(128, -1), Fm.reshape(128, -1), ident, mle, mge, ones], axis=1)
    return cb.astype(ml_dtypes.bfloat16)


NCB = 2 * 1920 + 4 * 128
V_NG = 0
V_FG = 48
V_PB = 56
V_PS = 64
V_SD = 72
V_GB = 80
V_MB = 96
V_C = 240
V_EDGE = 248
V_ICE = 250
V_CM = 314
NV = 338
POOLW = (2, 4, 8, 16)


def prep_inputs(cfg, inp):
    T, CPS = cfg.T, cfg.CPS
    S = cfg.S
    x = np.asarray(inp["x"], np.float32)
    cb = const_tables()
    ssm = {}
    ssm["lamre"] = _gen_layout(inp["ssm_lam_re"][0])
    ssm["lamim"] = _gen_layout(inp["ssm_lam_im"][0])
    ld = np.asarray(inp["ssm_log_dt"][0], np.float32)
    ssm["logdt"] = _gen_layout(np.broadcast_to(ld[:, :, None], (2, 64, 64)))
    ssm["bre"] = _gen_layout(inp["ssm_b_re"][0])
    ssm["bim"] = _gen_layout(inp["ssm_b_im"][0])
    ssm["cre"] = _gen_layout(np.swapaxes(np.asarray(inp["ssm_c_re"][0]), 2, 3))
    ssm["cim"] = _gen_layout(np.swapaxes(np.asarray(inp["ssm_c_im"][0]), 2, 3))
    ssmp = np.concatenate([ssm["lamre"], ssm["lamim"], ssm["logdt"]], axis=1)
    ssmbc = np.concatenate([ssm[k].reshape(128, -1) for k in ("bre", "bim", "cre", "cim")], axis=1)
    shared = {
        "mod_w": np.ascontiguousarray(inp["mod_w"], np.float32),
        "w_in": np.ascontiguousarray(inp["ffn_w_in"], np.float32),
        "w_out": np.ascontiguousarray(inp["ffn_w_out"], np.float32),
        "pool_w": np.ascontiguousarray(inp["pool_w"][0], np.float32),
        "glu_w": np.ascontiguousarray(inp["glu_w"][0], np.float32),
        "cb": cb, "ssmp": np.ascontiguousarray(ssmp), "ssmbc": np.ascontiguousarray(ssmbc),
    }
    maps = []
    for k in range(cfg.NCORES):
        b, pos = k // CPS, k % CPS
        t0 = pos * T
        xs = x[b, t0:t0 + T, :]
        xh = np.zeros((16, D), np.float32)
        if t0 >= 8:
            xh[0:8] = x[b, t0 - 8:t0]
        if t0 + T + 8 <= S:
            xh[8:16] = x[b, t0 + T:t0 + T + 8]
        vec = np.zeros((128, NV), np.float32)
        vec[:, V_NG:V_NG + 48] = _pk(np.asarray(inp["norm_g"]).reshape(-1))
        vec[:, V_FG:V_FG + 8] = _pk(inp["final_g"])
        vec[:, V_PB:V_PB + 8] = _pk(inp["pool_b"][0])
        vec[:, V_PS:V_PS + 8] = _pk(inp["pool_scale"][0])
        vec[:, V_SD:V_SD + 8] = _pk(inp["ssm_d"][0])
        vec[:, V_GB:V_GB + 16] = _pk(inp["glu_b"][0])
        vec[:, V_MB:V_MB + 144] = _pk(np.asarray(inp["mod_b"]).reshape(-1))
        vec[:, V_C:V_C + 8] = _pk(inp["c"][b])
        vec[:, V_EDGE] = 1.0 if t0 >= 8 else 0.0
        vec[:, V_EDGE + 1] = 1.0 if t0 + T + 8 <= S else 0.0
        for wi, w in enumerate(POOLW):
            left, right = w // 2, w - 1 - w // 2
            for e in range(16):
                t = t0 + e if e < 8 else t0 + T - 16 + e
                lo, hi = max(t - left, 0), min(t + right, S - 1)
                vec[:, V_ICE + wi * 16 + e] = 1.0 / float(hi - lo + 1)
        for i in range(3):
            jf, jb = pos - 3 + i, pos + 3 - i
            if 0 <= jf < CPS:
                vec[:, V_CM + (0 * 3 + i) * 4 + jf] = 1.0
            if 0 <= jb < CPS:
                vec[:, V_CM + (1 * 3 + i) * 4 + jb] = 1.0
        m = dict(shared)
        m["xT"] = np.ascontiguousarray(xs.T)
        m["xTh"] = np.ascontiguousarray(xh.T)
        m["vecs"] = vec
        maps.append(m)
    return maps


class K:
    def __init__(self, cfg):
        self.cfg = cfg
        T = cfg.T
        nc = self.nc = bass.Bass("TRN2", target_bir_lowering=False)
        es = self.es = ExitStack()
        P = self.P = Prog(nc, es)

        def din(name, shape, dt=F32):
            return nc.dram_tensor(name, list(shape), dt, kind="ExternalInput").ap()

        self.d_xT = din("xT", [D, T])
        self.d_xTh = din("xTh", [D, 16])
        self.d_vecs = din("vecs", [128, NV])
        self.d_cb = din("cb", [128, NCB], BF16)
        self.d_ssmp = din("ssmp", [128, 192])
        self.d_ssmbc = din("ssmbc", [128, 4096])
        self.d_modw = din("mod_w", [2, D, 9 * D])
        self.d_win = din("w_in", [2, 2, D, 2 * DFF])
        self.d_wout = din("w_out", [2, 2, DFF, D])
        self.d_poolw = din("pool_w", [4, 256, 256])
        self.d_gluw = din("glu_w", [D, 2 * D])
        if cfg.mode == "A":
            self.d_xTo = nc.dram_tensor("xTo", [D, T], F32, kind="ExternalOutput").ap()
            self.d_hTo = nc.dram_tensor("hTo", [D, T], BF16, kind="ExternalOutput").ap()
            self.d_Eo = nc.dram_tensor("Eo", [128, 128], F32, kind="ExternalOutput").ap()
        else:
            self.d_out = nc.dram_tensor("outT", [D, T], F32, kind="ExternalOutput").ap()
        if cfg.mode == "B":
            self.d_hTi = din("hTi", [D, T], BF16)
            self.d_Eg = din("Eg", [128, cfg.CPS * 128])
        self.d_BA = nc.dram_tensor("BAd", [32, 128, 1024], BF16, kind="Internal").ap()
        self.d_CA = nc.dram_tensor("CAd", [32, 128, 1024], BF16, kind="Internal").ap()
        self.d_T = nc.dram_tensor("Td", [64, 128, 512], BF16, kind="Internal").ap()
        self.d_bounce = nc.dram_tensor("ebounce", [128, 128], F32).ap()
        self.d_gath = nc.dram_tensor("egath", [cfg.CPS * 128, 128], F32).ap()

        def sb(name, shape, dt=F32):
            return es.enter_context(nc.sbuf_tensor(name, list(shape), dt))

        self.xT = sb("xT_sb", [128, KT, T])
        self.TH = T + 16
        AR_H = KT * self.TH // 2
        self.AR_R = 9216
        self.AR = sb("arena", [128, max(AR_H + self.AR_R, 16896)])
        self.hT = self.AR[:, 0:AR_H].bitcast(BF16).rearrange("p (k t) -> p k t", k=KT)
        self.R0 = AR_H
        self.wstage = sb("wstage", [128, 3, 1024])
        self.wbf = sb("wbf", [128, 6, 4096], BF16)
        self.cb = sb("cb_sb", [128, NCB], BF16)
        self.vecs = sb("vecs_sb", [128, NV])
        self.modT = sb("modT", [128, 2, 72])
        self.der = sb("der", [128, 2, 3, 3, KT])
        self.xTh = sb("xTh_sb", [128, KT, 16])
        self.hTh = sb("hTh_sb", [128, KT, 16], BF16)
        self.condb = sb("condb", [128, KT], BF16)
        self.modrow = sb("modrow", [1, 1, 256])
        self.one11 = sb("one11", [1, 1])
        self.epsc = sb("epsc", [128, 1])
        self.pv = sb("poolvec", [128, 2, KT])
        self.scan = sb("scanc", [128, 4, 128])
        self.ps = [es.enter_context(nc.psum_tensor(f"ps{i}", [128, 512], F32)) for i in range(8)]
        self.slot_free = [None] * 3
        self.si = 0
        self.buf_free = [None] * 6
        c = self.cb
        self.cE = c[:, 0:1920].rearrange("p (g u h) -> p g u h", g=8, u=15)
        self.cF = c[:, 1920:3840].rearrange("p (g u h) -> p g u h", g=8, u=15)
        self.cI = c[:, 3840:3968]
        self.cMLE = c[:, 3968:4096]
        self.cMGE = c[:, 4096:4224]
        self.cONE = c[:, 4224:4352]
        self.sig_x = None
        self.sig_c = None

    def R(self, off, n):
        assert off + n <= self.AR_R, (off, n)
        return self.AR[:, self.R0 + off:self.R0 + off + n]

    def Rb(self, off, n):
        assert off + n // 2 <= self.AR_R
        return self.AR[:, self.R0 + off:self.R0 + off + n // 2].bitcast(BF16)

    def barrier(self, engs=("pe", "act", "dve", "pool", "sp")):
        P = self.P
        sigs = []
        for e in ("pe", "act", "dve", "pool"):
            nm = "m_" + e
            if nm in P.cnt and P.cnt[nm] > 0:
                sigs.append((nm, P.cnt[nm]))
        for e in engs:
            for s in sigs:
                if s[0] != "m_" + e:
                    P.wait(e, s)

    def witem(self, src, dst, npart, a, b, dst_wait=None, cast_eng="pool"):
        P = self.P
        s = self.si % 3
        self.si += 1
        st = self.wstage[:npart, s, 0:a * b].rearrange("p (a b) -> p a b", a=a)
        sd = P.op("sp", lambda e: e.dma_start(out=st, in_=src), waits=[self.slot_free[s]], dma="wd%d" % s)
        w = [sd]
        if dst_wait is not None:
            w.append(dst_wait)
        sc = P.op(cast_eng, lambda e: e.tensor_copy(out=dst, in_=st) if cast_eng != "act"
                  else e.activation(out=dst, in_=st, func=AF.Copy),
                  waits=w, reads=["wstage%d" % s], writes=["wbf_%d" % self.si])
        self.slot_free[s] = sc
        return sc

    def startup(self):
        P, cfg = self.P, self.cfg
        q = "act"
        s1 = P.op(q, lambda e: e.dma_start(out=self.vecs[:], in_=self.d_vecs[:, :]), dma="ld")
        s2 = P.op(q, lambda e: e.dma_start(out=self.cb[:], in_=self.d_cb[:, :]), dma="ld")
        self.sig_c = s2
        if cfg.do_s5:
            xs_ = self.xT[:].rearrange("p k t -> p (k t)")
            raw_ = self.wbf[:, 4:6, :].rearrange("p a n -> p (a n)").bitcast(F32)
            self.sig_sp = P.op(q, lambda e: e.dma_start(out=xs_[:, 3200:3392], in_=self.d_ssmp[:, :]), dma="ldg1")
            self.sig_sbc = P.op(q, lambda e: e.dma_start(out=raw_, in_=self.d_ssmbc[:, :]), dma="ldg2")
        if not cfg.do_s5:
            self.load_x()
        self.sig_one = P.op("dve", lambda e: e.memset(self.one11[:], 1.0), writes=["one11"])
        self.sig_eps = P.op("dve", lambda e: e.memset(self.epsc[:], EPS), writes=["epsc"])
        self.sig_cond = P.op("act", lambda e: e.activation(out=self.condb[:], in_=self.vecs[:, V_C:V_C + 8],
                                                           func=AF.Silu), waits=[s2], writes=["condb"])

    def load_x(self):
        P = self.P
        q = "act"
        xv = self.d_xT.rearrange("(k p) t -> p k t", p=128)
        sx = None
        for k in range(KT):
            sx = P.op(q, lambda e, k=k: e.dma_start(out=self.xT[:, k, :], in_=xv[:, k, :]), dma="ldx")
        sx = P.op(q, lambda e: e.dma_start(out=self.xTh[:], in_=self.d_xTh.rearrange("(k p) t -> p k t", p=128)),
                  dma="ldx")
        self.sig_x = sx

    def mod_block(self, L, col0, ncol, dst, b, cast_eng, pr, pt, defer=False):
        P = self.P
        src = self.d_modw[L].rearrange("(k p) n -> p k n", p=128)
        kper = 1024 // ncol
        sc = None
        for i in range(8 // kper):
            sc = self.witem(src[:, kper * i:kper * (i + 1), col0:col0 + ncol], dst[:, kper * i:kper * (i + 1), :],
                            128, kper, ncol, dst_wait=self.mbuf_free[b] if i == 0 else None, cast_eng=cast_eng)
        sg = None
        for k in range(KT):
            sg = P.op("pe", lambda e, k=k: e.matmul(pr, lhsT=self.condb[:, k:k + 1], rhs=dst[:, k, :],
                                                    start=(k == 0), stop=(k == KT - 1)),
                      waits=[sc, self.sig_cond, self.mod_ps_free] + list(self.o_free) if k == 0 else (),
                      mark=(k == KT - 1))
        self.mbuf_free[b] = sg
        r = 0
        s_ev = P.op("act", lambda e: e.activation(out=self.modrow[0:1, r, 0:ncol], in_=pr, func=AF.Copy),
                    waits=[sg, self.modrow_free[r]], writes=["modrow%d" % r])
        s_t = None
        nt = ncol // 128
        for j in range(nt):
            s_t = P.op("pe", lambda e, j=j: e.matmul(pt[:, j:j + 1], lhsT=self.modrow[0:1, r, j * 128:(j + 1) * 128],
                                                     rhs=self.one11[0:1, 0:1], start=True, stop=True),
                       waits=[s_ev, self.modT_ps_free, self.sig_one, self.pss_free] if j == 0 else (), mark=(j == nt - 1))
        self.modrow_free[r] = s_t
        self.mod_ps_free = s_ev
        if defer:
            return s_t
        c0 = col0 // 128
        s_m = P.op("dve", lambda e: e.tensor_tensor(out=self.modT[:, L, c0:c0 + nt], in0=pt[:, 0:nt],
                                                    in1=self.vecs[:, V_MB + L * 72 + c0:V_MB + L * 72 + c0 + nt],
                                                    op=ALU.add),
                   waits=[s_t], writes=["modT"])
        self.modT_ps_free = s_m
        return s_m

    def mod_init(self, nb):
        self.mod_ri = 0
        self.mbuf_free = [None] * nb
        self.modrow_free = [None, None]
        self.mod_ps_free = None
        self.modT_ps_free = None

    def mod_derive(self, L, sig):
        P = self.P
        out = None
        for s in range(3):
            mt = self.modT[:, L, s * 24:(s + 1) * 24].rearrange("p (m k) -> p m k", m=3)
            ng = self.vecs[:, V_NG + (L * 3 + s) * 8:V_NG + (L * 3 + s) * 8 + 8]
            P.op("dve", lambda e, mt=mt, ng=ng, s=s: e.scalar_tensor_tensor(
                out=self.der[:, L, s, 0, :], in0=mt[:, 1, :], scalar=1.0, in1=ng, op0=ALU.add, op1=ALU.mult),
                waits=[sig], writes=["der"])
            P.op("dve", lambda e, mt=mt, s=s: e.tensor_copy(out=self.der[:, L, s, 1, :], in_=mt[:, 0, :]),
                 writes=["der"])
            out = P.op("dve", lambda e, mt=mt, s=s: e.tensor_scalar(
                out=self.der[:, L, s, 2, :], in0=mt[:, 2, :], scalar1=(1.0 if s == 1 else 0.5), scalar2=None,
                op0=ALU.mult), writes=["der"])
        return out

    def norm_block(self, xsrc, n, dst, L, s, waits=(), final=False, outbuf=None):
        P = self.P
        i = self.nb_i
        self.nb_i += 1
        xsq = self.Rb(3072, KT * 512).rearrange("p (k t) -> p k t", k=KT)
        rstd = self.R(7168 + (i % 2) * 512, 512)
        pss = self.ps[6]
        s_sq = P.op("act", lambda e: e.activation(out=xsq[:, :, 0:n], in_=xsrc, func=AF.Square),
                    waits=list(waits) + [self.xsq_free[0]], writes=["xsq"])
        s_mm = None
        for k in range(KT):
            s_mm = P.op("pe", lambda e, k=k: e.matmul(pss[:, 0:n], lhsT=self.cONE, rhs=xsq[:, k, 0:n],
                                                      start=(k == 0), stop=(k == KT - 1)),
                        waits=[s_sq, self.pss_free, self.sig_c, self.modT_ps_free] + list(self.o_free)
                        if k == 0 else (), mark=(k == KT - 1))
        self.xsq_free[0] = s_mm
        s_q = P.op("act", lambda e: e.activation(out=rstd[:, 0:n], in_=pss[:, 0:n], func=AF.Sqrt, bias=self.epsc[:, 0:1]),
                   waits=[s_mm, self.rstd_free[i % 2], self.sig_eps], writes=["rstd%d" % (i % 2)])
        self.pss_free = s_q
        s_r = P.op("dve", lambda e: e.reciprocal(out=rstd[:, 0:n], in_=rstd[:, 0:n]),
                   waits=[s_q], writes=["rstd%d" % (i % 2)])
        last = None
        for k in range(KT):
            if final:
                gk = self.vecs[:, V_FG + k:V_FG + k + 1]
                last = P.op("dve", lambda e, k=k, gk=gk: e.scalar_tensor_tensor(
                    out=outbuf[:, k, 0:n], in0=xsrc[:, k, :], scalar=gk, in1=rstd[:, 0:n], op0=ALU.mult, op1=ALU.mult),
                    reads=["rstd%d" % (i % 2)], writes=["outbuf"])
            else:
                G = self.der[:, L, s, 0, k:k + 1]
                Sh = self.der[:, L, s, 1, k:k + 1]
                j = self.tmp_i
                self.tmp_i += 1
                tmp = self.R(8192 + (j % 2) * 512, 512)
                s_t = P.op("dve", lambda e, k=k, tmp=tmp: e.tensor_tensor(out=tmp[:, 0:n], in0=xsrc[:, k, :],
                                                                          in1=rstd[:, 0:n], op=ALU.mult),
                           waits=[self.tmp_free[j % 2]], reads=["rstd%d" % (i % 2)], writes=["tmpn%d" % (j % 2)])
                last = P.op("act", lambda e, k=k, tmp=tmp, G=G, Sh=Sh: e.activation(
                    out=dst[:, k, :], in_=tmp[:, 0:n], func=AF.Identity, bias=Sh, scale=G),
                    waits=[s_t], writes=["hT"])
                self.tmp_free[j % 2] = last
        self.rstd_free[i % 2] = last
        return last

    def norm_init(self):
        self.nb_i = 0
        self.tmp_i = 0
        self.xsq_free = [None, None]
        self.rstd_free = [None, None]
        self.tmp_free = [None, None]
        self.pss_free = None

    def norm_all(self, L, s, waits=(), halo=False):
        cfg = self.cfg
        sigs = []
        for tb in range(cfg.NTB):
            c0 = tb * 512
            sigs.append(self.norm_block(self.xT[:, :, c0:c0 + 512], 512, self.hT[:, :, 8 + c0:8 + c0 + 512], L, s,
                                        waits=waits))
        if halo:
            sigs.append(self.norm_block(self.xTh[:, :, :], 16, self.hTh[:, :, :], L, s, waits=waits))
        return sigs

    def ffn(self, L, j, s, h_sigs, halo=False, extra=None):
        P, cfg = self.P, self.cfg
        chunks = [[0, 1, 2, 3], [4, 5, 6, 7], [8, 9, 10, 11], [12, 13, 14, 15], [16, 17, 18, 19], [20, 21]]
        win = self.d_win[L, j].rearrange("(k p) n -> p k n", p=128)
        wout = self.d_wout[L, j]
        nblk = cfg.NTB + (1 if halo else 0)
        units = []
        nob = 2 if extra is not None else 4

        def load_chunk(q):
            F = chunks[q]
            f0 = F[0] * 128
            W = sum(128 if f < 21 else 64 for f in F)
            bidx = [self.wb_i % 6, (self.wb_i + 1) % 6, (self.wb_i + 2) % 6]
            self.wb_i += 3
            g = self.wbf[:, bidx[0], :].rearrange("p (k n) -> p k n", k=8)
            u = self.wbf[:, bidx[1], :].rearrange("p (k n) -> p k n", k=8)
            o = self.wbf[:, bidx[2], :].rearrange("p (f n) -> p f n", f=4)
            items = []
            for (dst, cbase, bi) in ((g, f0, bidx[0]), (u, DFF + f0, bidx[1])):
                for i in range(4):
                    items.append((win[:, 2 * i:2 * i + 2, cbase:cbase + W], dst[:, 2 * i:2 * i + 2, 0:W], 128, 2, W,
                                  bi if i == 0 else None))
            for fi, f in enumerate(F):
                rows = 128 if f < 21 else 64
                items.append((wout[f * 128:f * 128 + rows, :].rearrange("p (a n) -> p a n", a=1),
                              o[:rows, fi:fi + 1, :], rows, 1, 1024, bidx[2] if fi == 0 else None))
            return dict(F=F, g=g, u=u, o=o, bidx=bidx, ready=[None, None], items=items, n0=len(items))

        def pump(ck, n):
            while n > 0 and ck["items"]:
                src, dst, npart, a, b, bi = ck["items"].pop(0)
                k = self.cast_i
                self.cast_i += 1
                eng = "pool" if k % 3 == 2 else "dve"
                sc = self.witem(src, dst, npart, a, b, dst_wait=self.buf_free[bi] if bi is not None else None,
                                cast_eng=eng)
                ck["ready"][0 if eng == "dve" else 1] = sc
                n -= 1

        def blk_cols(tb):
            if tb < cfg.NTB:
                return (self.hT[:, :, 8 + tb * 512:8 + tb * 512 + 512], self.xT[:, :, tb * 512:tb * 512 + 512], 512)
            return (self.hTh[:, :, :], self.xTh[:, :, :], 16)

        def emit_gu(ck, tb, ui):
            hv, xv, n = blk_cols(tb)
            ab = ui % 2
            actT = self.Rb(ab * 1024, 4 * 512).rearrange("p (f t) -> p f t", f=4)
            sig = None
            for fi, f in enumerate(ck["F"]):
                rows = 128 if f < 21 else 64
                pi = self.gu_i % 2
                self.gu_i += 1
                pg, pu = self.ps[pi], self.ps[2 + pi]
                w0 = list(ck["ready"]) + [h_sigs[tb], self.gu_free[pi]]
                if fi == 0:
                    w0.append(self.act_free[ab])
                for (pt, wt) in ((pg, ck["g"]), (pu, ck["u"])):
                    for k in range(KT):
                        sig = P.op("pe", lambda e, pt=pt, wt=wt, k=k, fi=fi, rows=rows: e.matmul(
                            pt[:rows, 0:n], lhsT=wt[:, k, fi * 128:fi * 128 + rows], rhs=hv[:, k, :],
                            start=(k == 0), stop=(k == KT - 1)),
                            waits=w0 if (pt is pg and k == 0) else (), mark=(pt is pu and k == KT - 1))
                sgb = self.R(2048 + pi * 512, 512)
                s_a = P.op("act", lambda e, pg=pg, sgb=sgb, rows=rows: e.activation(
                    out=sgb[:rows, 0:n], in_=pg[:rows, 0:n], func=AF.Silu),
                    waits=[sig, self.sg_free[pi]], writes=["sg%d" % pi])
                s_d = P.op("dve", lambda e, pu=pu, sgb=sgb, rows=rows, fi=fi, actT=actT: e.tensor_tensor(
                    out=actT[:rows, fi, 0:n], in0=sgb[:rows, 0:n], in1=pu[:rows, 0:n], op=ALU.mult),
                    waits=[s_a], writes=["actT%d" % ab])
                self.gu_free[pi] = s_d
                self.sg_free[pi] = s_d
            return s_d

        def emit_out(ck, tb, ui, act_sig, lastchunk):
            hv, xv, n = blk_cols(tb)
            ab = ui % 2
            actT = self.Rb(ab * 1024, 4 * 512).rearrange("p (f t) -> p f t", f=4)
            F = ck["F"]
            sig = None
            s_x = None
            for dj in range(KT):
                pi = self.o_i % nob
                self.o_i += 1
                po = self.ps[4 + pi]
                for fi, f in enumerate(F):
                    rows = 128 if f < 21 else 64
                    sig = P.op("pe", lambda e, po=po, fi=fi, rows=rows, dj=dj: e.matmul(
                        po[:, 0:n], lhsT=ck["o"][:rows, fi, dj * 128:(dj + 1) * 128], rhs=actT[:rows, fi, 0:n],
                        start=(fi == 0), stop=(fi == len(F) - 1)),
                        waits=[act_sig, self.o_free[pi], self.pss_free, self.modT_ps_free, self.mod_ps_free]
                        if fi == 0 else (), mark=(fi == len(F) - 1))
                Gt = self.der[:, L, s, 2, dj:dj + 1]
                s_x = P.op("dve", lambda e, po=po, dj=dj, Gt=Gt: e.scalar_tensor_tensor(
                    out=xv[:, dj, :], in0=po[:, 0:n], scalar=Gt, in1=xv[:, dj, :], op0=ALU.mult, op1=ALU.add),
                    waits=[sig], writes=["xT%d" % dj])
                self.o_free[pi] = s_x
            self.act_free[ab] = sig
            return sig, s_x

        seq = [(q, tb) for q in range(len(chunks)) for tb in range(nblk)]
        cks = {}
        cks[0] = load_chunk(0)
        pump(cks[0], 1000)
        done = [None] * nblk
        prev = None
        for ui, (q, tb) in enumerate(seq):
            a_sig = emit_gu(cks[q], tb, self.unit_i + ui)
            if prev is not None:
                pq, ptb, pui, pa = prev
                pe_sig, x_sig = emit_out(cks[pq], ptb, pui, pa, pq == len(chunks) - 1)
                if ptb == nblk - 1:
                    for b in cks[pq]["bidx"]:
                        self.buf_free[b] = pe_sig
                if pq == len(chunks) - 1:
                    done[ptb] = x_sig
            prev = (q, tb, self.unit_i + ui, a_sig)
            if q + 1 < len(chunks):
                if tb == 0:
                    cks[q + 1] = load_chunk(q + 1)
                    if extra is not None:
                        extra(q)
                per = -(-cks[q + 1]["n0"] // nblk)
                pump(cks[q + 1], 1000 if tb == nblk - 1 else per)
        pq, ptb, pui, pa = prev
        pe_sig, x_sig = emit_out(cks[pq], ptb, pui, pa, True)
        for b in cks[pq]["bidx"]:
            self.buf_free[b] = pe_sig
        done[ptb] = x_sig
        self.unit_i += len(seq)
        return done

    def ffn_init(self):
        self.wb_i = 0
        self.cast_i = 0
        self.gu_i = 0
        self.o_i = 0
        self.unit_i = 0
        self.gu_free = [None, None]
        self.sg_free = [None, None]
        self.o_free = [None, None, None, None]
        self.act_free = [None, None]

    def pool_mixer(self, L, h_sigs):
        P, cfg = self.P, self.cfg
        T, TH = cfg.T, self.TH
        hw = h_sigs
        sl = P.op("dve", lambda e: e.tensor_scalar(out=self.hT[:, :, 0:8], in0=self.hTh[:, :, 0:8],
                                                   scalar1=self.vecs[:, V_EDGE:V_EDGE + 1], scalar2=None, op0=ALU.mult),
                  waits=hw, writes=["hT"])
        sr = P.op("dve", lambda e: e.tensor_scalar(out=self.hT[:, :, 8 + T:16 + T], in0=self.hTh[:, :, 8:16],
                                                   scalar1=self.vecs[:, V_EDGE + 1:V_EDGE + 2], scalar2=None,
                                                   op0=ALU.mult), writes=["hT"])
        b = self.wb_i % 6
        self.wb_i += 1
        pw = self.wbf[:, b, 0:2048].rearrange("p (k g d) -> p k g d", k=2, g=4)
        src = self.d_poolw.rearrange("g (k p) d -> p k g d", p=128)
        scw = None
        for kk in range(2):
            scw = self.witem(src[:, kk, :, :], pw[:, kk, :, :], 128, 4, 256,
                             dst_wait=self.buf_free[b] if kk == 0 else None)
        Gt = self.der[:, L, 1, 2, :]
        P.op("dve", lambda e: e.tensor_tensor(out=self.pv[:, 0, :], in0=Gt, in1=self.vecs[:, V_PS:V_PS + 8],
                                              op=ALU.mult), writes=["pv"])
        s_pv = P.op("dve", lambda e: e.tensor_tensor(out=self.pv[:, 1, :], in0=self.pv[:, 0, :],
                                                     in1=self.vecs[:, V_PB:V_PB + 8], op=ALU.mult),
                    reads=["pv"], writes=["pv"])
        sa = self.R(0, TH)
        sbb = self.R(TH, TH)
        done = [None] * cfg.NTB
        pT_free = [None, None]
        last_pe = None
        for gi, w in enumerate(POOLW):
            pT = self.Rb(2 * TH + 32 + (gi % 2) * T, 2 * T).rearrange("p (k t) -> p k t", k=2)
            s_p = None
            for kk in range(2):
                k = 2 * gi + kk
                hk = self.hT[:, k, :]
                sg_ = P.op("dve", lambda e, hk=hk: e.tensor_tensor(out=sa[:, 1:TH], in0=hk[:, 0:TH - 1], in1=hk[:, 1:TH],
                                                                   op=ALU.add), waits=[sl, sr], reads=["hT"], writes=["sa"])
                cur, oth = sa, sbb
                curn, othn = "sa", "sb"
                step = 1
                lo, hi = 1, TH
                ww = 2
                while ww < w:
                    nlo, nhi = lo + step, hi - step
                    P.op("dve", lambda e, cur=cur, oth=oth, nlo=nlo, nhi=nhi, step=step: e.tensor_tensor(
                        out=oth[:, nlo:nhi], in0=cur[:, nlo - step:nhi - step], in1=cur[:, nlo + step:nhi + step],
                        op=ALU.add), reads=[curn], writes=[othn])
                    cur, oth = oth, cur
                    curn, othn = othn, curn
                    lo, hi = nlo, nhi
                    step *= 2
                    ww *= 2
                P.op("dve", lambda e, cur=cur, hk=hk, kk=kk, pT=pT, w=w: e.scalar_tensor_tensor(
                    out=pT[:, kk, :], in0=cur[:, 8:8 + T], scalar=1.0 / w, in1=hk[:, 8:8 + T],
                    op0=ALU.mult, op1=ALU.subtract), waits=[pT_free[gi % 2]], reads=[curn], writes=["pT"])
                ice = self.vecs[:, V_ICE + gi * 16:V_ICE + gi * 16 + 16]
                for (c0, e0) in ((8, 0), (T, 8)):
                    tmpe = self.R(2 * TH, 8)
                    P.op("dve", lambda e, cur=cur, c0=c0, e0=e0, ice=ice, tmpe=tmpe: e.tensor_tensor(
                        out=tmpe[:, 0:8], in0=cur[:, c0:c0 + 8], in1=ice[:, e0:e0 + 8], op=ALU.mult),
                        reads=[curn, "tmpe"], writes=["tmpe"])
                    s_p = P.op("dve", lambda e, hk=hk, c0=c0, kk=kk, pT=pT, tmpe=tmpe: e.tensor_tensor(
                        out=pT[:, kk, c0 - 8:c0], in0=tmpe[:, 0:8], in1=hk[:, c0:c0 + 8], op=ALU.subtract),
                        reads=["tmpe", "pT"], writes=["pT"])
            for dl in range(2):
                dj = 2 * gi + dl
                for tb in range(cfg.NTB):
                    pi = self.o_i % 2
                    self.o_i += 1
                    po = self.ps[4 + pi]
                    sig = None
                    for kk in range(2):
                        sig = P.op("pe", lambda e, po=po, kk=kk, gi=gi, dl=dl, tb=tb, pT=pT: e.matmul(
                            po[:, 0:512], lhsT=pw[:, kk, gi, dl * 128:(dl + 1) * 128],
                            rhs=pT[:, kk, tb * 512:(tb + 1) * 512], start=(kk == 0), stop=(kk == 1)),
                            waits=[s_p, scw, self.o_free[pi]] if kk == 0 else (), mark=(kk == 1))
                    last_pe = sig
                    xv = self.xT[:, dj, tb * 512:(tb + 1) * 512]
                    P.op("dve", lambda e, po=po, xv=xv, dj=dj: e.scalar_tensor_tensor(
                        out=xv, in0=po[:, 0:512], scalar=self.pv[:, 0, dj:dj + 1], in1=xv, op0=ALU.mult, op1=ALU.add),
                        waits=[sig, s_pv], writes=["xTa"])
                    s_x = P.op("dve", lambda e, xv=xv, dj=dj: e.tensor_scalar(
                        out=xv, in0=xv, scalar1=self.pv[:, 1, dj:dj + 1], scalar2=None, op0=ALU.add),
                        reads=["xTa"], writes=["xTa"])
                    self.o_free[pi] = s_x
                    done[tb] = s_x
            pT_free[gi % 2] = last_pe
        self.buf_free[b] = last_pe
        return done

    def final(self, waits_per_tb):
        P, cfg = self.P, self.cfg
        ov = self.d_out.rearrange("(k p) t -> p k t", p=128)
        st = None
        ob_free = [None, None]
        for tb in range(cfg.NTB):
            ob = self.wbf[:, 2 * (tb % 2):2 * (tb % 2) + 2, :].rearrange("p a n -> p (a n)").bitcast(F32) \
                .rearrange("p (k t) -> p k t", k=KT)
            sig = self.norm_block(self.xT[:, :, tb * 512:(tb + 1) * 512], 512, None, 0, 0,
                                  waits=[waits_per_tb[tb], ob_free[tb % 2]], final=True, outbuf=ob)
            st = P.op("sp", lambda e, ob=ob, tb=tb: e.dma_start(out=ov[:, :, tb * 512:(tb + 1) * 512], in_=ob),
                      waits=[sig], dma="st")
            ob_free[tb % 2] = st
        P.wait("sp", st)
        for e in ("pe", "act", "dve", "pool"):
            P.wait(e, st)

    def build(self):
        P, cfg = self.P, self.cfg
        self.norm_init()
        self.ffn_init()
        self.startup()
        if cfg.mode == "B":
            self.mod_init(3)
            sm = None
            for cbi in range(36):
                b = cbi % 3
                dst = self.wbf[:, b, 0:2048].rearrange("p (k n) -> p k n", k=8)
                sm = self.mod_block(1, cbi * 256, 256, dst, b, "act", self.ps[7][0:1, 0:256], self.ps[6][:, 0:2])
                self.buf_free[b] = self.mbuf_free[b]
            d1 = self.mod_derive(1, sm)
            for _ in self.s5_gen():
                pass
            self.barrier()
            self.load_x()
            s_h = P.op("act", lambda e: e.dma_start(out=self.hT[:, :, 8:8 + cfg.T],
                                                    in_=self.d_hTi.rearrange("(k p) t -> p k t", p=128)), dma="ldh")
            for e in ("pe", "act", "dve", "pool", "sp"):
                P.wait(e, s_h)
                P.wait(e, self.sig_x)
            self.barrier()
            self.wb_i = 0
            done = self.s5_mixer(1, [s_h, d1])
            hs = self.norm_all(1, 2, waits=done)
            done = self.ffn(1, 1, 2, hs)
            self.barrier()
            self.final(done)
            P.emit()
            return self.nc
        gen = None
        if cfg.do_s5:
            gen = self.s5_gen()
            next(gen)
        self.mod_init(3)
        sm = None
        for cbi in range(36):
            b = cbi % 3
            dst = self.wbf[:, b, 0:2048].rearrange("p (k n) -> p k n", k=8)
            sm = self.mod_block(0, cbi * 256, 256, dst, b, "act", self.ps[7][0:1, 0:256],
                                self.ps[6][:, 2 * cbi:2 * cbi + 2], defer=True)
            self.buf_free[b] = self.mbuf_free[b]
        sm = P.op("dve", lambda e: e.tensor_tensor(out=self.modT[:, 0, :], in0=self.ps[6][:, 0:72],
                                                   in1=self.vecs[:, V_MB:V_MB + 72], op=ALU.add),
                  waits=[sm, self.sig_c], writes=["modT"])
        self.modT_ps_free = sm
        d0 = self.mod_derive(0, sm)
        if cfg.do_s5:
            for _ in gen:
                pass
            self.barrier()
            self.load_x()
        self.barrier()
        self.wb_i = 0
        hs = self.norm_all(0, 0, waits=[self.sig_x, d0], halo=True)
        done = self.ffn(0, 0, 0, hs, halo=True)
        hs = self.norm_all(0, 1, waits=done, halo=True)
        self.barrier()
        done = self.pool_mixer(0, hs)
        self.barrier()
        hs = self.norm_all(0, 2, waits=done)
        self.mod_init(2)
        st = {"cb": 0, "sm": None}

        def mod1():
            cbi = st["cb"]
            b = cbi % 2
            dst = self.Rb(5120 + b * 1024, 8 * 256).rearrange("p (k n) -> p k n", k=8)
            st["sm"] = self.mod_block(1, cbi * 256, 256, dst, b, "pool", self.ps[7][0:1, 0:256], self.ps[6][:, 0:2])
            st["cb"] += 1

        def extra(q):
            for _ in range(6 if q < 4 else 12):
                if st["cb"] < 36:
                    mod1()

        done = self.ffn(0, 1, 2, hs, extra=extra)
        while st["cb"] < 36:
            mod1()
        d1 = self.mod_derive(1, st["sm"])
        hs = self.norm_all(1, 0, waits=list(done) + [d1])
        done = self.ffn(1, 0, 0, hs)
        if cfg.do_s5 and not (cfg.dbg or "").startswith("g"):
            hs = self.norm_all(1, 1, waits=done)
            self.barrier()
            done = self.s5_mixer(1, hs)
            if cfg.mode == "A":
                P.emit()
                return self.nc
            self.barrier()
        hs = self.norm_all(1, 2, waits=done)
        done = self.ffn(1, 1, 2, hs)
        self.barrier()
        self.final(done)
        P.emit()
        return self.nc


def build_nc(cfg):
    k = K(cfg)
    return k.build()


def kernel(**inputs):
    cfg = Cfg(mode="fused")
    maps = prep_inputs(cfg, inputs)
    nc = build_nc(cfg)
    res = run_bass_kernel_spmd(nc, maps, core_ids=list(range(cfg.NCORES)))
    out = np.zeros((cfg.NCORES // cfg.CPS, cfg.S, D), np.float32)
    for k in range(cfg.NCORES):
        b, pos = k // cfg.CPS, k % cfg.CPS
        out[b, pos * cfg.T:(pos + 1) * cfg.T, :] = np.asarray(res.results[k]["outT"], np.float32).T
    return out


def kernel_two_launch(**inputs):
    cfgA = Cfg(mode="A")
    cfgB = Cfg(mode="B")
    maps = prep_inputs(cfgA, inputs)
    ncA = build_nc(cfgA)
    resA = run_bass_kernel_spmd(ncA, maps, core_ids=list(range(cfgA.NCORES)))
    mapsB = []
    for k in range(cfgB.NCORES):
        b = k // cfgB.CPS
        m = dict(maps[k])
        m["xT"] = np.ascontiguousarray(resA.results[k]["xTo"])
        m["hTi"] = np.ascontiguousarray(resA.results[k]["hTo"])
        m["Eg"] = np.ascontiguousarray(np.concatenate(
            [np.asarray(resA.results[b * cfgB.CPS + j]["Eo"], np.float32) for j in range(cfgB.CPS)], axis=1))
        mapsB.append(m)
    ncB = build_nc(cfgB)
    resB = run_bass_kernel_spmd(ncB, mapsB, core_ids=list(range(cfgB.NCORES)))
    out = np.zeros((cfgB.NCORES // cfgB.CPS, cfgB.S, D), np.float32)
    for k in range(cfgB.NCORES):
        b, pos = k // cfgB.CPS, k % cfgB.CPS
        out[b, pos * cfgB.T:(pos + 1) * cfgB.T, :] = np.asarray(resB.results[k]["outT"], np.float32).T
    return out


def bc_last(ap, n):
    return AP(tensor=ap.tensor, offset=ap.offset, ap=[list(x) for x in ap.ap] + [[0, n]])


def bc_mid(ap, n):
    a = [list(x) for x in ap.ap]
    return AP(tensor=ap.tensor, offset=ap.offset, ap=[a[0], [0, n]] + a[1:])


def _s5_gen(self):
    P, cfg = self.P, self.cfg
    NCH = cfg.NCH
    xs = self.xT[:].rearrange("p k t -> p (k t)")
    ar_ = self.AR

    def xt(off, n):
        return xs[:, off:off + n]

    PWR = xt(0, 1088).rearrange("p (k c) -> p k c", k=17)
    PWI = xt(1088, 1088).rearrange("p (k c) -> p k c", k=17)
    IPR = xt(2176, 512).rearrange("p (k c) -> p k c", k=8)
    IPI = xt(2688, 512).rearrange("p (k c) -> p k c", k=8)
    m = [xt(3200 + i * 64, 64) for i in range(14)]
    negpi = self.pv[:, 0, 0:1]
    raw = self.wbf[:, 4:6, :].rearrange("p a n -> p (a n)").bitcast(F32)
    bre = raw[:, 0:1024].rearrange("p (c h) -> p c h", h=16)
    bim = raw[:, 1024:2048].rearrange("p (c h) -> p c h", h=16)
    cre = raw[:, 2048:3072].rearrange("p (c h) -> p c h", h=16)
    cim = raw[:, 3072:4096].rearrange("p (c h) -> p c h", h=16)
    bb = self.wbf[:, 3, :].bitcast(F32)
    Bbr = bb[:, 0:1024].rearrange("p (c h) -> p c h", h=16)
    Bbi = bb[:, 1024:2048].rearrange("p (c h) -> p c h", h=16)

    s_p = self.sig_sp
    s_bc = self.sig_sbc
    s_np = P.op("dve", lambda e: e.memset(negpi, -math.pi), writes=["pv"])

    st = {"a": None, "d": s_np}

    def A(out, in_, func, bias=None):
        kw = {}
        if bias is not None:
            kw["bias"] = bias
        st["a"] = P.op("act", lambda e: e.activation(out=out, in_=in_, func=func, **kw),
                       waits=[st["d"], s_p], reads=["g"], writes=["g"])

    def Dtt(out, a, b, op, eng="dve"):
        st["d"] = P.op(eng, lambda e: e.tensor_tensor(out=out, in0=a, in1=b, op=op),
                       waits=[st["a"], s_p, s_bc, st["d"]], reads=["g"], writes=["g"])

    def Dts(out, a, s1, op0, s2=None, op1=None, eng="dve"):
        if op1 is None:
            st["d"] = P.op(eng, lambda e: e.tensor_scalar(out=out, in0=a, scalar1=s1, scalar2=None, op0=op0),
                           waits=[st["a"], s_p, st["d"]], reads=["g"], writes=["g"])
        else:
            st["d"] = P.op(eng, lambda e: e.tensor_scalar(out=out, in0=a, scalar1=s1, scalar2=s2, op0=op0, op1=op1),
                           waits=[st["a"], s_p, st["d"]], reads=["g"], writes=["g"])

    def Drec(out, a):
        st["d"] = P.op("dve", lambda e: e.reciprocal(out=out, in_=a), waits=[st["a"], st["d"]], reads=["g"],
                       writes=["g"])

    MUL, ADD, SUB = ALU.mult, ALU.add, ALU.subtract
    A(m[2], m[2], AF.Exp)
    Dtt(m[3], m[0], m[2], MUL)
    Dtt(m[4], m[1], m[2], MUL)
    A(m[3], m[3], AF.Exp)
    I32 = mybir.dt.int32

    def reduce_pi(dst, src, shift):
        Dts(m[9], src, shift, ADD)
        Dts(m[10], m[9], 1.0 / (2 * math.pi), MUL)
        st["d"] = P.op("dve", lambda e: e.tensor_copy(out=m[11].bitcast(I32), in_=m[10]), waits=[st["d"]],
                       reads=["g"], writes=["g"])
        st["d"] = P.op("dve", lambda e: e.tensor_copy(out=m[10], in_=m[11].bitcast(I32)), reads=["g"], writes=["g"])
        st["d"] = P.op("dve", lambda e: e.scalar_tensor_tensor(out=dst, in0=m[10], scalar=-2 * math.pi, in1=m[9],
                                                              op0=MUL, op1=ADD), reads=["g"], writes=["g"])
        Dts(m[10], dst, math.pi, ALU.is_gt, 2 * math.pi, MUL)
        Dtt(dst, dst, m[10], SUB)
        Dts(m[10], dst, -math.pi, ALU.is_lt, 2 * math.pi, MUL)
        Dtt(dst, dst, m[10], ADD)

    reduce_pi(m[5], m[4], 0.0)
    reduce_pi(m[6], m[4], 0.5 * math.pi)
    A(m[5], m[5], AF.Sin)
    A(m[6], m[6], AF.Sin)
    Dtt(m[7], m[3], m[6], MUL)
    Dtt(m[8], m[3], m[5], MUL)
    Dtt(m[9], m[0], m[0], MUL)
    Dtt(m[10], m[1], m[1], MUL)
    Dtt(m[9], m[9], m[10], ADD)
    Drec(m[11], m[9])
    Dts(m[9], m[7], -1.0, ADD)
    Dtt(m[10], m[9], m[0], MUL)
    Dtt(m[12], m[8], m[1], MUL)
    Dtt(m[12], m[10], m[12], ADD)
    Dtt(m[12], m[12], m[11], MUL)
    Dtt(m[10], m[8], m[0], MUL)
    Dtt(m[13], m[9], m[1], MUL)
    Dtt(m[13], m[10], m[13], SUB)
    Dtt(m[13], m[13], m[11], MUL)
    tg = [ar_[:, 12288 + i * 1024:12288 + (i + 1) * 1024].rearrange("p (c h) -> p c h", h=16) for i in range(4)]
    frb, fib = bc_last(m[12], 16), bc_last(m[13], 16)
    Dtt(tg[0], bre, frb, MUL)
    Dtt(tg[1], bim, fib, MUL)
    Dtt(Bbr, tg[0], tg[1], SUB)
    Dtt(tg[2], bim, frb, MUL)
    Dtt(tg[3], bre, fib, MUL)
    Dtt(Bbi, tg[2], tg[3], ADD)
    P.op("dve", lambda e: e.memset(PWR[:, 0, :], 1.0), reads=["g"], writes=["g"])
    P.op("dve", lambda e: e.memset(PWI[:, 0, :], 0.0), reads=["g"], writes=["g"])
    P.op("dve", lambda e: e.memset(IPR[:, 0, :], 1.0), reads=["g"], writes=["g"])
    P.op("dve", lambda e: e.memset(IPI[:, 0, :], 0.0), reads=["g"], writes=["g"])
    P.op("dve", lambda e: e.tensor_copy(out=PWR[:, 1, :], in_=m[7]), reads=["g"], writes=["g"])
    P.op("dve", lambda e: e.tensor_copy(out=PWI[:, 1, :], in_=m[8]), reads=["g"], writes=["g"])

    def cmul(orr, oii, xr, xi, yr, yi, t1, t2):
        Dtt(t1, xr, yr, MUL)
        Dtt(t2, xi, yi, MUL)
        Dtt(orr, t1, t2, SUB)
        Dtt(t1, xr, yi, MUL)
        Dtt(t2, xi, yr, MUL)
        Dtt(oii, t1, t2, ADD)

    for k in range(2, 17):
        cmul(PWR[:, k, :], PWI[:, k, :], PWR[:, k - 1, :], PWI[:, k - 1, :], m[7], m[8], m[9], m[10])
    Dtt(m[9], m[3], m[3], MUL)
    Drec(m[9], m[9])
    Dtt(IPR[:, 1, :], m[7], m[9], MUL)
    Dtt(m[5], m[8], m[9], MUL)
    Dts(IPI[:, 1, :], m[5], -1.0, MUL)
    for k in range(2, 8):
        cmul(IPR[:, k, :], IPI[:, k, :], IPR[:, k - 1, :], IPI[:, k - 1, :], IPR[:, 1, :], IPI[:, 1, :], m[9], m[10])
    AA, BB, ZA, ZB = (self.scan[:, i, :] for i in range(4))
    P.op("dve", lambda e: e.tensor_copy(out=AA[:, 0:64], in_=PWR[:, 16, :]), reads=["g"], writes=["scan"])
    P.op("dve", lambda e: e.tensor_copy(out=AA[:, 64:128], in_=PWR[:, 16, :]), reads=["g"], writes=["scan"])
    P.op("dve", lambda e: e.tensor_scalar(out=BB[:, 0:64], in0=PWI[:, 16, :], scalar1=-1.0, scalar2=None,
                                          op0=MUL), reads=["g"], writes=["scan"])
    P.op("dve", lambda e: e.tensor_copy(out=BB[:, 64:128], in_=PWI[:, 16, :]), reads=["g"], writes=["scan"])
    Dtt(m[0], PWR[:, 16, :], PWR[:, 0, :], MUL)
    Dtt(m[1], PWI[:, 16, :], PWR[:, 0, :], MUL)
    nsq = int(round(math.log2(NCH)))
    assert 2 ** nsq == NCH
    for _ in range(nsq):
        Dtt(m[9], m[0], m[1], MUL)
        Dtt(m[10], m[0], m[0], MUL)
        Dtt(m[6], m[1], m[1], MUL)
        Dtt(m[0], m[10], m[6], SUB)
        Dts(m[1], m[9], 2.0, MUL)
    P.op("dve", lambda e: e.tensor_copy(out=ZA[:, 0:64], in_=m[0]), reads=["g"], writes=["scan"])
    P.op("dve", lambda e: e.tensor_copy(out=ZA[:, 64:128], in_=m[0]), reads=["g"], writes=["scan"])
    P.op("dve", lambda e: e.tensor_scalar(out=ZB[:, 0:64], in0=m[1], scalar1=-1.0, scalar2=None, op0=MUL),
         reads=["g"], writes=["scan"])
    s_small = P.op("dve", lambda e: e.tensor_copy(out=ZB[:, 64:128], in_=m[1]), reads=["g"], writes=["scan"])

    yield "small"
    if cfg.dbg == "g1":
        for e in ("pe", "act", "dve", "pool", "sp"):
            P.wait(e, s_small)
        return
    bufA = ar_[:, 0:4096].bitcast(BF16)
    bufC = ar_[:, 4096:8192].bitcast(BF16)
    bufD = ar_[:, 8192:12288].bitcast(BF16)
    stg = [ar_[:, 16384 + i * 256:16384 + (i + 1) * 256].bitcast(BF16).rearrange("p (a n) -> p a n", a=4)
           for i in range(2)]

    def bview(buf):
        return buf.rearrange("p (r g s h) -> p r g s h", r=2, g=32, s=8)

    def gen_tab(name, buf, dirn, Xr, Xi, PR, PI, kfun, neg_im, waits):
        bv = bview(buf)
        sigs = []
        for s_lo in range(8):
            eng = "dve"
            tb = 12288 + (0 if eng == "dve" else 2048)
            T = [ar_[:, tb + i * 512:tb + (i + 1) * 512].rearrange("p (c h) -> p c h", h=16) for i in range(4)]
            k = kfun(s_lo)
            pr = bc_last(PR[:, k, dirn * 32:(dirn + 1) * 32], 16)
            pi = bc_last(PI[:, k, dirn * 32:(dirn + 1) * 32], 16)
            xr = Xr[:, dirn * 32:(dirn + 1) * 32, :]
            xi = Xi[:, dirn * 32:(dirn + 1) * 32, :]
            tag = "gt_" + eng
            w = list(waits) + [s_small, st["d"], s_bc]
            P.op(eng, lambda e, T=T, xr=xr, pr=pr: e.tensor_tensor(out=T[0], in0=xr, in1=pr, op=MUL), waits=w,
                 reads=[tag], writes=[tag])
            P.op(eng, lambda e, T=T, xi=xi, pi=pi: e.tensor_tensor(out=T[1], in0=xi, in1=pi, op=MUL),
                 reads=[tag], writes=[tag])
            P.op(eng, lambda e, T=T, bv=bv, s_lo=s_lo: e.tensor_tensor(out=bv[:, 0, :, s_lo, :], in0=T[0], in1=T[1],
                                                                       op=SUB), reads=[tag], writes=[tag, name])
            P.op(eng, lambda e, T=T, xi=xi, pr=pr: e.tensor_tensor(out=T[2], in0=xi, in1=pr, op=MUL),
                 reads=[tag], writes=[tag])
            P.op(eng, lambda e, T=T, xr=xr, pi=pi: e.tensor_tensor(out=T[3], in0=xr, in1=pi, op=MUL),
                 reads=[tag], writes=[tag])
            if neg_im and eng == "dve":
                sg = P.op(eng, lambda e, T=T, bv=bv, s_lo=s_lo: e.scalar_tensor_tensor(
                    out=bv[:, 1, :, s_lo, :], in0=T[2], scalar=-1.0, in1=T[3], op0=MUL, op1=SUB),
                    reads=[tag], writes=[tag, name])
            elif neg_im:
                P.op(eng, lambda e, T=T: e.tensor_scalar(out=T[2], in0=T[2], scalar1=-1.0, scalar2=None, op0=MUL),
                     reads=[tag], writes=[tag])
                sg = P.op(eng, lambda e, T=T, bv=bv, s_lo=s_lo: e.tensor_tensor(
                    out=bv[:, 1, :, s_lo, :], in0=T[2], in1=T[3], op=SUB), reads=[tag], writes=[tag, name])
            else:
                sg = P.op(eng, lambda e, T=T, bv=bv, s_lo=s_lo: e.tensor_tensor(
                    out=bv[:, 1, :, s_lo, :], in0=T[2], in1=T[3], op=ADD), reads=[tag], writes=[tag, name])
            sigs.append(sg)
        return sigs[-1:]

    evi = [0]
    stg_free = [None, None]
    psb_free = [None, None]

    def t_blocks(dirn, bufL, bufR, blk, mask, w_in):
        L, Rr = bview(bufL), bview(bufR)
        last_pe = None
        dT2 = self.d_T.rearrange("(q t) p n -> t q p n", t=2)
        for g0 in range(0, 64, 4):
            i = evi[0] % 2
            evi[0] += 1
            pst = self.ps[i][:, :].rearrange("p (a n) -> p a n", a=4)
            sig = None
            g2 = g0 // 32
            gq0 = (g0 % 32)
            for gg in range(4):
                gq = gq0 + gg
                for ri in range(2):
                    sig = P.op("pe", lambda e, gg=gg, gq=gq, g2=g2, ri=ri, pst=pst, L=L, Rr=Rr: e.matmul(
                        pst[:, gg, :], lhsT=L[64 * g2:64 * g2 + 64, ri, gq, :, :].rearrange("p s h -> p (s h)"),
                        rhs=Rr[64 * g2:64 * g2 + 64, ri, gq, :, :].rearrange("p s h -> p (s h)"),
                        start=(ri == 0), stop=(ri == 1)),
                        waits=list(w_in) + [psb_free[i]] if (gg == 0 and ri == 0) else (),
                        mark=(gg == 3 and ri == 1))
            last_pe = sig
            sb_ = stg[i]
            if mask is not None:
                s_e = P.op("dve", lambda e, pst=pst, sb_=sb_: e.tensor_tensor(out=sb_, in0=pst, in1=bc_mid(mask, 4),
                                                                             op=MUL),
                           waits=[sig, stg_free[i], self.sig_c], writes=["stg%d" % i])
            else:
                s_e = P.op("act", lambda e, pst=pst, sb_=sb_: e.activation(out=sb_, in_=pst, func=AF.Copy),
                           waits=[sig, stg_free[i]], writes=["stg%d" % i])
            psb_free[i] = s_e
            dst = dT2[g2, gq0:gq0 + 4, :, blk * 128:(blk + 1) * 128].rearrange("g p n -> p g n")
            stg_free[i] = P.op("sp", lambda e, dst=dst, sb_=sb_: e.dma_start(out=dst, in_=sb_), waits=[s_e],
                               dma="tbl%d" % i)
        return last_pe

    def ba_out(dirn, s_hi, buf, w_in):
        bv = bview(buf)
        last_pe = None
        for ri in range(2):
            for q0 in range(0, 32, 4):
                i = evi[0] % 2
                evi[0] += 1
                pst = self.ps[i][:, 0:256].bitcast(BF16).rearrange("p (a n) -> p a n", a=4)
                sig = None
                for qq in range(4):
                    sig = P.op("pe", lambda e, qq=qq, q0=q0, ri=ri, pst=pst, bv=bv: e.transpose(
                        pst[:, qq, :], bv[:, ri, q0 + qq, :, :].rearrange("p s h -> p (s h)"), self.cI),
                        waits=list(w_in) + [psb_free[i], self.sig_c] if qq == 0 else (), mark=(qq == 3))
                last_pe = sig
                sb_ = stg[i]
                s_e = P.op("act", lambda e, pst=pst, sb_=sb_: e.activation(out=sb_, in_=pst, func=AF.Copy),
                           waits=[sig, stg_free[i]], writes=["stg%d" % i])
                psb_free[i] = s_e
                col = ((dirn * 2 + s_hi) * 2 + ri) * 128
                dst = self.d_BA[q0:q0 + 4, :, col:col + 128].rearrange("g p n -> p g n")
                stg_free[i] = P.op("sp", lambda e, dst=dst, sb_=sb_: e.dma_start(out=dst, in_=sb_), waits=[s_e],
                                   dma="tbl%d" % i)
        return last_pe

    def ca_out(dirn, mt, buf, w_in):
        bv = bview(buf)
        sg = None
        for ri in range(2):
            col = ((dirn * 2 + mt) * 2 + ri) * 128
            for q0 in range(0, 32, 4):
                dst = self.d_CA[q0:q0 + 4, :, col:col + 128].rearrange("g p n -> p g n")
                sg = P.op("sp", lambda e, dst=dst, ri=ri, bv=bv, q0=q0: e.dma_start(
                    out=dst, in_=bv[:, ri, q0:q0 + 4, :, :].rearrange("p g s h -> p g (s h)")), waits=list(w_in),
                    dma="tblc")
        return sg

    free = {"A": [], "C": [], "D": []}
    for dirn in range(2):
        hA = 1 if dirn == 0 else 0
        hD = 0 if dirn == 0 else 1
        if dirn == 0:
            kBA = lambda s_hi: (lambda s_lo: 15 - (8 * s_hi + s_lo))
            kCA = lambda mt: (lambda t_lo: 8 * mt + t_lo + 1)
            kC0 = lambda t_lo: 7 - t_lo
            mask = self.cMLE
        else:
            kBA = lambda s_hi: (lambda s_lo: 8 * s_hi + s_lo)
            kCA = lambda mt: (lambda t_lo: 16 - (8 * mt + t_lo))
            kC0 = lambda t_lo: t_lo
            mask = self.cMGE
        sA = gen_tab("bA", bufA, dirn, Bbr, Bbi, PWR, PWI, kBA(hA), False, free["A"])
        sC = gen_tab("bC", bufC, dirn, cre, cim, IPR, IPI, kC0, True, free["C"])
        sD = gen_tab("bD", bufD, dirn, cre, cim, PWR, PWI, kCA(hD), True, free["D"])
        if cfg.dbg == "g2":
            for e in ("pe", "act", "dve", "pool", "sp"):
                for sg_ in sA + sC + sD:
                    P.wait(e, sg_)
            return
        pe1 = t_blocks(dirn, bufA, bufC, 0 + dirn, mask, sA + sC)
        pe2 = t_blocks(dirn, bufA, bufD, 2 + dirn, None, sA + sD)
        if cfg.dbg == "g3":
            for e in ("pe", "act", "dve", "pool", "sp"):
                for sg_ in [pe1, pe2, stg_free[0], stg_free[1]]:
                    P.wait(e, sg_)
            return
        pe3 = ba_out(dirn, hA, bufA, sA)
        if cfg.dbg == "g4":
            for e in ("pe", "act", "dve", "pool", "sp"):
                for sg_ in [pe3, stg_free[0], stg_free[1]]:
                    P.wait(e, sg_)
            return
        d1 = ca_out(dirn, hD, bufD, sD)
        sC2 = gen_tab("bC", bufC, dirn, Bbr, Bbi, PWR, PWI, kBA(1 - hA), False, [pe1])
        pe4 = ba_out(dirn, 1 - hA, bufC, sC2)
        sA2 = gen_tab("bA", bufA, dirn, cre, cim, PWR, PWI, kCA(1 - hD), True, [pe3, pe2])
        d2 = ca_out(dirn, 1 - hD, bufA, sA2)
        free["A"] = [d2]
        free["C"] = [pe4]
        free["D"] = [d1, pe2]
    fin = [stg_free[0], stg_free[1], free["A"][0], free["D"][0]]
    for e in ("pe", "act", "dve", "pool", "sp"):
        for s in fin:
            P.wait(e, s)


K.s5_gen = _s5_gen


def _s5_views(self):
    cfg = self.cfg
    NCH, T = cfg.NCH, cfg.T
    v = {}
    v["SX"] = self.Rb(0, (NCH + 1) * 128).rearrange("p (j c) -> p j c", c=128)
    o = (NCH + 1) * 64
    v["Q"] = [self.R(o + i * 128, 128) for i in range(2)]
    v["t"] = [self.R(o + 256 + i * 128, 128) for i in range(3)]
    v["W"] = self.wbf[:, 0:4, :].rearrange("p a n -> p (a n)")[:, 0:64 * 2 * NCH].rearrange(
        "p (g k c) -> p g k c", g=64, k=2)
    misc = self.wbf[:, 4:6, :].rearrange("p a n -> p (a n)")
    v["ring"] = [misc[:, i * 1024:(i + 1) * 1024].rearrange("p (a n) -> p a n", a=8) for i in range(2)]
    v["Tr"] = [misc[:, 2048 + i * 1024:2048 + (i + 1) * 1024].rearrange("p (g b n) -> p g b n", g=2, b=4)
               for i in range(2)]
    v["YW"] = misc[:, 4096:4096 + 16 * NCH].rearrange("p (g m c) -> p g m c", g=8, m=2)
    mf = misc[:, 6144:8192].bitcast(F32)
    v["Eg"] = mf[:, 0:512].rearrange("p (r c) -> p r c", c=128)
    v["F"] = [mf[:, 512 + i * 128:512 + (i + 1) * 128] for i in range(3)]
    ws = self.wstage[:].rearrange("p a n -> p (a n)")
    v["ybuf"] = ws[:, 0:T]
    v["gt"] = [ws[:, 2048 + i * 512:2048 + (i + 1) * 512] for i in range(2)]
    return v


def swap_ri(ap):
    a = [list(x) for x in ap.ap]
    assert a[1] == [1, 128], a
    return AP(tensor=ap.tensor, offset=ap.offset + 64, ap=[a[0], [-64, 2], [1, 64]])


def as3(ap):
    a = [list(x) for x in ap.ap]
    return AP(tensor=ap.tensor, offset=ap.offset, ap=[a[0], [64, 2], [1, 64]])


def _cols(ap, d):
    a = [list(x) for x in ap.ap]
    return AP(tensor=ap.tensor, offset=ap.offset + d * 32, ap=[a[0], [64, 2], [1, 32]])


def _s5_p1(self, h_sigs):
    P, cfg = self.P, self.cfg
    NCH = cfg.NCH
    v = self.s5v
    SX, W = v["SX"], v["W"]
    s0 = P.op("dve", lambda e: e.memset(SX[:, 0, :], 0.0), writes=["SX0"])
    ring_free = [None, None]
    psw_free = [None, None]
    pss_free = [None, None]
    last = s0
    hw = list(h_sigs)
    for gq in range(32):
        r = gq % 2
        kt, gi0 = gq // 4, 2 * (gq % 4)
        BAr = v["ring"][r]
        s_ba = P.op("sp", lambda e, gq=gq, BAr=BAr: e.dma_start(out=BAr.rearrange("p a n -> p (a n)"),
                                                               in_=self.d_BA[gq]), waits=[ring_free[r]],
                    dma="ba%d" % r)
        psW = self.ps[r][:, 0:4 * NCH].rearrange("p (g k c) -> p g k c", g=2, k=2)
        sig = None
        hrow = self.hT[:, kt, :]
        pstr = list(hrow.ap[0])
        for g2 in range(2):
            gi = gi0 + g2
            for ks in range(2):
                for s_lo in range(8):
                    rhs = AP(tensor=hrow.tensor, offset=hrow.offset + 8 + 8 * ks + s_lo, ap=[pstr, [16, NCH]])
                    first = (g2 == 0 and ks == 0 and s_lo == 0)
                    lastm = (g2 == 1 and ks == 1 and s_lo == 7)
                    sig = P.op("pe", lambda e, g2=g2, ks=ks, s_lo=s_lo, gi=gi, rhs=rhs, psW=psW: e.matmul(
                        psW[:, g2, ks, :], lhsT=self.cE[:, gi, 7 - s_lo:15 - s_lo, :].rearrange("p u h -> p (u h)"),
                        rhs=rhs, start=(s_lo == 0), stop=(s_lo == 7)),
                        waits=hw + [psw_free[r], self.sig_c] if first else (), mark=lastm)
        s_w = P.op("act", lambda e, gq=gq, psW=psW: e.activation(out=W[:, 2 * gq:2 * gq + 2, :, :], in_=psW,
                                                                func=AF.Copy), waits=[sig], writes=["W"])
        psw_free[r] = s_w
        psS = self.ps[2 + r][:, 0:4 * NCH].rearrange("p (a c) -> p a c", a=4)
        sig = None
        for dirn in range(2):
            for ri in range(2):
                for g2 in range(2):
                    g = 2 * gq + g2
                    for ks in range(2):
                        first = (dirn == 0 and ri == 0 and g2 == 0 and ks == 0)
                        lastm = (dirn == 1 and ri == 1 and g2 == 1 and ks == 1)
                        sig = P.op("pe", lambda e, dirn=dirn, ri=ri, g2=g2, g=g, ks=ks, psS=psS, BAr=BAr: e.matmul(
                            psS[64 * g2:64 * g2 + 64, dirn * 2 + ri, :],
                            lhsT=BAr[:, (dirn * 2 + ks) * 2 + ri, 64 * g2:64 * g2 + 64], rhs=W[:, g, ks, :],
                            start=(ks == 0), stop=(ks == 1)),
                            waits=[s_w, s_ba, pss_free[r]] if first else (), mark=lastm)
        ring_free[r] = sig
        for dirn in range(2):
            col = dirn * 32 + gq
            if dirn == 0:
                o = AP(tensor=SX.tensor, offset=SX.offset + 128 + col, ap=[list(SX.ap[0]), [64, 2], [128, NCH]])
            else:
                o = AP(tensor=SX.tensor, offset=SX.offset + NCH * 128 + col, ap=[list(SX.ap[0]), [64, 2], [-128, NCH]])
            last = P.op("dve", lambda e, o=o, dirn=dirn, psS=psS: e.tensor_copy(out=o, in_=psS[:, 2 * dirn:2 * dirn + 2, :]),
                        waits=[sig], writes=["SXs"])
        pss_free[r] = last
    return last, ring_free


def _s5_scan(self, init, w_in):
    P, cfg = self.P, self.cfg
    NCH = cfg.NCH
    v = self.s5v
    SX = v["SX"]
    Q, t = v["Q"], v["t"]
    AA, BB = self.scan[:, 0, :], self.scan[:, 1, :]
    MUL, ADD = ALU.mult, ALU.add
    if init is None:
        s = P.op("dve", lambda e: e.memset(Q[0], 0.0), waits=w_in, writes=["Q0"])
    else:
        s = P.op("dve", lambda e: e.tensor_copy(out=Q[0], in_=init), waits=w_in, writes=["Q0"])
        P.op("dve", lambda e: e.tensor_copy(out=SX[:, 0, :], in_=init), writes=["SX0"])
    sx_w = None
    for j in range(1, NCH + 1):
        qp, qn = Q[(j - 1) % 2], Q[j % 2]
        tp, tn = "Q%d" % ((j - 1) % 2), "Q%d" % (j % 2)
        P.op("dve", lambda e, qp=qp: e.tensor_tensor(out=t[0], in0=qp, in1=AA, op=MUL), waits=w_in if j == 1 else (),
             reads=[tp], writes=["t0"])
        P.op("dve", lambda e, qp=qp: e.tensor_tensor(out=as3(t[1]), in0=swap_ri(qp), in1=as3(BB), op=MUL),
             reads=[tp], writes=["t1"])
        P.op("dve", lambda e: e.tensor_tensor(out=t[2], in0=t[0], in1=t[1], op=ADD), reads=["t0", "t1"], writes=["t2"])
        s = P.op("dve", lambda e, qn=qn, j=j: e.tensor_tensor(out=qn, in0=t[2], in1=SX[:, j, :], op=ADD),
                 waits=[sx_w] if sx_w is not None else (), reads=["t2", "SXs"], writes=[tn])
        sx_w = P.op("act", lambda e, qn=qn, j=j: e.activation(out=SX[:, j, :], in_=qn, func=AF.Copy), waits=[s],
                    writes=["SXa"])
    return Q[NCH % 2], s, sx_w


def _s5_p3(self, L, w_in):
    P, cfg = self.P, self.cfg
    NCH, T = cfg.NCH, cfg.T
    v = self.s5v
    SX, W, YW = v["SX"], v["W"], v["YW"]
    ybuf, gt = v["ybuf"], v["gt"]
    ring_free = [None, None]
    psy_free = [None, None]
    psi_free = [None, None]
    yw_free = None
    ii = 0
    last_g = None
    gelu_free = [None, None]
    for gq in range(32):
        r = gq % 2
        kt, gi0 = gq // 4, 2 * (gq % 4)
        CAr, Tr = v["ring"][r], v["Tr"][r]
        s_ca = P.op("sp", lambda e, gq=gq, CAr=CAr: e.dma_start(out=CAr.rearrange("p a n -> p (a n)"),
                                                               in_=self.d_CA[gq]), waits=[ring_free[r]] + list(w_in),
                    dma="ca%d" % r)
        s_t = P.op("sp", lambda e, gq=gq, Tr=Tr: e.dma_start(
            out=Tr.rearrange("p g b n -> p g (b n)"),
            in_=self.d_T[2 * gq:2 * gq + 2, :, :].rearrange("g p n -> p g n")), dma="tt%d" % r)
        psY = self.ps[r][:, 0:4 * NCH].rearrange("p (g m c) -> p g m c", g=2, m=2)
        sig = None
        for g2 in range(2):
            g = 2 * gq + g2
            for mt in range(2):
                ops = [(Tr[:, g2, 0, :], W[:, g, mt, :]), (Tr[:, g2, 1, :], W[:, g, mt, :])]
                if mt == 1:
                    ops.append((Tr[:, g2, 2, :], W[:, g, 0, :]))
                else:
                    ops.append((Tr[:, g2, 3, :], W[:, g, 1, :]))
                for dirn in range(2):
                    for ri in range(2):
                        col = ri * 64 + dirn * 32 + gq
                        pst = list(SX.ap[0])
                        base = SX.offset + 64 * g2 * pst[0]
                        if dirn == 0:
                            rhs = AP(tensor=SX.tensor, offset=base + col, ap=[[pst[0], 64], [128, NCH]])
                        else:
                            rhs = AP(tensor=SX.tensor, offset=base + (NCH - 1) * 128 + col,
                                     ap=[[pst[0], 64], [-128, NCH]])
                        ops.append((CAr[64 * g2:64 * g2 + 64, (dirn * 2 + mt) * 2 + ri, :], rhs))
                for oi, (lh, rh) in enumerate(ops):
                    first = (g2 == 0 and mt == 0 and oi == 0)
                    lastm = (g2 == 1 and mt == 1 and oi == len(ops) - 1)
                    sig = P.op("pe", lambda e, g2=g2, mt=mt, lh=lh, rh=rh, oi=oi, n=len(ops), psY=psY: e.matmul(
                        psY[:, g2, mt, :], lhsT=lh, rhs=rh, start=(oi == 0), stop=(oi == n - 1)),
                        waits=[s_ca, s_t, psy_free[r]] + list(w_in) if first else (), mark=lastm)
        ring_free[r] = sig
        s_y = P.op("act", lambda e, gi0=gi0, psY=psY: e.activation(out=YW[:, gi0:gi0 + 2, :, :], in_=psY, func=AF.Copy),
                   waits=[sig, yw_free] if gq % 4 == 0 else [sig], writes=["YW"])
        psy_free[r] = s_y
        if gq % 4 != 3:
            continue
        sig = None
        hrow = self.hT[:, kt, :]
        pstr = list(hrow.ap[0])
        s_e = None
        for sgrp in range(4):
            pi = ii % 2
            ii += 1
            psI = self.ps[2 + pi][:, 0:4 * NCH].rearrange("p (s c) -> p s c", s=4)
            for si in range(4):
                s = 4 * sgrp + si
                mt, s_lo = s // 8, s % 8
                for gi in range(8):
                    sig = P.op("pe", lambda e, si=si, mt=mt, s_lo=s_lo, gi=gi, psI=psI: e.matmul(
                        psI[:, si, :], lhsT=self.cF[:, s_lo, 7 - gi:15 - gi, :].rearrange("p u h -> p (u h)"),
                        rhs=YW[:, gi, mt, :], start=(gi == 0), stop=(gi == 7)),
                        waits=[s_y, psi_free[pi]] if (si == 0 and gi == 0) else (), mark=(si == 3 and gi == 7))
            yo = AP(tensor=ybuf.tensor, offset=ybuf.offset + 4 * sgrp, ap=[list(ybuf.ap[0]), [1, 4], [16, NCH]])
            hi = AP(tensor=hrow.tensor, offset=hrow.offset + 8 + 4 * sgrp, ap=[pstr, [1, 4], [16, NCH]])
            s_e = P.op("dve", lambda e, yo=yo, hi=hi, psI=psI, kt=kt: e.scalar_tensor_tensor(
                out=yo, in0=hi, scalar=self.vecs[:, V_SD + kt:V_SD + kt + 1], in1=psI, op0=ALU.mult, op1=ALU.add),
                waits=[sig, last_g], writes=["ybuf"])
            psi_free[pi] = s_e
        yw_free = sig
        for c0 in range(0, T, 512):
            b = (c0 // 512) % 2
            yv = ybuf[:, c0:c0 + 512]
            s1 = P.op("act", lambda e, yv=yv, b=b: e.activation(out=gt[b], in_=yv, func=AF.Square),
                      waits=[s_e, gelu_free[b]], writes=["gt%d" % b])
            P.op("dve", lambda e, b=b: e.tensor_scalar(out=gt[b], in0=gt[b], scalar1=0.044715, scalar2=1.0,
                                                       op0=ALU.mult, op1=ALU.add), waits=[s1], writes=["gt%d" % b])
            s2 = P.op("dve", lambda e, b=b, yv=yv: e.tensor_tensor(out=gt[b], in0=gt[b], in1=yv, op=ALU.mult),
                      reads=["ybuf"], writes=["gt%d" % b])
            s3 = P.op("act", lambda e, b=b: e.activation(out=gt[b], in_=gt[b], func=AF.Sigmoid, scale=1.5957691216),
                      waits=[s2], writes=["gt%d" % b])
            last_g = P.op("dve", lambda e, b=b, yv=yv, kt=kt, c0=c0: e.tensor_tensor(
                out=self.hT[:, kt, 8 + c0:8 + c0 + 512], in0=gt[b], in1=yv, op=ALU.mult),
                waits=[s3], reads=["ybuf"], writes=["hTg"])
            gelu_free[b] = last_g
    return last_g


def _s5_glu(self, L, w_in):
    P, cfg = self.P, self.cfg
    ws = self.wstage[:].rearrange("p a n -> p (a n)")
    tmpa = [ws[:, i * 512:(i + 1) * 512] for i in range(2)]
    tmpb = [ws[:, 1024 + i * 512:1024 + (i + 1) * 512] for i in range(2)]
    gw = self.d_gluw.rearrange("(k p) n -> p k n", p=128)
    done = [None] * cfg.NTB
    pa_free = [None, None]
    pb_free = [None, None]
    ta_free = [None, None]
    n = 0
    last_pe = None
    for ci in range(2):
        bufA = self.wbf[:, 2 * ci, :].rearrange("p (k n) -> p k n", k=8)
        bufB = self.wbf[:, 2 * ci + 1, :].rearrange("p (k n) -> p k n", k=8)
        stg = self.R(0, 4096).rearrange("p (s n) -> p s n", s=4)
        sc = None
        for (dst, cbase) in ((bufA, 512 * ci), (bufB, 1024 + 512 * ci)):
            for i in range(4):
                sl = self.glu_si % 4
                self.glu_si += 1
                st = stg[:, sl, :].rearrange("p (a b) -> p a b", a=2)
                sd = P.op("sp", lambda e, st=st, i=i, cbase=cbase: e.dma_start(
                    out=st, in_=gw[:, 2 * i:2 * i + 2, cbase:cbase + 512]),
                    waits=[self.glu_slot_free[sl]] + list(w_in), dma="gl%d" % sl)
                sc = P.op("pool", lambda e, st=st, dst=dst, i=i: e.tensor_copy(out=dst[:, 2 * i:2 * i + 2, :], in_=st),
                          waits=[sd] + list(w_in), reads=["glst"], writes=["wbf"])
                self.glu_slot_free[sl] = sc
        for tb in range(cfg.NTB):
            gv = self.hT[:, :, 8 + tb * 512:8 + tb * 512 + 512]
            for dl in range(4):
                dj = 4 * ci + dl
                pi = n % 2
                n += 1
                pa, pb = self.ps[pi], self.ps[2 + pi]
                sig = None
                for (pt, wt) in ((pa, bufA), (pb, bufB)):
                    for k in range(KT):
                        sig = P.op("pe", lambda e, pt=pt, wt=wt, k=k, dl=dl, gv=gv: e.matmul(
                            pt[:, 0:512], lhsT=wt[:, k, dl * 128:(dl + 1) * 128], rhs=gv[:, k, :],
                            start=(k == 0), stop=(k == KT - 1)),
                            waits=[sc, pa_free[pi], pb_free[pi]] + list(w_in) if (pt is pa and k == 0) else (),
                            mark=(pt is pb and k == KT - 1))
                last_pe = sig
                s_s = P.op("act", lambda e, pb=pb, pi=pi, dj=dj: e.activation(
                    out=tmpa[pi], in_=pb[:, 0:512], func=AF.Sigmoid, bias=self.vecs[:, V_GB + 8 + dj:V_GB + 9 + dj]),
                    waits=[sig, ta_free[pi]], writes=["ta%d" % pi])
                pb_free[pi] = s_s
                s_m = P.op("dve", lambda e, pa=pa, pi=pi, dj=dj: e.scalar_tensor_tensor(
                    out=tmpb[pi], in0=pa[:, 0:512], scalar=self.vecs[:, V_GB + dj:V_GB + dj + 1], in1=tmpa[pi],
                    op0=ALU.add, op1=ALU.mult), waits=[s_s], writes=["tb%d" % pi])
                pa_free[pi] = s_m
                ta_free[pi] = s_m
                xv = self.xT[:, dj, tb * 512:(tb + 1) * 512]
                sx = P.op("dve", lambda e, pi=pi, xv=xv, dj=dj: e.scalar_tensor_tensor(
                    out=xv, in0=tmpb[pi], scalar=self.der[:, L, 1, 2, dj:dj + 1], in1=xv, op0=ALU.mult, op1=ALU.add),
                    reads=["tb%d" % pi], writes=["xg%d" % dj])
                if ci == 1:
                    done[tb] = sx
    return done, last_pe


K.s5_p1 = _s5_p1
K.s5_scan = _s5_scan
K.s5_p3 = _s5_p3
K.s5_glu = _s5_glu


def _s5_cin(self, w_in):
    P, cfg = self.P, self.cfg
    v = self.s5v
    Eg, F, t = v["Eg"], v["F"], v["t"]
    ZA, ZB = self.scan[:, 2, :], self.scan[:, 3, :]
    MUL, ADD = ALU.mult, ALU.add
    sig = None
    for i in range(3):
        for d in range(2):
            for j in range(cfg.CPS):
                mcol = V_CM + (d * 3 + i) * 4 + j
                msc = self.vecs[:, mcol:mcol + 1]
                if j == 0:
                    sig = P.op("dve", lambda e, i=i, d=d, j=j, msc=msc: e.tensor_scalar(
                        out=_cols(F[i], d), in0=_cols(Eg[:, j, :], d), scalar1=msc, scalar2=None, op0=MUL),
                        waits=w_in, reads=["Eg"], writes=["F%d" % i])
                else:
                    sig = P.op("dve", lambda e, i=i, d=d, j=j, msc=msc: e.scalar_tensor_tensor(
                        out=_cols(F[i], d), in0=_cols(Eg[:, j, :], d), scalar=msc, in1=_cols(F[i], d),
                        op0=MUL, op1=ADD), reads=["Eg", "F%d" % i], writes=["F%d" % i])
    for i in (1, 2):
        P.op("dve", lambda e: e.tensor_tensor(out=t[0], in0=F[0], in1=ZA, op=MUL), reads=["F0"], writes=["t0"])
        P.op("dve", lambda e: e.tensor_tensor(out=as3(t[1]), in0=swap_ri(F[0]), in1=as3(ZB), op=MUL),
             reads=["F0"], writes=["t1"])
        P.op("dve", lambda e: e.tensor_tensor(out=t[2], in0=t[0], in1=t[1], op=ADD), reads=["t0", "t1"], writes=["t2"])
        sig = P.op("dve", lambda e, i=i: e.tensor_tensor(out=F[0], in0=t[2], in1=F[i], op=ADD),
                   reads=["t2", "F%d" % i], writes=["F0"])
    return F[0], sig


def _s5_corr(self, cin, w_in):
    P, cfg = self.P, self.cfg
    NCH = cfg.NCH
    v = self.s5v
    SX, Q, t = v["SX"], v["Q"], v["t"]
    AA, BB = self.scan[:, 0, :], self.scan[:, 1, :]
    MUL, ADD = ALU.mult, ALU.add
    s = P.op("dve", lambda e: e.tensor_copy(out=Q[0], in_=cin), waits=w_in, writes=["Q0"])
    last = P.op("dve", lambda e: e.tensor_copy(out=SX[:, 0, :], in_=cin), waits=list(w_in) + [s], writes=["SXp"])
    for j in range(1, NCH + 1):
        qp, qn = Q[(j - 1) % 2], Q[j % 2]
        tp, tn = "Q%d" % ((j - 1) % 2), "Q%d" % (j % 2)
        P.op("dve", lambda e, qp=qp: e.tensor_tensor(out=t[0], in0=qp, in1=AA, op=MUL), reads=[tp], writes=["t0"])
        P.op("dve", lambda e, qp=qp: e.tensor_tensor(out=as3(t[1]), in0=swap_ri(qp), in1=as3(BB), op=MUL),
             reads=[tp], writes=["t1"])
        s = P.op("dve", lambda e, qn=qn: e.tensor_tensor(out=qn, in0=t[0], in1=t[1], op=ADD),
                 reads=["t0", "t1"], writes=[tn])
        if j < NCH:
            last = P.op("dve", lambda e, qn=qn, j=j: e.tensor_tensor(out=SX[:, j, :], in0=SX[:, j, :], in1=qn, op=ADD),
                        reads=[tn], writes=["SXp"])
    return last


def _s5_mixer(self, L, h_sigs):
    P, cfg = self.P, self.cfg
    self.s5v = _s5_views(self)
    v = self.s5v
    self.glu_si = 0
    self.glu_slot_free = [None] * 4
    s_p1, _ = self.s5_p1(h_sigs)
    self.barrier()
    if cfg.dbg == "p1":
        return [s_p1] * cfg.NTB
    if cfg.mode == "A":
        q, s_q, s_x = self.s5_scan(None, [s_p1])
        o1 = P.op("sp", lambda e: e.dma_start(out=self.d_Eo[:, :], in_=q), waits=[s_q], dma="oA")
        o2 = P.op("sp", lambda e: e.dma_start(out=self.d_xTo.rearrange("(k p) t -> p k t", p=128), in_=self.xT[:]),
                  dma="oA")
        o3 = P.op("sp", lambda e: e.dma_start(out=self.d_hTo.rearrange("(k p) t -> p k t", p=128),
                                              in_=self.hT[:, :, 8:8 + cfg.T]), dma="oA")
        for e in ("sp", "pe", "act", "dve", "pool"):
            P.wait(e, o3)
        return None
    if cfg.mode == "B":
        s_e = P.op("sp", lambda e: e.dma_start(out=v["Eg"].rearrange("p r c -> p (r c)")[:, 0:cfg.CPS * 128],
                                               in_=self.d_Eg[:, :]), dma="ldE")
        cin, s_c = _s5_cin(self, [s_e, s_p1])
        q, s_q, s_x = self.s5_scan(cin, [s_c])
    else:
        q, s_q, s_x = self.s5_scan(None, [s_p1])
        if cfg.do_cc:
            o1 = P.op("sp", lambda e: e.dma_start(out=self.d_bounce[:, :], in_=q), waits=[s_q], dma="cc1")
            groups = [list(range(b * cfg.CPS, (b + 1) * cfg.CPS)) for b in range(cfg.NCORES // cfg.CPS)]
            s_cc = P.op("pool", lambda e: e.collective_compute(
                "AllGather", ALU.bypass, replica_groups=groups, ins=[self.d_bounce.opt()],
                outs=[self.d_gath.opt()]), waits=[o1], dma="ccs", inc=1)
            s_e = P.op("sp", lambda e: e.dma_start(out=v["Eg"][:, 0:cfg.CPS, :],
                                                   in_=self.d_gath.rearrange("(r p) n -> p r n", p=128)),
                       waits=[s_cc], dma="cc2")
            cin, s_c = _s5_cin(self, [s_e, s_x])
            s_x = _s5_corr(self, cin, [s_c, s_x])
    self.barrier()
    if cfg.dbg == "scan":
        return [s_x] * cfg.NTB
    last_g = self.s5_p3(L, [s_x])
    self.barrier()
    if cfg.dbg == "p3":
        self.buf_free = [None] * 6
        self.slot_free = [None] * 3
        self.wb_i = 0
        return [last_g] * cfg.NTB
    done, last_pe = self.s5_glu(L, [last_g])
    self.barrier()
    self.buf_free = [None] * 6
    self.slot_free = [None] * 3
    self.wb_i = 0
    return done


K.s5_mixer = _s5_mixer
```

```python
import math
from contextlib import ExitStack

import numpy as np
import ml_dtypes

import concourse.bass as bass
import concourse.mybir as mybir
from concourse.ap import AP
from concourse.bass_utils import run_bass_kernel_spmd

F32 = mybir.dt.float32
BF16 = mybir.dt.bfloat16
AF = mybir.ActivationFunctionType
ALU = mybir.AluOpType

D = 1024
KT = 8
DFF = 2752
FT = 22
SEQ = 8192
EPS = 1e-6
LCH = 16


class Cfg:
    def __init__(self, T=2048, CPS=4, NCORES=8, do_s5=True, do_cc=True, mode="fused"):
        self.mode = mode
        self.dbg = None
        self.T = T
        self.CPS = CPS
        self.NCORES = NCORES
        self.S = T * CPS
        self.NTB = T // 512
        self.NCH = T // LCH
        self.do_s5 = do_s5
        self.do_cc = do_cc and CPS > 1


class Prog:
    ENG = ("pe", "act", "dve", "pool", "sp")

    def __init__(self, nc, es):
        self.nc, self.es = nc, es
        self.q = {e: [] for e in self.ENG}
        self.sem = {}
        self.cnt = {}
        self.waited = {}
        self.tagw = {e: {} for e in self.ENG}
        self.tagr = {e: {} for e in self.ENG}

    def _sem(self, name):
        if name not in self.sem:
            self.sem[name] = self.es.enter_context(self.nc.semaphore(name))
            self.cnt[name] = 0
        return name

    def wait(self, eng, sig):
        if sig is None:
            return
        name, val = sig
        if self.waited.get((eng, name), 0) >= val:
            return
        self.waited[(eng, name)] = val
        self.q[eng].append(("w", name, val))

    def op(self, eng, fn, waits=(), reads=(), writes=(), mark=None, dma=None, inc=16):
        for s in waits:
            self.wait(eng, s)
        if dma is None and eng != "pe":
            tw, tr = self.tagw[eng], self.tagr[eng]
            for t in reads:
                if t in tw:
                    self.wait(eng, tw[t])
            for t in writes:
                if t in tw:
                    self.wait(eng, tw[t])
                if t in tr:
                    self.wait(eng, tr[t])
        if dma is not None:
            name = self._sem(dma)
            self.cnt[name] += inc
            sig = (name, self.cnt[name])
            self.q[eng].append(("o", fn, name, inc))
            return sig
        if eng == "pe" and not mark:
            self.q[eng].append(("o", fn, None, 0))
            return None
        name = self._sem("m_" + eng)
        self.cnt[name] += 1
        sig = (name, self.cnt[name])
        self.q[eng].append(("o", fn, name, 1))
        if eng != "pe":
            for t in reads:
                self.tagr[eng][t] = sig
            for t in writes:
                self.tagw[eng][t] = sig
        return sig

    def emit(self):
        with self.nc.Block() as block:
            for eng, deco in (("pe", block.tensor), ("act", block.scalar), ("dve", block.vector),
                              ("pool", block.gpsimd), ("sp", block.sync)):
                def f(e, eng=eng):
                    for it in self.q[eng]:
                        if it[0] == "w":
                            e.wait_ge(self.sem[it[1]], it[2])
                        else:
                            ins = it[1](e)
                            if it[2] is not None:
                                ins.then_inc(self.sem[it[2]], it[3])
                deco(f)


def sub(ap, p0, p1):
    return ap[p0:p1]


def _pk(v):
    v = np.asarray(v, np.float32).reshape(-1, 128)
    return np.ascontiguousarray(v.T)


def _gen_layout(a):
    a = np.asarray(a, np.float32)
    rest = a.shape[3:]
    a = a.reshape(2, 32, 2, 64, *rest)
    a = np.moveaxis(a, [2, 3], [0, 1])
    return np.ascontiguousarray(a.reshape(128, 64, *rest))


def const_tables():
    k = np.arange(128)
    hk, gk, sk = k % 16, k // 16, k // 16
    E = np.zeros((128, 8, 15, 16), np.float32)
    Fm = np.zeros((128, 8, 15, 16), np.float32)
    for kk in range(128):
        E[kk, gk[kk], 7, hk[kk]] = 1.0
        Fm[kk, sk[kk], 7, hk[kk]] = 1.0
    ident = np.eye(128, dtype=np.float32)
    s_lo = (np.arange(128) // 16)[:, None]
    t_lo = (np.arange(128) // 16)[None, :]
    mle = (s_lo <= t_lo).astype(np.float32)
    mge = (s_lo >= t_lo).astype(np.float32)
    ones = np.full((128, 128), 1.0 / D, np.float32)
    cb = np.concatenate([E.reshape(128, -1), Fm.reshape(128, -1), ident, mle, mge, ones], axis=1)
    return cb.astype(ml_dtypes.bfloat16)


NCB = 2 * 1920 + 4 * 128
V_NG = 0
V_FG = 48
V_PB = 56
V_PS = 64
V_SD = 72
V_GB = 80
V_MB = 96
V_C = 240
V_EDGE = 248
V_ICE = 250
V_CM = 314
NV = 338
POOLW = (2, 4, 8, 16)


def prep_inputs(cfg, inp):
    T, CPS = cfg.T, cfg.CPS
    S = cfg.S
    x = np.asarray(inp["x"], np.float32)
    cb = const_tables()
    ssm = {}
    ssm["lamre"] = _gen_layout(inp["ssm_lam_re"][0])
    ssm["lamim"] = _gen_layout(inp["ssm_lam_im"][0])
    ld = np.asarray(inp["ssm_log_dt"][0], np.float32)
    ssm["logdt"] = _gen_layout(np.broadcast_to(ld[:, :, None], (2, 64, 64)))
    ssm["bre"] = _gen_layout(inp["ssm_b_re"][0])
    ssm["bim"] = _gen_layout(inp["ssm_b_im"][0])
    ssm["cre"] = _gen_layout(np.swapaxes(np.asarray(inp["ssm_c_re"][0]), 2, 3))
    ssm["cim"] = _gen_layout(np.swapaxes(np.asarray(inp["ssm_c_im"][0]), 2, 3))
    ssmp = np.concatenate([ssm["lamre"], ssm["lamim"], ssm["logdt"]], axis=1)
    ssmbc = np.concatenate([ssm[k].reshape(128, -1) for k in ("bre", "bim", "cre", "cim")], axis=1)
    shared = {
        "mod_w": np.ascontiguousarray(inp["mod_w"], np.float32),
        "w_in": np.ascontiguousarray(inp["ffn_w_in"], np.float32),
        "w_out": np.ascontiguousarray(inp["ffn_w_out"], np.float32),
        "pool_w": np.ascontiguousarray(inp["pool_w"][0], np.float32),
        "glu_w": np.ascontiguousarray(inp["glu_w"][0], np.float32),
        "cb": cb, "ssmp": np.ascontiguousarray(ssmp), "ssmbc": np.ascontiguousarray(ssmbc),
    }
    maps = []
    for k in range(cfg.NCORES):
        b, pos = k // CPS, k % CPS
        t0 = pos * T
        xs = x[b, t0:t0 + T, :]
        xh = np.zeros((16, D), np.float32)
        if t0 >= 8:
            xh[0:8] = x[b, t0 - 8:t0]
        if t0 + T + 8 <= S:
            xh[8:16] = x[b, t0 + T:t0 + T + 8]
        vec = np.zeros((128, NV), np.float32)
        vec[:, V_NG:V_NG + 48] = _pk(np.asarray(inp["norm_g"]).reshape(-1))
        vec[:, V_FG:V_FG + 8] = _pk(inp["final_g"])
        vec[:, V_PB:V_PB + 8] = _pk(inp["pool_b"][0])
        vec[:, V_PS:V_PS + 8] = _pk(inp["pool_scale"][0])
        vec[:, V_SD:V_SD + 8] = _pk(inp["ssm_d"][0])
        vec[:, V_GB:V_GB + 16] = _pk(inp["glu_b"][0])
        vec[:, V_MB:V_MB + 144] = _pk(np.asarray(inp["mod_b"]).reshape(-1))
        vec[:, V_C:V_C + 8] = _pk(inp["c"][b])
        vec[:, V_EDGE] = 1.0 if t0 >= 8 else 0.0
        vec[:, V_EDGE + 1] = 1.0 if t0 + T + 8 <= S else 0.0
        for wi, w in enumerate(POOLW):
            left, right = w // 2, w - 1 - w // 2
            for e in range(16):
                t = t0 + e if e < 8 else t0 + T - 16 + e
                lo, hi = max(t - left, 0), min(t + right, S - 1)
                vec[:, V_ICE + wi * 16 + e] = 1.0 / float(hi - lo + 1)
        for i in range(3):
            jf, jb = pos - 3 + i, pos + 3 - i
            if 0 <= jf < CPS:
                vec[:, V_CM + (0 * 3 + i) * 4 + jf] = 1.0
            if 0 <= jb < CPS:
                vec[:, V_CM + (1 * 3 + i) * 4 + jb] = 1.0
        m = dict(shared)
        m["xT"] = np.ascontiguousarray(xs.T)
        m["xTh"] = np.ascontiguousarray(xh.T)
        m["vecs"] = vec
        maps.append(m)
    return maps


class K:
    def __init__(self, cfg):
        self.cfg = cfg
        T = cfg.T
        nc = self.nc = bass.Bass("TRN2", target_bir_lowering=False)
        es = self.es = ExitStack()
        P = self.P = Prog(nc, es)

        def din(name, shape, dt=F32):
            return nc.dram_tensor(name, list(shape), dt, kind="ExternalInput").ap()

        self.d_xT = din("xT", [D, T])
        self.d_xTh = din("xTh", [D, 16])
        self.d_vecs = din("vecs", [128, NV])
        self.d_cb = din("cb", [128, NCB], BF16)
        self.d_ssmp = din("ssmp", [128, 192])
        self.d_ssmbc = din("ssmbc", [128, 4096])
        self.d_modw = din("mod_w", [2, D, 9 * D])
        self.d_win = din("w_in", [2, 2, D, 2 * DFF])
        self.d_wout = din("w_out", [2, 2, DFF, D])
        self.d_poolw = din("pool_w", [4, 256, 256])
        self.d_gluw = din("glu_w", [D, 2 * D])
        if cfg.mode == "A":
            self.d_xTo = nc.dram_tensor("xTo", [D, T], F32, kind="ExternalOutput").ap()
            self.d_hTo = nc.dram_tensor("hTo", [D, T], BF16, kind="ExternalOutput").ap()
            self.d_Eo = nc.dram_tensor("Eo", [128, 128], F32, kind="ExternalOutput").ap()
        else:
            self.d_out = nc.dram_tensor("outT", [D, T], F32, kind="ExternalOutput").ap()
        if cfg.mode == "B":
            self.d_hTi = din("hTi", [D, T], BF16)
            self.d_Eg = din("Eg", [128, cfg.CPS * 128])
        self.d_BA = nc.dram_tensor("BAd", [32, 128, 1024], BF16, kind="Internal").ap()
        self.d_CA = nc.dram_tensor("CAd", [32, 128, 1024], BF16, kind="Internal").ap()
        self.d_T = nc.dram_tensor("Td", [64, 128, 512], BF16, kind="Internal").ap()
        self.d_bounce = nc.dram_tensor("ebounce", [128, 128], F32).ap()
        self.d_gath = nc.dram_tensor("egath", [cfg.CPS * 128, 128], F32).ap()

        def sb(name, shape, dt=F32):
            return es.enter_context(nc.sbuf_tensor(name, list(shape), dt))

        self.xT = sb("xT_sb", [128, KT, T])
        self.TH = T + 16
        AR_H = KT * self.TH // 2
        self.AR_R = 9216
        self.AR = sb("arena", [128, max(AR_H + self.AR_R, 16896)])
        self.hT = self.AR[:, 0:AR_H].bitcast(BF16).rearrange("p (k t) -> p k t", k=KT)
        self.R0 = AR_H
        self.wstage = sb("wstage", [128, 3, 1024])
        self.wbf = sb("wbf", [128, 6, 4096], BF16)
        self.cb = sb("cb_sb", [128, NCB], BF16)
        self.vecs = sb("vecs_sb", [128, NV])
        self.modT = sb("modT", [128, 2, 72])
        self.der = sb("der", [128, 2, 3, 3, KT])
        self.xTh = sb("xTh_sb", [128, KT, 16])
        self.hTh = sb("hTh_sb", [128, KT, 16], BF16)
        self.condb = sb("condb", [128, KT], BF16)
        self.modrow = sb("modrow", [1, 1, 256])
        self.one11 = sb("one11", [1, 1])
        self.epsc = sb("epsc", [128, 1])
        self.pv = sb("poolvec", [128, 2, KT])
        self.scan = sb("scanc", [128, 4, 128])
        self.ps = [es.enter_context(nc.psum_tensor(f"ps{i}", [128, 512], F32)) for i in range(8)]
        self.slot_free = [None] * 3
        self.si = 0
        self.buf_free = [None] * 6
        c = self.cb
        self.cE = c[:, 0:1920].rearrange("p (g u h) -> p g u h", g=8, u=15)
        self.cF = c[:, 1920:3840].rearrange("p (g u h) -> p g u h", g=8, u=15)
        self.cI = c[:, 3840:3968]
        self.cMLE = c[:, 3968:4096]
        self.cMGE = c[:, 4096:4224]
        self.cONE = c[:, 4224:4352]
        self.sig_x = None
        self.sig_c = None

    def R(self, off, n):
        assert off + n <= self.AR_R, (off, n)
        return self.AR[:, self.R0 + off:self.R0 + off + n]

    def Rb(self, off, n):
        assert off + n // 2 <= self.AR_R
        return self.AR[:, self.R0 + off:self.R0 + off + n // 2].bitcast(BF16)

    def barrier(self, engs=("pe", "act", "dve", "pool", "sp")):
        P = self.P
        sigs = []
        for e in ("pe", "act", "dve", "pool"):
            nm = "m_" + e
            if nm in P.cnt and P.cnt[nm] > 0:
                sigs.append((nm, P.cnt[nm]))
        for e in engs:
            for s in sigs:
                if s[0] != "m_" + e:
                    P.wait(e, s)

    def witem(self, src, dst, npart, a, b, dst_wait=None, cast_eng="pool"):
        P = self.P
        s = self.si % 3
        self.si += 1
        st = self.wstage[:npart, s, 0:a * b].rearrange("p (a b) -> p a b", a=a)
        sd = P.op("sp", lambda e: e.dma_start(out=st, in_=src), waits=[self.slot_free[s]], dma="wd%d" % s)
        w = [sd]
        if dst_wait is not None:
            w.append(dst_wait)
        sc = P.op(cast_eng, lambda e: e.tensor_copy(out=dst, in_=st) if cast_eng != "act"
                  else e.activation(out=dst, in_=st, func=AF.Copy),
                  waits=w, reads=["wstage%d" % s], writes=["wbf_%d" % self.si])
        self.slot_free[s] = sc
        return sc

    def startup(self):
        P, cfg = self.P, self.cfg
        q = "act"
        s1 = P.op(q, lambda e: e.dma_start(out=self.vecs[:], in_=self.d_vecs[:, :]), dma="ld")
        s2 = P.op(q, lambda e: e.dma_start(out=self.cb[:], in_=self.d_cb[:, :]), dma="ld")
        self.sig_c = s2
        if cfg.do_s5:
            xs_ = self.xT[:].rearrange("p k t -> p (k t)")
            raw_ = self.wbf[:, 4:6, :].rearrange("p a n -> p (a n)").bitcast(F32)
            self.sig_sp = P.op(q, lambda e: e.dma_start(out=xs_[:, 3200:3392], in_=self.d_ssmp[:, :]), dma="ldg1")
            self.sig_sbc = P.op(q, lambda e: e.dma_start(out=raw_, in_=self.d_ssmbc[:, :]), dma="ldg2")
        if not cfg.do_s5:
            self.load_x()
        self.sig_one = P.op("dve", lambda e: e.memset(self.one11[:], 1.0), writes=["one11"])
        self.sig_eps = P.op("dve", lambda e: e.memset(self.epsc[:], EPS), writes=["epsc"])
        self.sig_cond = P.op("act", lambda e: e.activation(out=self.condb[:], in_=self.vecs[:, V_C:V_C + 8],
                                                           func=AF.Silu), waits=[s2], writes=["condb"])

    def load_x(self):
        P = self.P
        q = "act"
        xv = self.d_xT.rearrange("(k p) t -> p k t", p=128)
        sx = None
        for k in range(KT):
            sx = P.op(q, lambda e, k=k: e.dma_start(out=self.xT[:, k, :], in_=xv[:, k, :]), dma="ldx")
        sx = P.op(q, lambda e: e.dma_start(out=self.xTh[:], in_=self.d_xTh.rearrange("(k p) t -> p k t", p=128)),
                  dma="ldx")
        self.sig_x = sx

    def mod_block(self, L, col0, ncol, dst, b, cast_eng, pr, pt, defer=False):
        P = self.P
        src = self.d_modw[L].rearrange("(k p) n -> p k n", p=128)
        kper = 1024 // ncol
        sc = None
        for i in range(8 // kper):
            sc = self.witem(src[:, kper * i:kper * (i + 1), col0:col0 + ncol], dst[:, kper * i:kper * (i + 1), :],
                            128, kper, ncol, dst_wait=self.mbuf_free[b] if i == 0 else None, cast_eng=cast_eng)
        sg = None
        for k in range(KT):
            sg = P.op("pe", lambda e, k=k: e.matmul(pr, lhsT=self.condb[:, k:k + 1], rhs=dst[:, k, :],
                                                    start=(k == 0), stop=(k == KT - 1)),
                      waits=[sc, self.sig_cond, self.mod_ps_free] + list(self.o_free) if k == 0 else (),
                      mark=(k == KT - 1))
        self.mbuf_free[b] = sg
        r = 0
        s_ev = P.op("act", lambda e: e.activation(out=self.modrow[0:1, r, 0:ncol], in_=pr, func=AF.Copy),
                    waits=[sg, self.modrow_free[r]], writes=["modrow%d" % r])
        s_t = None
        nt = ncol // 128
        for j in range(nt):
            s_t = P.op("pe", lambda e, j=j: e.matmul(pt[:, j:j + 1], lhsT=self.modrow[0:1, r, j * 128:(j + 1) * 128],
                                                     rhs=self.one11[0:1, 0:1], start=True, stop=True),
                       waits=[s_ev, self.modT_ps_free, self.sig_one, self.pss_free] if j == 0 else (), mark=(j == nt - 1))
        self.modrow_free[r] = s_t
        self.mod_ps_free = s_ev
        if defer:
            return s_t
        c0 = col0 // 128
        s_m = P.op("dve", lambda e: e.tensor_tensor(out=self.modT[:, L, c0:c0 + nt], in0=pt[:, 0:nt],
                                                    in1=self.vecs[:, V_MB + L * 72 + c0:V_MB + L * 72 + c0 + nt],
                                                    op=ALU.add),
                   waits=[s_t], writes=["modT"])
        self.modT_ps_free = s_m
        return s_m

    def mod_init(self, nb):
        self.mod_ri = 0
        self.mbuf_free = [None] * nb
        self.modrow_free = [None, None]
        self.mod_ps_free = None
        self.modT_ps_free = None

    def mod_derive(self, L, sig):
        P = self.P
        out = None
        for s in range(3):
            mt = self.modT[:, L, s * 24:(s + 1) * 24].rearrange("p (m k) -> p m k", m=3)
            ng = self.vecs[:, V_NG + (L * 3 + s) * 8:V_NG + (L * 3 + s) * 8 + 8]
            P.op("dve", lambda e, mt=mt, ng=ng, s=s: e.scalar_tensor_tensor(
                out=self.der[:, L, s, 0, :], in0=mt[:, 1, :], scalar=1.0, in1=ng, op0=ALU.add, op1=ALU.mult),
                waits=[sig], writes=["der"])
            P.op("dve", lambda e, mt=mt, s=s: e.tensor_copy(out=self.der[:, L, s, 1, :], in_=mt[:, 0, :]),
                 writes=["der"])
            out = P.op("dve", lambda e, mt=mt, s=s: e.tensor_scalar(
                out=self.der[:, L, s, 2, :], in0=mt[:, 2, :], scalar1=(1.0 if s == 1 else 0.5), scalar2=None,
                op0=ALU.mult), writes=["der"])
        return out

    def norm_block(self, xsrc, n, dst, L, s, waits=(), final=False, outbuf=None):
        P = self.P
        i = self.nb_i
        self.nb_i += 1
        xsq = self.Rb(3072, KT * 512).rearrange("p (k t) -> p k t", k=KT)
        rstd = self.R(7168 + (i % 2) * 512, 512)
        pss = self.ps[6]
        s_sq = P.op("act", lambda e: e.activation(out=xsq[:, :, 0:n], in_=xsrc, func=AF.Square),
                    waits=list(waits) + [self.xsq_free[0]], writes=["xsq"])
        s_mm = None
        for k in range(KT):
            s_mm = P.op("pe", lambda e, k=k: e.matmul(pss[:, 0:n], lhsT=self.cONE, rhs=xsq[:, k, 0:n],
                                                      start=(k == 0), stop=(k == KT - 1)),
                        waits=[s_sq, self.pss_free, self.sig_c, self.modT_ps_free] + list(self.o_free)
                        if k == 0 else (), mark=(k == KT - 1))
        self.xsq_free[0] = s_mm
        s_q = P.op("act", lambda e: e.activation(out=rstd[:, 0:n], in_=pss[:, 0:n], func=AF.Sqrt, bias=self.epsc[:, 0:1]),
                   waits=[s_mm, self.rstd_free[i % 2], self.sig_eps], writes=["rstd%d" % (i % 2)])
        self.pss_free = s_q
        s_r = P.op("dve", lambda e: e.reciprocal(out=rstd[:, 0:n], in_=rstd[:, 0:n]),
                   waits=[s_q], writes=["rstd%d" % (i % 2)])
        last = None
        for k in range(KT):
            if final:
                gk = self.vecs[:, V_FG + k:V_FG + k + 1]
                last = P.op("dve", lambda e, k=k, gk=gk: e.scalar_tensor_tensor(
                    out=outbuf[:, k, 0:n], in0=xsrc[:, k, :], scalar=gk, in1=rstd[:, 0:n], op0=ALU.mult, op1=ALU.mult),
                    reads=["rstd%d" % (i % 2)], writes=["outbuf"])
            else:
                G = self.der[:, L, s, 0, k:k + 1]
                Sh = self.der[:, L, s, 1, k:k + 1]
                j = self.tmp_i
                self.tmp_i += 1
                tmp = self.R(8192 + (j % 2) * 512, 512)
                s_t = P.op("dve", lambda e, k=k, tmp=tmp: e.tensor_tensor(out=tmp[:, 0:n], in0=xsrc[:, k, :],
                                                                          in1=rstd[:, 0:n], op=ALU.mult),
                           waits=[self.tmp_free[j % 2]], reads=["rstd%d" % (i % 2)], writes=["tmpn%d" % (j % 2)])
                last = P.op("act", lambda e, k=k, tmp=tmp, G=G, Sh=Sh: e.activation(
                    out=dst[:, k, :], in_=tmp[:, 0:n], func=AF.Identity, bias=Sh, scale=G),
                    waits=[s_t], writes=["hT"])
                self.tmp_free[j % 2] = last
        self.rstd_free[i % 2] = last
        return last

    def norm_init(self):
        self.nb_i = 0
        self.tmp_i = 0
        self.xsq_free = [None, None]
        self.rstd_free = [None, None]
        self.tmp_free = [None, None]
        self.pss_free = None

    def norm_all(self, L, s, waits=(), halo=False):
        cfg = self.cfg
        sigs = []
        for tb in range(cfg.NTB):
            c0 = tb * 512
            sigs.append(self.norm_block(self.xT[:, :, c0:c0 + 512], 512, self.hT[:, :, 8 + c0:8 + c0 + 512], L, s,
                                        waits=waits))
        if halo:
            sigs.append(self.norm_block(self.xTh[:, :, :], 16, self.hTh[:, :, :], L, s, waits=waits))
        return sigs

    def ffn(self, L, j, s, h_sigs, halo=False, extra=None):
        P, cfg = self.P, self.cfg
        chunks = [[0, 1, 2, 3], [4, 5, 6, 7], [8, 9, 10, 11], [12, 13, 14, 15], [16, 17, 18, 19], [20, 21]]
        win = self.d_win[L, j].rearrange("(k p) n -> p k n", p=128)
        wout = self.d_wout[L, j]
        nblk = cfg.NTB + (1 if halo else 0)
        units = []
        nob = 2 if extra is not None else 4

        def load_chunk(q):
            F = chunks[q]
            f0 = F[0] * 128
            W = sum(128 if f < 21 else 64 for f in F)
            bidx = [self.wb_i % 6, (self.wb_i + 1) % 6, (self.wb_i + 2) % 6]
            self.wb_i += 3
            g = self.wbf[:, bidx[0], :].rearrange("p (k n) -> p k n", k=8)
            u = self.wbf[:, bidx[1], :].rearrange("p (k n) -> p k n", k=8)
            o = self.wbf[:, bidx[2], :].rearrange("p (f n) -> p f n", f=4)
            items = []
            for (dst, cbase, bi) in ((g, f0, bidx[0]), (u, DFF + f0, bidx[1])):
                for i in range(4):
                    items.append((win[:, 2 * i:2 * i + 2, cbase:cbase + W], dst[:, 2 * i:2 * i + 2, 0:W], 128, 2, W,
                                  bi))
            for fi, f in enumerate(F):
                rows = 128 if f < 21 else 64
                items.append((wout[f * 128:f * 128 + rows, :].rearrange("p (a n) -> p a n", a=1),
                              o[:rows, fi:fi + 1, :], rows, 1, 1024, bidx[2]))
            return dict(F=F, g=g, u=u, o=o, bidx=bidx, ready=[None, None], items=items, n0=len(items))

        def pump(ck, n):
            while n > 0 and ck["items"]:
                src, dst, npart, a, b, bi = ck["items"].pop(0)
                k = self.cast_i
                self.cast_i += 1
                eng = "pool" if k % 3 == 2 else "dve"
                sc = self.witem(src, dst, npart, a, b, dst_wait=self.buf_free[bi] if bi is not None else None,
                                cast_eng=eng)
                ck["ready"][0 if eng == "dve" else 1] = sc
                n -= 1

        def blk_cols(tb):
            if tb < cfg.NTB:
                return (self.hT[:, :, 8 + tb * 512:8 + tb * 512 + 512], self.xT[:, :, tb * 512:tb * 512 + 512], 512)
            return (self.hTh[:, :, :], self.xTh[:, :, :], 16)

        def emit_gu(ck, tb, ui):
            hv, xv, n = blk_cols(tb)
            ab = ui % 2
            actT = self.Rb(ab * 1024, 4 * 512).rearrange("p (f t) -> p f t", f=4)
            sig = None
            for fi, f in enumerate(ck["F"]):
                rows = 128 if f < 21 else 64
                pi = self.gu_i % 2
                self.gu_i += 1
                pg, pu = self.ps[pi], self.ps[2 + pi]
                w0 = list(ck["ready"]) + [h_sigs[tb], self.gu_free[pi]]
                if fi == 0:
                    w0.append(self.act_free[ab])
                for (pt, wt) in ((pg, ck["g"]), (pu, ck["u"])):
                    for k in range(KT):
                        sig = P.op("pe", lambda e, pt=pt, wt=wt, k=k, fi=fi, rows=rows: e.matmul(
                            pt[:rows, 0:n], lhsT=wt[:, k, fi * 128:fi * 128 + rows], rhs=hv[:, k, :],
                            start=(k == 0), stop=(k == KT - 1)),
                            waits=w0 if (pt is pg and k == 0) else (), mark=(pt is pu and k == KT - 1))
                sgb = self.R(2048 + pi * 512, 512)
                s_a = P.op("act", lambda e, pg=pg, sgb=sgb, rows=rows: e.activation(
                    out=sgb[:rows, 0:n], in_=pg[:rows, 0:n], func=AF.Silu),
                    waits=[sig, self.sg_free[pi]], writes=["sg%d" % pi])
                s_d = P.op("dve", lambda e, pu=pu, sgb=sgb, rows=rows, fi=fi, actT=actT: e.tensor_tensor(
                    out=actT[:rows, fi, 0:n], in0=sgb[:rows, 0:n], in1=pu[:rows, 0:n], op=ALU.mult),
                    waits=[s_a], writes=["actT%d" % ab])
                self.gu_free[pi] = s_d
                self.sg_free[pi] = s_d
            return s_d

        def emit_out(ck, tb, ui, act_sig, lastchunk):
            hv, xv, n = blk_cols(tb)
            ab = ui % 2
            actT = self.Rb(ab * 1024, 4 * 512).rearrange("p (f t) -> p f t", f=4)
            F = ck["F"]
            sig = None
            s_x = None
            for dj in range(KT):
                pi = self.o_i % nob
                self.o_i += 1
                po = self.ps[4 + pi]
                for fi, f in enumerate(F):
                    rows = 128 if f < 21 else 64
                    sig = P.op("pe", lambda e, po=po, fi=fi, rows=rows, dj=dj: e.matmul(
                        po[:, 0:n], lhsT=ck["o"][:rows, fi, dj * 128:(dj + 1) * 128], rhs=actT[:rows, fi, 0:n],
                        start=(fi == 0), stop=(fi == len(F) - 1)),
                        waits=[act_sig, self.o_free[pi], self.pss_free, self.modT_ps_free, self.mod_ps_free]
                        if fi == 0 else (), mark=(fi == len(F) - 1))
                Gt = self.der[:, L, s, 2, dj:dj + 1]
                s_x = P.op("dve", lambda e, po=po, dj=dj, Gt=Gt: e.scalar_tensor_tensor(
                    out=xv[:, dj, :], in0=po[:, 0:n], scalar=Gt, in1=xv[:, dj, :], op0=ALU.mult, op1=ALU.add),
                    waits=[sig], writes=["xT%d" % dj])
                self.o_free[pi] = s_x
            self.act_free[ab] = sig
            return sig, s_x

        seq = [(q, tb) for q in range(len(chunks)) for tb in range(nblk)]
        cks = {}
        cks[0] = load_chunk(0)
        pump(cks[0], 1000)
        done = [None] * nblk
        prev = None
        for ui, (q, tb) in enumerate(seq):
            a_sig = emit_gu(cks[q], tb, self.unit_i + ui)
            if prev is not None:
                pq, ptb, pui, pa = prev
                pe_sig, x_sig = emit_out(cks[pq], ptb, pui, pa, pq == len(chunks) - 1)
                if ptb == nblk - 1:
                    for b in cks[pq]["bidx"]:
                        self.buf_free[b] = pe_sig
                if pq == len(chunks) - 1:
                    done[ptb] = x_sig
            prev = (q, tb, self.unit_i + ui, a_sig)
            if q + 1 < len(chunks):
                if tb == 0:
                    cks[q + 1] = load_chunk(q + 1)
                    if extra is not None:
                        extra(q)
                per = -(-cks[q + 1]["n0"] // nblk)
                pump(cks[q + 1], 1000 if tb == nblk - 1 else per)
        pq, ptb, pui, pa = prev
        pe_sig, x_sig = emit_out(cks[pq], ptb, pui, pa, True)
        for b in cks[pq]["bidx"]:
            self.buf_free[b] = pe_sig
        done[ptb] = x_sig
        self.unit_i += len(seq)
        return done

    def ffn_init(self):
        self.wb_i = 0
        self.cast_i = 0
        self.gu_i = 0
        self.o_i = 0
        self.unit_i = 0
        self.gu_free = [None, None]
        self.sg_free = [None, None]
        self.o_free = [None, None, None, None]
        self.act_free = [None, None]

    def pool_mixer(self, L, h_sigs):
        P, cfg = self.P, self.cfg
        T, TH = cfg.T, self.TH
        hw = h_sigs
        sl = P.op("dve", lambda e: e.tensor_scalar(out=self.hT[:, :, 0:8], in0=self.hTh[:, :, 0:8],
                                                   scalar1=self.vecs[:, V_EDGE:V_EDGE + 1], scalar2=None, op0=ALU.mult),
                  waits=hw, writes=["hT"])
        sr = P.op("dve", lambda e: e.tensor_scalar(out=self.hT[:, :, 8 + T:16 + T], in0=self.hTh[:, :, 8:16],
                                                   scalar1=self.vecs[:, V_EDGE + 1:V_EDGE + 2], scalar2=None,
                                                   op0=ALU.mult), writes=["hT"])
        b = self.wb_i % 6
        self.wb_i += 1
        pw = self.wbf[:, b, 0:2048].rearrange("p (k g d) -> p k g d", k=2, g=4)
        src = self.d_poolw.rearrange("g (k p) d -> p k g d", p=128)
        scw = None
        for kk in range(2):
            scw = self.witem(src[:, kk, :, :], pw[:, kk, :, :], 128, 4, 256,
                             dst_wait=self.buf_free[b] if kk == 0 else None)
        Gt = self.der[:, L, 1, 2, :]
        P.op("dve", lambda e: e.tensor_tensor(out=self.pv[:, 0, :], in0=Gt, in1=self.vecs[:, V_PS:V_PS + 8],
                                              op=ALU.mult), writes=["pv"])
        s_pv = P.op("dve", lambda e: e.tensor_tensor(out=self.pv[:, 1, :], in0=self.pv[:, 0, :],
                                                     in1=self.vecs[:, V_PB:V_PB + 8], op=ALU.mult),
                    reads=["pv"], writes=["pv"])
        sa = self.R(0, TH)
        sbb = self.R(TH, TH)
        done = [None] * cfg.NTB
        pT_free = [None, None]
        last_pe = None
        for gi, w in enumerate(POOLW):
            pT = self.Rb(2 * TH + 32 + (gi % 2) * T, 2 * T).rearrange("p (k t) -> p k t", k=2)
            s_p = None
            for kk in range(2):
                k = 2 * gi + kk
                hk = self.hT[:, k, :]
                sg_ = P.op("dve", lambda e, hk=hk: e.tensor_tensor(out=sa[:, 1:TH], in0=hk[:, 0:TH - 1], in1=hk[:, 1:TH],
                                                                   op=ALU.add), waits=[sl, sr], reads=["hT"], writes=["sa"])
                cur, oth = sa, sbb
                curn, othn = "sa", "sb"
                step = 1
                lo, hi = 1, TH
                ww = 2
                while ww < w:
                    nlo, nhi = lo + step, hi - step
                    P.op("dve", lambda e, cur=cur, oth=oth, nlo=nlo, nhi=nhi, step=step: e.tensor_tensor(
                        out=oth[:, nlo:nhi], in0=cur[:, nlo - step:nhi - step], in1=cur[:, nlo + step:nhi + step],
                        op=ALU.add), reads=[curn], writes=[othn])
                    cur, oth = oth, cur
                    curn, othn = othn, curn
                    lo, hi = nlo, nhi
                    step *= 2
                    ww *= 2
                P.op("dve", lambda e, cur=cur, hk=hk, kk=kk, pT=pT, w=w: e.scalar_tensor_tensor(
                    out=pT[:, kk, :], in0=cur[:, 8:8 + T], scalar=1.0 / w, in1=hk[:, 8:8 + T],
                    op0=ALU.mult, op1=ALU.subtract), waits=[pT_free[gi % 2]], reads=[curn], writes=["pT"])
                ice = self.vecs[:, V_ICE + gi * 16:V_ICE + gi * 16 + 16]
                for (c0, e0) in ((8, 0), (T, 8)):
                    tmpe = self.R(2 * TH, 8)
                    P.op("dve", lambda e, cur=cur, c0=c0, e0=e0, ice=ice, tmpe=tmpe: e.tensor_tensor(
                        out=tmpe[:, 0:8], in0=cur[:, c0:c0 + 8], in1=ice[:, e0:e0 + 8], op=ALU.mult),
                        reads=[curn, "tmpe"], writes=["tmpe"])
                    s_p = P.op("dve", lambda e, hk=hk, c0=c0, kk=kk, pT=pT, tmpe=tmpe: e.tensor_tensor(
                        out=pT[:, kk, c0 - 8:c0], in0=tmpe[:, 0:8], in1=hk[:, c0:c0 + 8], op=ALU.subtract),
                        reads=["tmpe", "pT"], writes=["pT"])
            for dl in range(2):
                dj = 2 * gi + dl
                for tb in range(cfg.NTB):
                    pi = self.o_i % 2
                    self.o_i += 1
                    po = self.ps[4 + pi]
                    sig = None
                    for kk in range(2):
                        sig = P.op("pe", lambda e, po=po, kk=kk, gi=gi, dl=dl, tb=tb, pT=pT: e.matmul(
                            po[:, 0:512], lhsT=pw[:, kk, gi, dl * 128:(dl + 1) * 128],
                            rhs=pT[:, kk, tb * 512:(tb + 1) * 512], start=(kk == 0), stop=(kk == 1)),
                            waits=[s_p, scw, self.o_free[pi]] if kk == 0 else (), mark=(kk == 1))
                    last_pe = sig
                    xv = self.xT[:, dj, tb * 512:(tb + 1) * 512]
                    P.op("dve", lambda e, po=po, xv=xv, dj=dj: e.scalar_tensor_tensor(
                        out=xv, in0=po[:, 0:512], scalar=self.pv[:, 0, dj:dj + 1], in1=xv, op0=ALU.mult, op1=ALU.add),
                        waits=[sig, s_pv], writes=["xTa"])
                    s_x = P.op("dve", lambda e, xv=xv, dj=dj: e.tensor_scalar(
                        out=xv, in0=xv, scalar1=self.pv[:, 1, dj:dj + 1], scalar2=None, op0=ALU.add),
                        reads=["xTa"], writes=["xTa"])
                    self.o_free[pi] = s_x
                    done[tb] = s_x
            pT_free[gi % 2] = last_pe
        self.buf_free[b] = last_pe
        return done

    def final(self, waits_per_tb):
        P, cfg = self.P, self.cfg
        ov = self.d_out.rearrange("(k p) t -> p k t", p=128)
        st = None
        ob_free = [None, None]
        for tb in range(cfg.NTB):
            ob = self.wbf[:, 2 * (tb % 2):2 * (tb % 2) + 2, :].rearrange("p a n -> p (a n)").bitcast(F32) \
                .rearrange("p (k t) -> p k t", k=KT)
            sig = self.norm_block(self.xT[:, :, tb * 512:(tb + 1) * 512], 512, None, 0, 0,
                                  waits=[waits_per_tb[tb], ob_free[tb % 2]], final=True, outbuf=ob)
            st = P.op("sp", lambda e, ob=ob, tb=tb: e.dma_start(out=ov[:, :, tb * 512:(tb + 1) * 512], in_=ob),
                      waits=[sig], dma="st")
            ob_free[tb % 2] = st
        P.wait("sp", st)
        for e in ("pe", "act", "dve", "pool"):
            P.wait(e, st)

    def build(self):
        P, cfg = self.P, self.cfg
        self.norm_init()
        self.ffn_init()
        self.startup()
        if cfg.mode == "B":
            self.mod_init(3)
            sm = None
            for cbi in range(36):
                b = cbi % 3
                dst = self.wbf[:, b, 0:2048].rearrange("p (k n) -> p k n", k=8)
                sm = self.mod_block(1, cbi * 256, 256, dst, b, "act", self.ps[7][0:1, 0:256], self.ps[6][:, 0:2])
                self.buf_free[b] = self.mbuf_free[b]
            d1 = self.mod_derive(1, sm)
            for _ in self.s5_gen():
                pass
            self.barrier()
            self.load_x()
            s_h = P.op("act", lambda e: e.dma_start(out=self.hT[:, :, 8:8 + cfg.T],
                                                    in_=self.d_hTi.rearrange("(k p) t -> p k t", p=128)), dma="ldh")
            for e in ("pe", "act", "dve", "pool", "sp"):
                P.wait(e, s_h)
                P.wait(e, self.sig_x)
            self.barrier()
            self.wb_i = 0
            done = self.s5_mixer(1, [s_h, d1])
            hs = self.norm_all(1, 2, waits=done)
            done = self.ffn(1, 1, 2, hs)
            self.barrier()
            self.final(done)
            P.emit()
            return self.nc
        gen = None
        if cfg.do_s5:
            gen = self.s5_gen()
            next(gen)
        self.mod_init(3)
        sm = None
        for cbi in range(36):
            b = cbi % 3
            dst = self.wbf[:, b, 0:2048].rearrange("p (k n) -> p k n", k=8)
            sm = self.mod_block(0, cbi * 256, 256, dst, b, "act", self.ps[7][0:1, 0:256],
                                self.ps[6][:, 2 * cbi:2 * cbi + 2], defer=True)
            self.buf_free[b] = self.mbuf_free[b]
        sm = P.op("dve", lambda e: e.tensor_tensor(out=self.modT[:, 0, :], in0=self.ps[6][:, 0:72],
                                                   in1=self.vecs[:, V_MB:V_MB + 72], op=ALU.add),
                  waits=[sm, self.sig_c], writes=["modT"])
        self.modT_ps_free = sm
        d0 = self.mod_derive(0, sm)
        if cfg.do_s5:
            for _ in gen:
                pass
            self.barrier()
            self.load_x()
        self.barrier()
        self.wb_i = 0
        hs = self.norm_all(0, 0, waits=[self.sig_x, d0], halo=True)
        done = self.ffn(0, 0, 0, hs, halo=True)
        hs = self.norm_all(0, 1, waits=done, halo=True)
        self.barrier()
        done = self.pool_mixer(0, hs)
        self.barrier()
        hs = self.norm_all(0, 2, waits=done)
        self.mod_init(2)
        st = {"cb": 0, "sm": None}

        def mod1():
            cbi = st["cb"]
            b = cbi % 2
            dst = self.Rb(5120 + b * 1024, 8 * 256).rearrange("p (k n) -> p k n", k=8)
            st["sm"] = self.mod_block(1, cbi * 256, 256, dst, b, "pool", self.ps[7][0:1, 0:256], self.ps[6][:, 0:2])
            st["cb"] += 1

        def extra(q):
            for _ in range(6 if q < 4 else 12):
                if st["cb"] < 36:
                    mod1()

        done = self.ffn(0, 1, 2, hs, extra=extra)
        while st["cb"] < 36:
            mod1()
        d1 = self.mod_derive(1, st["sm"])
        hs = self.norm_all(1, 0, waits=list(done) + [d1])
        done = self.ffn(1, 0, 0, hs)
        if cfg.do_s5 and not (cfg.dbg or "").startswith("g"):
            hs = self.norm_all(1, 1, waits=done)
            self.barrier()
            done = self.s5_mixer(1, hs)
            if cfg.mode == "A":
                P.emit()
                return self.nc
            self.barrier()
        hs = self.norm_all(1, 2, waits=done)
        done = self.ffn(1, 1, 2, hs)
        self.barrier()
        self.final(done)
        P.emit()
        return self.nc


def build_nc(cfg):
    k = K(cfg)
    return k.build()


def kernel(**inputs):
    cfg = Cfg(mode="fused")
    maps = prep_inputs(cfg, inputs)
    nc = build_nc(cfg)
    res = run_bass_kernel_spmd(nc, maps, core_ids=list(range(cfg.NCORES)))
    out = np.zeros((cfg.NCORES // cfg.CPS, cfg.S, D), np.float32)
    for k in range(cfg.NCORES):
        b, pos = k // cfg.CPS, k % cfg.CPS
        out[b, pos * cfg.T:(pos + 1) * cfg.T, :] = np.asarray(res.results[k]["outT"], np.float32).T
    return out


def kernel_two_launch(**inputs):
    cfgA = Cfg(mode="A")
    cfgB = Cfg(mode="B")
    maps = prep_inputs(cfgA, inputs)
    ncA = build_nc(cfgA)
    resA = run_bass_kernel_spmd(ncA, maps, core_ids=list(range(cfgA.NCORES)))
    mapsB = []
    for k in range(cfgB.NCORES):
        b = k // cfgB.CPS
        m = dict(maps[k])
        m["xT"] = np.ascontiguousarray(resA.results[k]["xTo"])
        m["hTi"] = np.ascontiguousarray(resA.results[k]["hTo"])
        m["Eg"] = np.ascontiguousarray(np.concatenate(
            [np.asarray(resA.results[b * cfgB.CPS + j]["Eo"], np.float32) for j in range(cfgB.CPS)], axis=1))
        mapsB.append(m)
    ncB = build_nc(cfgB)
    resB = run_bass_kernel_spmd(ncB, mapsB, core_ids=list(range(cfgB.NCORES)))
    out = np.zeros((cfgB.NCORES // cfgB.CPS, cfgB.S, D), np.float32)
    for k in range(cfgB.NCORES):
        b, pos = k // cfgB.CPS, k % cfgB.CPS
        out[b, pos * cfgB.T:(pos + 1) * cfgB.T, :] = np.asarray(resB.results[k]["outT"], np.float32).T
    return out


def bc_last(ap, n):
    return AP(tensor=ap.tensor, offset=ap.offset, ap=[list(x) for x in ap.ap] + [[0, n]])


def bc_mid(ap, n):
    a = [list(x) for x in ap.ap]
    return AP(tensor=ap.tensor, offset=ap.offset, ap=[a[0], [0, n]] + a[1:])


def _s5_gen(self):
    P, cfg = self.P, self.cfg
    NCH = cfg.NCH
    xs = self.xT[:].rearrange("p k t -> p (k t)")
    ar_ = self.AR

    def xt(off, n):
        return xs[:, off:off + n]

    PWR = xt(0, 1088).rearrange("p (k c) -> p k c", k=17)
    PWI = xt(1088, 1088).rearrange("p (k c) -> p k c", k=17)
    IPR = xt(2176, 512).rearrange("p (k c) -> p k c", k=8)
    IPI = xt(2688, 512).rearrange("p (k c) -> p k c", k=8)
    m = [xt(3200 + i * 64, 64) for i in range(14)]
    negpi = self.pv[:, 0, 0:1]
    raw = self.wbf[:, 4:6, :].rearrange("p a n -> p (a n)").bitcast(F32)
    bre = raw[:, 0:1024].rearrange("p (c h) -> p c h", h=16)
    bim = raw[:, 1024:2048].rearrange("p (c h) -> p c h", h=16)
    cre = raw[:, 2048:3072].rearrange("p (c h) -> p c h", h=16)
    cim = raw[:, 3072:4096].rearrange("p (c h) -> p c h", h=16)
    bb = self.wbf[:, 3, :].bitcast(F32)
    Bbr = bb[:, 0:1024].rearrange("p (c h) -> p c h", h=16)
    Bbi = bb[:, 1024:2048].rearrange("p (c h) -> p c h", h=16)

    s_p = self.sig_sp
    s_bc = self.sig_sbc
    s_np = P.op("dve", lambda e: e.memset(negpi, -math.pi), writes=["pv"])

    st = {"a": None, "d": s_np}

    def A(out, in_, func, bias=None):
        kw = {}
        if bias is not None:
            kw["bias"] = bias
        st["a"] = P.op("act", lambda e: e.activation(out=out, in_=in_, func=func, **kw),
                       waits=[st["d"], s_p], reads=["g"], writes=["g"])

    def Dtt(out, a, b, op, eng="dve"):
        st["d"] = P.op(eng, lambda e: e.tensor_tensor(out=out, in0=a, in1=b, op=op),
                       waits=[st["a"], s_p, s_bc, st["d"]], reads=["g"], writes=["g"])

    def Dts(out, a, s1, op0, s2=None, op1=None, eng="dve"):
        if op1 is None:
            st["d"] = P.op(eng, lambda e: e.tensor_scalar(out=out, in0=a, scalar1=s1, scalar2=None, op0=op0),
                           waits=[st["a"], s_p, st["d"]], reads=["g"], writes=["g"])
        else:
            st["d"] = P.op(eng, lambda e: e.tensor_scalar(out=out, in0=a, scalar1=s1, scalar2=s2, op0=op0, op1=op1),
                           waits=[st["a"], s_p, st["d"]], reads=["g"], writes=["g"])

    def Drec(out, a):
        st["d"] = P.op("dve", lambda e: e.reciprocal(out=out, in_=a), waits=[st["a"], st["d"]], reads=["g"],
                       writes=["g"])

    MUL, ADD, SUB = ALU.mult, ALU.add, ALU.subtract
    A(m[2], m[2], AF.Exp)
    Dtt(m[3], m[0], m[2], MUL)
    Dtt(m[4], m[1], m[2], MUL)
    A(m[3], m[3], AF.Exp)
    I32 = mybir.dt.int32

    def reduce_pi(dst, src, shift):
        Dts(m[9], src, shift, ADD)
        Dts(m[10], m[9], 1.0 / (2 * math.pi), MUL)
        st["d"] = P.op("dve", lambda e: e.tensor_copy(out=m[11].bitcast(I32), in_=m[10]), waits=[st["d"]],
                       reads=["g"], writes=["g"])
        st["d"] = P.op("dve", lambda e: e.tensor_copy(out=m[10], in_=m[11].bitcast(I32)), reads=["g"], writes=["g"])
        st["d"] = P.op("dve", lambda e: e.scalar_tensor_tensor(out=dst, in0=m[10], scalar=-2 * math.pi, in1=m[9],
                                                              op0=MUL, op1=ADD), reads=["g"], writes=["g"])
        Dts(m[10], dst, math.pi, ALU.is_gt, 2 * math.pi, MUL)
        Dtt(dst, dst, m[10], SUB)
        Dts(m[10], dst, -math.pi, ALU.is_lt, 2 * math.pi, MUL)
        Dtt(dst, dst, m[10], ADD)

    reduce_pi(m[5], m[4], 0.0)
    reduce_pi(m[6], m[4], 0.5 * math.pi)
    A(m[5], m[5], AF.Sin)
    A(m[6], m[6], AF.Sin)
    Dtt(m[7], m[3], m[6], MUL)
    Dtt(m[8], m[3], m[5], MUL)
    Dtt(m[9], m[0], m[0], MUL)
    Dtt(m[10], m[1], m[1], MUL)
    Dtt(m[9], m[9], m[10], ADD)
    Drec(m[11], m[9])
    Dts(m[9], m[7], -1.0, ADD)
    Dtt(m[10], m[9], m[0], MUL)
    Dtt(m[12], m[8], m[1], MUL)
    Dtt(m[12], m[10], m[12], ADD)
    Dtt(m[12], m[12], m[11], MUL)
    Dtt(m[10], m[8], m[0], MUL)
    Dtt(m[13], m[9], m[1], MUL)
    Dtt(m[13], m[10], m[13], SUB)
    Dtt(m[13], m[13], m[11], MUL)
    tg = [ar_[:, 12288 + i * 1024:12288 + (i + 1) * 1024].rearrange("p (c h) -> p c h", h=16) for i in range(4)]
    frb, fib = bc_last(m[12], 16), bc_last(m[13], 16)
    Dtt(tg[0], bre, frb, MUL)
    Dtt(tg[1], bim, fib, MUL)
    Dtt(Bbr, tg[0], tg[1], SUB)
    Dtt(tg[2], bim, frb, MUL)
    Dtt(tg[3], bre, fib, MUL)
    Dtt(Bbi, tg[2], tg[3], ADD)
    P.op("dve", lambda e: e.memset(PWR[:, 0, :], 1.0), reads=["g"], writes=["g"])
    P.op("dve", lambda e: e.memset(PWI[:, 0, :], 0.0), reads=["g"], writes=["g"])
    P.op("dve", lambda e: e.memset(IPR[:, 0, :], 1.0), reads=["g"], writes=["g"])
    P.op("dve", lambda e: e.memset(IPI[:, 0, :], 0.0), reads=["g"], writes=["g"])
    P.op("dve", lambda e: e.tensor_copy(out=PWR[:, 1, :], in_=m[7]), reads=["g"], writes=["g"])
    P.op("dve", lambda e: e.tensor_copy(out=PWI[:, 1, :], in_=m[8]), reads=["g"], writes=["g"])

    def cmul(orr, oii, xr, xi, yr, yi, t1, t2):
        Dtt(t1, xr, yr, MUL)
        Dtt(t2, xi, yi, MUL)
        Dtt(orr, t1, t2, SUB)
        Dtt(t1, xr, yi, MUL)
        Dtt(t2, xi, yr, MUL)
        Dtt(oii, t1, t2, ADD)

    for k in range(2, 17):
        cmul(PWR[:, k, :], PWI[:, k, :], PWR[:, k - 1, :], PWI[:, k - 1, :], m[7], m[8], m[9], m[10])
    Dtt(m[9], m[3], m[3], MUL)
    Drec(m[9], m[9])
    Dtt(IPR[:, 1, :], m[7], m[9], MUL)
    Dtt(m[5], m[8], m[9], MUL)
    Dts(IPI[:, 1, :], m[5], -1.0, MUL)
    for k in range(2, 8):
        cmul(IPR[:, k, :], IPI[:, k, :], IPR[:, k - 1, :], IPI[:, k - 1, :], IPR[:, 1, :], IPI[:, 1, :], m[9], m[10])
    AA, BB, ZA, ZB = (self.scan[:, i, :] for i in range(4))
    P.op("dve", lambda e: e.tensor_copy(out=AA[:, 0:64], in_=PWR[:, 16, :]), reads=["g"], writes=["scan"])
    P.op("dve", lambda e: e.tensor_copy(out=AA[:, 64:128], in_=PWR[:, 16, :]), reads=["g"], writes=["scan"])
    P.op("dve", lambda e: e.tensor_scalar(out=BB[:, 0:64], in0=PWI[:, 16, :], scalar1=-1.0, scalar2=None,
                                          op0=MUL), reads=["g"], writes=["scan"])
    P.op("dve", lambda e: e.tensor_copy(out=BB[:, 64:128], in_=PWI[:, 16, :]), reads=["g"], writes=["scan"])
    Dtt(m[0], PWR[:, 16, :], PWR[:, 0, :], MUL)
    Dtt(m[1], PWI[:, 16, :], PWR[:, 0, :], MUL)
    nsq = int(round(math.log2(NCH)))
    assert 2 ** nsq == NCH
    for _ in range(nsq):
        Dtt(m[9], m[0], m[1], MUL)
        Dtt(m[10], m[0], m[0], MUL)
        Dtt(m[6], m[1], m[1], MUL)
        Dtt(m[0], m[10], m[6], SUB)
        Dts(m[1], m[9], 2.0, MUL)
    P.op("dve", lambda e: e.tensor_copy(out=ZA[:, 0:64], in_=m[0]), reads=["g"], writes=["scan"])
    P.op("dve", lambda e: e.tensor_copy(out=ZA[:, 64:128], in_=m[0]), reads=["g"], writes=["scan"])
    P.op("dve", lambda e: e.tensor_scalar(out=ZB[:, 0:64], in0=m[1], scalar1=-1.0, scalar2=None, op0=MUL),
         reads=["g"], writes=["scan"])
    s_small = P.op("dve", lambda e: e.tensor_copy(out=ZB[:, 64:128], in_=m[1]), reads=["g"], writes=["scan"])

    yield "small"
    if cfg.dbg == "g1":
        for e in ("pe", "act", "dve", "pool", "sp"):
            P.wait(e, s_small)
        return
    bufA = ar_[:, 0:4096].bitcast(BF16)
    bufC = ar_[:, 4096:8192].bitcast(BF16)
    bufD = ar_[:, 8192:12288].bitcast(BF16)
    stg = [ar_[:, 16384 + i * 256:16384 + (i + 1) * 256].bitcast(BF16).rearrange("p (a n) -> p a n", a=4)
           for i in range(2)]

    def bview(buf):
        return buf.rearrange("p (r g s h) -> p r g s h", r=2, g=32, s=8)

    def gen_tab(name, buf, dirn, Xr, Xi, PR, PI, kfun, neg_im, waits):
        bv = bview(buf)
        sigs = []
        for s_lo in range(8):
            eng = "dve"
            tb = 12288 + (0 if eng == "dve" else 2048)
            T = [ar_[:, tb + i * 512:tb + (i + 1) * 512].rearrange("p (c h) -> p c h", h=16) for i in range(4)]
            k = kfun(s_lo)
            pr = bc_last(PR[:, k, dirn * 32:(dirn + 1) * 32], 16)
            pi = bc_last(PI[:, k, dirn * 32:(dirn + 1) * 32], 16)
            xr = Xr[:, dirn * 32:(dirn + 1) * 32, :]
            xi = Xi[:, dirn * 32:(dirn + 1) * 32, :]
            tag = "gt_" + eng
            w = list(waits) + [s_small, st["d"], s_bc]
            P.op(eng, lambda e, T=T, xr=xr, pr=pr: e.tensor_tensor(out=T[0], in0=xr, in1=pr, op=MUL), waits=w,
                 reads=[tag], writes=[tag])
            P.op(eng, lambda e, T=T, xi=xi, pi=pi: e.tensor_tensor(out=T[1], in0=xi, in1=pi, op=MUL),
                 reads=[tag], writes=[tag])
            P.op(eng, lambda e, T=T, bv=bv, s_lo=s_lo: e.tensor_tensor(out=bv[:, 0, :, s_lo, :], in0=T[0], in1=T[1],
                                                                       op=SUB), reads=[tag], writes=[tag, name])
            P.op(eng, lambda e, T=T, xi=xi, pr=pr: e.tensor_tensor(out=T[2], in0=xi, in1=pr, op=MUL),
                 reads=[tag], writes=[tag])
            P.op(eng, lambda e, T=T, xr=xr, pi=pi: e.tensor_tensor(out=T[3], in0=xr, in1=pi, op=MUL),
                 reads=[tag], writes=[tag])
            if neg_im and eng == "dve":
                sg = P.op(eng, lambda e, T=T, bv=bv, s_lo=s_lo: e.scalar_tensor_tensor(
                    out=bv[:, 1, :, s_lo, :], in0=T[2], scalar=-1.0, in1=T[3], op0=MUL, op1=SUB),
                    reads=[tag], writes=[tag, name])
            elif neg_im:
                P.op(eng, lambda e, T=T: e.tensor_scalar(out=T[2], in0=T[2], scalar1=-1.0, scalar2=None, op0=MUL),
                     reads=[tag], writes=[tag])
                sg = P.op(eng, lambda e, T=T, bv=bv, s_lo=s_lo: e.tensor_tensor(
                    out=bv[:, 1, :, s_lo, :], in0=T[2], in1=T[3], op=SUB), reads=[tag], writes=[tag, name])
            else:
                sg = P.op(eng, lambda e, T=T, bv=bv, s_lo=s_lo: e.tensor_tensor(
                    out=bv[:, 1, :, s_lo, :], in0=T[2], in1=T[3], op=ADD), reads=[tag], writes=[tag, name])
            sigs.append(sg)
        return sigs[-1:]

    evi = [0]
    stg_free = [None, None]
    psb_free = [None, None]

    def t_blocks(dirn, bufL, bufR, blk, mask, w_in):
        L, Rr = bview(bufL), bview(bufR)
        last_pe = None
        dT2 = self.d_T.rearrange("(q t) p n -> t q p n", t=2)
        for g0 in range(0, 64, 4):
            i = evi[0] % 2
            evi[0] += 1
            pst = self.ps[i][:, :].rearrange("p (a n) -> p a n", a=4)
            sig = None
            g2 = g0 // 32
            gq0 = (g0 % 32)
            for gg in range(4):
                gq = gq0 + gg
                for ri in range(2):
                    sig = P.op("pe", lambda e, gg=gg, gq=gq, g2=g2, ri=ri, pst=pst, L=L, Rr=Rr: e.matmul(
                        pst[:, gg, :], lhsT=L[64 * g2:64 * g2 + 64, ri, gq, :, :].rearrange("p s h -> p (s h)"),
                        rhs=Rr[64 * g2:64 * g2 + 64, ri, gq, :, :].rearrange("p s h -> p (s h)"),
                        start=(ri == 0), stop=(ri == 1)),
                        waits=list(w_in) + [psb_free[i]] if (gg == 0 and ri == 0) else (),
                        mark=(gg == 3 and ri == 1))
            last_pe = sig
            sb_ = stg[i]
            if mask is not None:
                s_e = P.op("dve", lambda e, pst=pst, sb_=sb_: e.tensor_tensor(out=sb_, in0=pst, in1=bc_mid(mask, 4),
                                                                             op=MUL),
                           waits=[sig, stg_free[i], self.sig_c], writes=["stg%d" % i])
            else:
                s_e = P.op("act", lambda e, pst=pst, sb_=sb_: e.activation(out=sb_, in_=pst, func=AF.Copy),
                           waits=[sig, stg_free[i]], writes=["stg%d" % i])
            psb_free[i] = s_e
            dst = dT2[g2, gq0:gq0 + 4, :, blk * 128:(blk + 1) * 128].rearrange("g p n -> p g n")
            stg_free[i] = P.op("sp", lambda e, dst=dst, sb_=sb_: e.dma_start(out=dst, in_=sb_), waits=[s_e],
                               dma="tbl%d" % i)
        return last_pe

    def ba_out(dirn, s_hi, buf, w_in):
        bv = bview(buf)
        last_pe = None
        for ri in range(2):
            for q0 in range(0, 32, 4):
                i = evi[0] % 2
                evi[0] += 1
                pst = self.ps[i][:, 0:256].bitcast(BF16).rearrange("p (a n) -> p a n", a=4)
                sig = None
                for qq in range(4):
                    sig = P.op("pe", lambda e, qq=qq, q0=q0, ri=ri, pst=pst, bv=bv: e.transpose(
                        pst[:, qq, :], bv[:, ri, q0 + qq, :, :].rearrange("p s h -> p (s h)"), self.cI),
                        waits=list(w_in) + [psb_free[i], self.sig_c] if qq == 0 else (), mark=(qq == 3))
                last_pe = sig
                sb_ = stg[i]
                s_e = P.op("act", lambda e, pst=pst, sb_=sb_: e.activation(out=sb_, in_=pst, func=AF.Copy),
                           waits=[sig, stg_free[i]], writes=["stg%d" % i])
                psb_free[i] = s_e
                col = ((dirn * 2 + s_hi) * 2 + ri) * 128
                dst = self.d_BA[q0:q0 + 4, :, col:col + 128].rearrange("g p n -> p g n")
                stg_free[i] = P.op("sp", lambda e, dst=dst, sb_=sb_: e.dma_start(out=dst, in_=sb_), waits=[s_e],
                                   dma="tbl%d" % i)
        return last_pe

    def ca_out(dirn, mt, buf, w_in):
        bv = bview(buf)
        sg = None
        for ri in range(2):
            col = ((dirn * 2 + mt) * 2 + ri) * 128
            for q0 in range(0, 32, 4):
                dst = self.d_CA[q0:q0 + 4, :, col:col + 128].rearrange("g p n -> p g n")
                sg = P.op("sp", lambda e, dst=dst, ri=ri, bv=bv, q0=q0: e.dma_start(
                    out=dst, in_=bv[:, ri, q0:q0 + 4, :, :].rearrange("p g s h -> p g (s h)")), waits=list(w_in),
                    dma="tblc")
        return sg

    free = {"A": [], "C": [], "D": []}
    for dirn in range(2):
        hA = 1 if dirn == 0 else 0
        hD = 0 if dirn == 0 else 1
        if dirn == 0:
            kBA = lambda s_hi: (lambda s_lo: 15 - (8 * s_hi + s_lo))
            kCA = lambda mt: (lambda t_lo: 8 * mt + t_lo + 1)
            kC0 = lambda t_lo: 7 - t_lo
            mask = self.cMLE
        else:
            kBA = lambda s_hi: (lambda s_lo: 8 * s_hi + s_lo)
            kCA = lambda mt: (lambda t_lo: 16 - (8 * mt + t_lo))
            kC0 = lambda t_lo: t_lo
            mask = self.cMGE
        sA = gen_tab("bA", bufA, dirn, Bbr, Bbi, PWR, PWI, kBA(hA), False, free["A"])
        sC = gen_tab("bC", bufC, dirn, cre, cim, IPR, IPI, kC0, True, free["C"])
        sD = gen_tab("bD", bufD, dirn, cre, cim, PWR, PWI, kCA(hD), True, free["D"])
        if cfg.dbg == "g2":
            for e in ("pe", "act", "dve", "pool", "sp"):
                for sg_ in sA + sC + sD:
                    P.wait(e, sg_)
            return
        pe1 = t_blocks(dirn, bufA, bufC, 0 + dirn, mask, sA + sC)
        pe2 = t_blocks(dirn, bufA, bufD, 2 + dirn, None, sA + sD)
        if cfg.dbg == "g3":
            for e in ("pe", "act", "dve", "pool", "sp"):
                for sg_ in [pe1, pe2, stg_free[0], stg_free[1]]:
                    P.wait(e, sg_)
            return
        pe3 = ba_out(dirn, hA, bufA, sA)
        if cfg.dbg == "g4":
            for e in ("pe", "act", "dve", "pool", "sp"):
                for sg_ in [pe3, stg_free[0], stg_free[1]]:
                    P.wait(e, sg_)
            return
        d1 = ca_out(dirn, hD, bufD, sD)
        sC2 = gen_tab("bC", bufC, dirn, Bbr, Bbi, PWR, PWI, kBA(1 - hA), False, [pe1])
        pe4 = ba_out(dirn, 1 - hA, bufC, sC2)
        sA2 = gen_tab("bA", bufA, dirn, cre, cim, PWR, PWI, kCA(1 - hD), True, [pe3, pe2])
        d2 = ca_out(dirn, 1 - hD, bufA, sA2)
        free["A"] = [d2]
        free["C"] = [pe4]
        free["D"] = [d1, pe2]
    fin = [stg_free[0], stg_free[1], free["A"][0], free["D"][0]]
    for e in ("pe", "act", "dve", "pool", "sp"):
        for s in fin:
            P.wait(e, s)


K.s5_gen = _s5_gen


def _s5_views(self):
    cfg = self.cfg
    NCH, T = cfg.NCH, cfg.T
    v = {}
    v["SX"] = self.Rb(0, (NCH + 1) * 128).rearrange("p (j c) -> p j c", c=128)
    o = (NCH + 1) * 64
    v["Q"] = [self.R(o + i * 128, 128) for i in range(2)]
    v["t"] = [self.R(o + 256 + i * 128, 128) for i in range(3)]
    v["W"] = self.wbf[:, 0:4, :].rearrange("p a n -> p (a n)")[:, 0:64 * 2 * NCH].rearrange(
        "p (g k c) -> p g k c", g=64, k=2)
    misc = self.wbf[:, 4:6, :].rearrange("p a n -> p (a n)")
    v["ring"] = [misc[:, i * 1024:(i + 1) * 1024].rearrange("p (a n) -> p a n", a=8) for i in range(2)]
    v["Tr"] = [misc[:, 2048 + i * 1024:2048 + (i + 1) * 1024].rearrange("p (g b n) -> p g b n", g=2, b=4)
               for i in range(2)]
    v["YW"] = misc[:, 4096:4096 + 16 * NCH].rearrange("p (g m c) -> p g m c", g=8, m=2)
    mf = misc[:, 6144:8192].bitcast(F32)
    v["Eg"] = mf[:, 0:512].rearrange("p (r c) -> p r c", c=128)
    v["F"] = [mf[:, 512 + i * 128:512 + (i + 1) * 128] for i in range(3)]
    ws = self.wstage[:].rearrange("p a n -> p (a n)")
    v["ybuf"] = ws[:, 0:T]
    v["gt"] = [ws[:, 2048 + i * 512:2048 + (i + 1) * 512] for i in range(2)]
    return v


def swap_ri(ap):
    a = [list(x) for x in ap.ap]
    assert a[1] == [1, 128], a
    return AP(tensor=ap.tensor, offset=ap.offset + 64, ap=[a[0], [-64, 2], [1, 64]])


def as3(ap):
    a = [list(x) for x in ap.ap]
    return AP(tensor=ap.tensor, offset=ap.offset, ap=[a[0], [64, 2], [1, 64]])


def _cols(ap, d):
    a = [list(x) for x in ap.ap]
    return AP(tensor=ap.tensor, offset=ap.offset + d * 32, ap=[a[0], [64, 2], [1, 32]])


def _s5_p1(self, h_sigs):
    P, cfg = self.P, self.cfg
    NCH = cfg.NCH
    v = self.s5v
    SX, W = v["SX"], v["W"]
    s0 = P.op("dve", lambda e: e.memset(SX[:, 0, :], 0.0), writes=["SX0"])
    ring_free = [None, None]
    psw_free = [None, None]
    pss_free = [None, None]
    last = s0
    hw = list(h_sigs)
    for gq in range(32):
        r = gq % 2
        kt, gi0 = gq // 4, 2 * (gq % 4)
        BAr = v["ring"][r]
        s_ba = P.op("sp", lambda e, gq=gq, BAr=BAr: e.dma_start(out=BAr.rearrange("p a n -> p (a n)"),
                                                               in_=self.d_BA[gq]), waits=[ring_free[r]],
                    dma="ba%d" % r)
        psW = self.ps[r][:, 0:4 * NCH].rearrange("p (g k c) -> p g k c", g=2, k=2)
        sig = None
        hrow = self.hT[:, kt, :]
        pstr = list(hrow.ap[0])
        for g2 in range(2):
            gi = gi0 + g2
            for ks in range(2):
                for s_lo in range(8):
                    rhs = AP(tensor=hrow.tensor, offset=hrow.offset + 8 + 8 * ks + s_lo, ap=[pstr, [16, NCH]])
                    first = (g2 == 0 and ks == 0 and s_lo == 0)
                    lastm = (g2 == 1 and ks == 1 and s_lo == 7)
                    sig = P.op("pe", lambda e, g2=g2, ks=ks, s_lo=s_lo, gi=gi, rhs=rhs, psW=psW: e.matmul(
                        psW[:, g2, ks, :], lhsT=self.cE[:, gi, 7 - s_lo:15 - s_lo, :].rearrange("p u h -> p (u h)"),
                        rhs=rhs, start=(s_lo == 0), stop=(s_lo == 7)),
                        waits=hw + [psw_free[r], self.sig_c] if first else (), mark=lastm)
        s_w = P.op("act", lambda e, gq=gq, psW=psW: e.activation(out=W[:, 2 * gq:2 * gq + 2, :, :], in_=psW,
                                                                func=AF.Copy), waits=[sig], writes=["W"])
        psw_free[r] = s_w
        psS = self.ps[2 + r][:, 0:4 * NCH].rearrange("p (a c) -> p a c", a=4)
        sig = None
        for dirn in range(2):
            for ri in range(2):
                for g2 in range(2):
                    g = 2 * gq + g2
                    for ks in range(2):
                        first = (dirn == 0 and ri == 0 and g2 == 0 and ks == 0)
                        lastm = (dirn == 1 and ri == 1 and g2 == 1 and ks == 1)
                        sig = P.op("pe", lambda e, dirn=dirn, ri=ri, g2=g2, g=g, ks=ks, psS=psS, BAr=BAr: e.matmul(
                            psS[64 * g2:64 * g2 + 64, dirn * 2 + ri, :],
                            lhsT=BAr[:, (dirn * 2 + ks) * 2 + ri, 64 * g2:64 * g2 + 64], rhs=W[:, g, ks, :],
                            start=(ks == 0), stop=(ks == 1)),
                            waits=[s_w, s_ba, pss_free[r]] if first else (), mark=lastm)
        ring_free[r] = sig
        for dirn in range(2):
            col = dirn * 32 + gq
            if dirn == 0:
                o = AP(tensor=SX.tensor, offset=SX.offset + 128 + col, ap=[list(SX.ap[0]), [64, 2], [128, NCH]])
            else:
                o = AP(tensor=SX.tensor, offset=SX.offset + NCH * 128 + col, ap=[list(SX.ap[0]), [64, 2], [-128, NCH]])
            last = P.op("dve", lambda e, o=o, dirn=dirn, psS=psS: e.tensor_copy(out=o, in_=psS[:, 2 * dirn:2 * dirn + 2, :]),
                        waits=[sig], writes=["SXs"])
        pss_free[r] = last
    return last, ring_free


def _s5_scan(self, init, w_in):
    P, cfg = self.P, self.cfg
    NCH = cfg.NCH
    v = self.s5v
    SX = v["SX"]
    Q, t = v["Q"], v["t"]
    AA, BB = self.scan[:, 0, :], self.scan[:, 1, :]
    MUL, ADD = ALU.mult, ALU.add
    if init is None:
        s = P.op("dve", lambda e: e.memset(Q[0], 0.0), waits=w_in, writes=["Q0"])
    else:
        s = P.op("dve", lambda e: e.tensor_copy(out=Q[0], in_=init), waits=w_in, writes=["Q0"])
        P.op("dve", lambda e: e.tensor_copy(out=SX[:, 0, :], in_=init), writes=["SX0"])
    sx_w = None
    for j in range(1, NCH + 1):
        qp, qn = Q[(j - 1) % 2], Q[j % 2]
        tp, tn = "Q%d" % ((j - 1) % 2), "Q%d" % (j % 2)
        P.op("dve", lambda e, qp=qp: e.tensor_tensor(out=t[0], in0=qp, in1=AA, op=MUL), waits=w_in if j == 1 else (),
             reads=[tp], writes=["t0"])
        P.op("dve", lambda e, qp=qp: e.tensor_tensor(out=as3(t[1]), in0=swap_ri(qp), in1=as3(BB), op=MUL),
             reads=[tp], writes=["t1"])
        P.op("dve", lambda e: e.tensor_tensor(out=t[2], in0=t[0], in1=t[1], op=ADD), reads=["t0", "t1"], writes=["t2"])
        s = P.op("dve", lambda e, qn=qn, j=j: e.tensor_tensor(out=qn, in0=t[2], in1=SX[:, j, :], op=ADD),
                 waits=[sx_w] if sx_w is not None else (), reads=["t2", "SXs"], writes=[tn])
        sx_w = P.op("act", lambda e, qn=qn, j=j: e.activation(out=SX[:, j, :], in_=qn, func=AF.Copy), waits=[s],
                    writes=["SXa"])
    return Q[NCH % 2], s, sx_w


def _s5_p3(self, L, w_in):
    P, cfg = self.P, self.cfg
    NCH, T = cfg.NCH, cfg.T
    v = self.s5v
    SX, W, YW = v["SX"], v["W"], v["YW"]
    ybuf, gt = v["ybuf"], v["gt"]
    ring_free = [None, None]
    psy_free = [None, None]
    psi_free = [None, None]
    yw_free = None
    ii = 0
    last_g = None
    gelu_free = [None, None]
    for gq in range(32):
        r = gq % 2
        kt, gi0 = gq // 4, 2 * (gq % 4)
        CAr, Tr = v["ring"][r], v["Tr"][r]
        s_ca = P.op("sp", lambda e, gq=gq, CAr=CAr: e.dma_start(out=CAr.rearrange("p a n -> p (a n)"),
                                                               in_=self.d_CA[gq]), waits=[ring_free[r]] + list(w_in),
                    dma="ca%d" % r)
        s_t = P.op("sp", lambda e, gq=gq, Tr=Tr: e.dma_start(
            out=Tr.rearrange("p g b n -> p g (b n)"),
            in_=self.d_T[2 * gq:2 * gq + 2, :, :].rearrange("g p n -> p g n")), dma="tt%d" % r)
        psY = self.ps[r][:, 0:4 * NCH].rearrange("p (g m c) -> p g m c", g=2, m=2)
        sig = None
        for g2 in range(2):
            g = 2 * gq + g2
            for mt in range(2):
                ops = [(Tr[:, g2, 0, :], W[:, g, mt, :]), (Tr[:, g2, 1, :], W[:, g, mt, :])]
                if mt == 1:
                    ops.append((Tr[:, g2, 2, :], W[:, g, 0, :]))
                else:
                    ops.append((Tr[:, g2, 3, :], W[:, g, 1, :]))
                for dirn in range(2):
                    for ri in range(2):
                        col = ri * 64 + dirn * 32 + gq
                        pst = list(SX.ap[0])
                        base = SX.offset + 64 * g2 * pst[0]
                        if dirn == 0:
                            rhs = AP(tensor=SX.tensor, offset=base + col, ap=[[pst[0], 64], [128, NCH]])
                        else:
                            rhs = AP(tensor=SX.tensor, offset=base + (NCH - 1) * 128 + col,
                                     ap=[[pst[0], 64], [-128, NCH]])
                        ops.append((CAr[64 * g2:64 * g2 + 64, (dirn * 2 + mt) * 2 + ri, :], rhs))
                for oi, (lh, rh) in enumerate(ops):
                    first = (g2 == 0 and mt == 0 and oi == 0)
                    lastm = (g2 == 1 and mt == 1 and oi == len(ops) - 1)
                    sig = P.op("pe", lambda e, g2=g2, mt=mt, lh=lh, rh=rh, oi=oi, n=len(ops), psY=psY: e.matmul(
                        psY[:, g2, mt, :], lhsT=lh, rhs=rh, start=(oi == 0), stop=(oi == n - 1)),
                        waits=[s_ca, s_t, psy_free[r]] + list(w_in) if first else (), mark=lastm)
        ring_free[r] = sig
        s_y = P.op("act", lambda e, gi0=gi0, psY=psY: e.activation(out=YW[:, gi0:gi0 + 2, :, :], in_=psY, func=AF.Copy),
                   waits=[sig, yw_free] if gq % 4 == 0 else [sig], writes=["YW"])
        psy_free[r] = s_y
        if gq % 4 != 3:
            continue
        sig = None
        hrow = self.hT[:, kt, :]
        pstr = list(hrow.ap[0])
        s_e = None
        for sgrp in range(4):
            pi = ii % 2
            ii += 1
            psI = self.ps[2 + pi][:, 0:4 * NCH].rearrange("p (s c) -> p s c", s=4)
            for si in range(4):
                s = 4 * sgrp + si
                mt, s_lo = s // 8, s % 8
                for gi in range(8):
                    sig = P.op("pe", lambda e, si=si, mt=mt, s_lo=s_lo, gi=gi, psI=psI: e.matmul(
                        psI[:, si, :], lhsT=self.cF[:, s_lo, 7 - gi:15 - gi, :].rearrange("p u h -> p (u h)"),
                        rhs=YW[:, gi, mt, :], start=(gi == 0), stop=(gi == 7)),
                        waits=[s_y, psi_free[pi]] if (si == 0 and gi == 0) else (), mark=(si == 3 and gi == 7))
            yo = AP(tensor=ybuf.tensor, offset=ybuf.offset + 4 * sgrp, ap=[list(ybuf.ap[0]), [1, 4], [16, NCH]])
            hi = AP(tensor=hrow.tensor, offset=hrow.offset + 8 + 4 * sgrp, ap=[pstr, [1, 4], [16, NCH]])
            s_e = P.op("dve", lambda e, yo=yo, hi=hi, psI=psI, kt=kt: e.scalar_tensor_tensor(
                out=yo, in0=hi, scalar=self.vecs[:, V_SD + kt:V_SD + kt + 1], in1=psI, op0=ALU.mult, op1=ALU.add),
                waits=[sig, last_g], writes=["ybuf"])
            psi_free[pi] = s_e
        yw_free = sig
        for c0 in range(0, T, 512):
            b = (c0 // 512) % 2
            yv = ybuf[:, c0:c0 + 512]
            s1 = P.op("act", lambda e, yv=yv, b=b: e.activation(out=gt[b], in_=yv, func=AF.Square),
                      waits=[s_e, gelu_free[b]], writes=["gt%d" % b])
            P.op("dve", lambda e, b=b: e.tensor_scalar(out=gt[b], in0=gt[b], scalar1=0.044715, scalar2=1.0,
                                                       op0=ALU.mult, op1=ALU.add), waits=[s1], writes=["gt%d" % b])
            s2 = P.op("dve", lambda e, b=b, yv=yv: e.tensor_tensor(out=gt[b], in0=gt[b], in1=yv, op=ALU.mult),
                      reads=["ybuf"], writes=["gt%d" % b])
            s3 = P.op("act", lambda e, b=b: e.activation(out=gt[b], in_=gt[b], func=AF.Sigmoid, scale=1.5957691216),
                      waits=[s2], writes=["gt%d" % b])
            last_g = P.op("dve", lambda e, b=b, yv=yv, kt=kt, c0=c0: e.tensor_tensor(
                out=self.hT[:, kt, 8 + c0:8 + c0 + 512], in0=gt[b], in1=yv, op=ALU.mult),
                waits=[s3], reads=["ybuf"], writes=["hTg"])
            gelu_free[b] = last_g
    return last_g


def _s5_glu(self, L, w_in):
    P, cfg = self.P, self.cfg
    ws = self.wstage[:].rearrange("p a n -> p (a n)")
    tmpa = [ws[:, i * 512:(i + 1) * 512] for i in range(2)]
    tmpb = [ws[:, 1024 + i * 512:1024 + (i + 1) * 512] for i in range(2)]
    gw = self.d_gluw.rearrange("(k p) n -> p k n", p=128)
    done = [None] * cfg.NTB
    pa_free = [None, None]
    pb_free = [None, None]
    ta_free = [None, None]
    n = 0
    last_pe = None
    for ci in range(2):
        bufA = self.wbf[:, 2 * ci, :].rearrange("p (k n) -> p k n", k=8)
        bufB = self.wbf[:, 2 * ci + 1, :].rearrange("p (k n) -> p k n", k=8)
        stg = self.R(0, 4096).rearrange("p (s n) -> p s n", s=4)
        sc = None
        for (dst, cbase) in ((bufA, 512 * ci), (bufB, 1024 + 512 * ci)):
            for i in range(4):
                sl = self.glu_si % 4
                self.glu_si += 1
                st = stg[:, sl, :].rearrange("p (a b) -> p a b", a=2)
                sd = P.op("sp", lambda e, st=st, i=i, cbase=cbase: e.dma_start(
                    out=st, in_=gw[:, 2 * i:2 * i + 2, cbase:cbase + 512]),
                    waits=[self.glu_slot_free[sl]] + list(w_in), dma="gl%d" % sl)
                sc = P.op("pool", lambda e, st=st, dst=dst, i=i: e.tensor_copy(out=dst[:, 2 * i:2 * i + 2, :], in_=st),
                          waits=[sd] + list(w_in), reads=["glst"], writes=["wbf"])
                self.glu_slot_free[sl] = sc
        for tb in range(cfg.NTB):
            gv = self.hT[:, :, 8 + tb * 512:8 + tb * 512 + 512]
            for dl in range(4):
                dj = 4 * ci + dl
                pi = n % 2
                n += 1
                pa, pb = self.ps[pi], self.ps[2 + pi]
                sig = None
                for (pt, wt) in ((pa, bufA), (pb, bufB)):
                    for k in range(KT):
                        sig = P.op("pe", lambda e, pt=pt, wt=wt, k=k, dl=dl, gv=gv: e.matmul(
                            pt[:, 0:512], lhsT=wt[:, k, dl * 128:(dl + 1) * 128], rhs=gv[:, k, :],
                            start=(k == 0), stop=(k == KT - 1)),
                            waits=[sc, pa_free[pi], pb_free[pi]] + list(w_in) if (pt is pa and k == 0) else (),
                            mark=(pt is pb and k == KT - 1))
                last_pe = sig
                s_s = P.op("act", lambda e, pb=pb, pi=pi, dj=dj: e.activation(
                    out=tmpa[pi], in_=pb[:, 0:512], func=AF.Sigmoid, bias=self.vecs[:, V_GB + 8 + dj:V_GB + 9 + dj]),
                    waits=[sig, ta_free[pi]], writes=["ta%d" % pi])
                pb_free[pi] = s_s
                s_m = P.op("dve", lambda e, pa=pa, pi=pi, dj=dj: e.scalar_tensor_tensor(
                    out=tmpb[pi], in0=pa[:, 0:512], scalar=self.vecs[:, V_GB + dj:V_GB + dj + 1], in1=tmpa[pi],
                    op0=ALU.add, op1=ALU.mult), waits=[s_s], writes=["tb%d" % pi])
                pa_free[pi] = s_m
                ta_free[pi] = s_m
                xv = self.xT[:, dj, tb * 512:(tb + 1) * 512]
                sx = P.op("dve", lambda e, pi=pi, xv=xv, dj=dj: e.scalar_tensor_tensor(
                    out=xv, in0=tmpb[pi], scalar=self.der[:, L, 1, 2, dj:dj + 1], in1=xv, op0=ALU.mult, op1=ALU.add),
                    reads=["tb%d" % pi], writes=["xg%d" % dj])
                if ci == 1:
                    done[tb] = sx
    return done, last_pe


K.s5_p1 = _s5_p1
K.s5_scan = _s5_scan
K.s5_p3 = _s5_p3
K.s5_glu = _s5_glu


def _s5_cin(self, w_in):
    P, cfg = self.P, self.cfg
    v = self.s5v
    Eg, F, t = v["Eg"], v["F"], v["t"]
    ZA, ZB = self.scan[:, 2, :], self.scan[:, 3, :]
    MUL, ADD = ALU.mult, ALU.add
    sig = None
    for i in range(3):
        for d in range(2):
            for j in range(cfg.CPS):
                mcol = V_CM + (d * 3 + i) * 4 + j
                msc = self.vecs[:, mcol:mcol + 1]
                if j == 0:
                    sig = P.op("dve", lambda e, i=i, d=d, j=j, msc=msc: e.tensor_scalar(
                        out=_cols(F[i], d), in0=_cols(Eg[:, j, :], d), scalar1=msc, scalar2=None, op0=MUL),
                        waits=w_in, reads=["Eg"], writes=["F%d" % i])
                else:
                    sig = P.op("dve", lambda e, i=i, d=d, j=j, msc=msc: e.scalar_tensor_tensor(
                        out=_cols(F[i], d), in0=_cols(Eg[:, j, :], d), scalar=msc, in1=_cols(F[i], d),
                        op0=MUL, op1=ADD), reads=["Eg", "F%d" % i], writes=["F%d" % i])
    for i in (1, 2):
        P.op("dve", lambda e: e.tensor_tensor(out=t[0], in0=F[0], in1=ZA, op=MUL), reads=["F0"], writes=["t0"])
        P.op("dve", lambda e: e.tensor_tensor(out=as3(t[1]), in0=swap_ri(F[0]), in1=as3(ZB), op=MUL),
             reads=["F0"], writes=["t1"])
        P.op("dve", lambda e: e.tensor_tensor(out=t[2], in0=t[0], in1=t[1], op=ADD), reads=["t0", "t1"], writes=["t2"])
        sig = P.op("dve", lambda e, i=i: e.tensor_tensor(out=F[0], in0=t[2], in1=F[i], op=ADD),
                   reads=["t2", "F%d" % i], writes=["F0"])
    return F[0], sig


def _s5_corr(self, cin, w_in):
    P, cfg = self.P, self.cfg
    NCH = cfg.NCH
    v = self.s5v
    SX, Q, t = v["SX"], v["Q"], v["t"]
    AA, BB = self.scan[:, 0, :], self.scan[:, 1, :]
    MUL, ADD = ALU.mult, ALU.add
    s = P.op("dve", lambda e: e.tensor_copy(out=Q[0], in_=cin), waits=w_in, writes=["Q0"])
    last = P.op("dve", lambda e: e.tensor_copy(out=SX[:, 0, :], in_=cin), waits=list(w_in) + [s], writes=["SXp"])
    for j in range(1, NCH + 1):
        qp, qn = Q[(j - 1) % 2], Q[j % 2]
        tp, tn = "Q%d" % ((j - 1) % 2), "Q%d" % (j % 2)
        P.op("dve", lambda e, qp=qp: e.tensor_tensor(out=t[0], in0=qp, in1=AA, op=MUL), reads=[tp], writes=["t0"])
        P.op("dve", lambda e, qp=qp: e.tensor_tensor(out=as3(t[1]), in0=swap_ri(qp), in1=as3(BB), op=MUL),
             reads=[tp], writes=["t1"])
        s = P.op("dve", lambda e, qn=qn: e.tensor_tensor(out=qn, in0=t[0], in1=t[1], op=ADD),
                 reads=["t0", "t1"], writes=[tn])
        if j < NCH:
            last = P.op("dve", lambda e, qn=qn, j=j: e.tensor_tensor(out=SX[:, j, :], in0=SX[:, j, :], in1=qn, op=ADD),
                        reads=[tn], writes=["SXp"])
    return last


def _s5_mixer(self, L, h_sigs):
    P, cfg = self.P, self.cfg
    self.s5v = _s5_views(self)
    v = self.s5v
    self.glu_si = 0
    self.glu_slot_free = [None] * 4
    s_p1, _ = self.s5_p1(h_sigs)
    self.barrier()
    if cfg.dbg == "p1":
        return [s_p1] * cfg.NTB
    if cfg.mode == "A":
        q, s_q, s_x = self.s5_scan(None, [s_p1])
        o1 = P.op("sp", lambda e: e.dma_start(out=self.d_Eo[:, :], in_=q), waits=[s_q], dma="oA")
        o2 = P.op("sp", lambda e: e.dma_start(out=self.d_xTo.rearrange("(k p) t -> p k t", p=128), in_=self.xT[:]),
                  dma="oA")
        o3 = P.op("sp", lambda e: e.dma_start(out=self.d_hTo.rearrange("(k p) t -> p k t", p=128),
                                              in_=self.hT[:, :, 8:8 + cfg.T]), dma="oA")
        for e in ("sp", "pe", "act", "dve", "pool"):
            P.wait(e, o3)
        return None
    if cfg.mode == "B":
        s_e = P.op("sp", lambda e: e.dma_start(out=v["Eg"].rearrange("p r c -> p (r c)")[:, 0:cfg.CPS * 128],
                                               in_=self.d_Eg[:, :]), dma="ldE")
        cin, s_c = _s5_cin(self, [s_e, s_p1])
        q, s_q, s_x = self.s5_scan(cin, [s_c])
    else:
        q, s_q, s_x = self.s5_scan(None, [s_p1])
        if cfg.do_cc:
            o1 = P.op("sp", lambda e: e.dma_start(out=self.d_bounce[:, :], in_=q), waits=[s_q], dma="cc1")
            groups = [list(range(b * cfg.CPS, (b + 1) * cfg.CPS)) for b in range(cfg.NCORES // cfg.CPS)]
            s_cc = P.op("pool", lambda e: e.collective_compute(
                "AllGather", ALU.bypass, replica_groups=groups, ins=[self.d_bounce.opt()],
                outs=[self.d_gath.opt()]), waits=[o1], dma="ccs", inc=1)
            s_e = P.op("sp", lambda e: e.dma_start(out=v["Eg"][:, 0:cfg.CPS, :],
                                                   in_=self.d_gath.rearrange("(r p) n -> p r n", p=128)),
                       waits=[s_cc], dma="cc2")
            cin, s_c = _s5_cin(self, [s_e, s_x])
            s_x = _s5_corr(self, cin, [s_c, s_x])
    self.barrier()
    if cfg.dbg == "scan":
        return [s_x] * cfg.NTB
    last_g = self.s5_p3(L, [s_x])
    self.barrier()
    if cfg.dbg == "p3":
        self.buf_free = [None] * 6
        self.slot_free = [None] * 3
        self.wb_i = 0
        return [last_g] * cfg.NTB
    done, last_pe = self.s5_glu(L, [last_g])
    self.barrier()
    self.buf_free = [None] * 6
    self.slot_free = [None] * 3
    self.wb_i = 0
    return done


K.s5_mixer = _s5_mixer
```
